# Optimizing a Trainium2 kernel written in Bass

```python
import jax, jax.numpy as jnp
from jax import lax
import numpy as np

D_MODEL = 1024
BATCH = 4
SEQ = 8192
DEPTH = 1
DEC_BATCH = 32
DEC_SEQ = 64
PAST_LEN = 4096

CHUNK = 64
QBLOCK = 128
D_FF = 2816
MLA_HEADS = 8
MLA_Q_LORA = 256
MLA_KV_LORA = 128
MLA_NOPE = 64
MLA_ROPE = 32
MLA_V = 64
SB_HEADS = 8
SB_DIM = 64
MIX_WIDTH = MLA_HEADS * MLA_V + SB_HEADS * SB_DIM
IN_WIDTH = MLA_Q_LORA + MLA_KV_LORA + MLA_ROPE + 3 * SB_HEADS * SB_DIM
MLA_SCALE = (MLA_NOPE + MLA_ROPE) ** -0.5
SB_SCALE = SB_DIM ** -0.5
ROPE_THETA = 10000.0
EPS = 1e-6

kernel_name = 'hybrid_mla_stickbreak_macaron_stream_step'


def rmsnorm(x, g):
    xf = x.astype(jnp.float32)
    y = xf * lax.rsqrt(jnp.mean(xf * xf, axis=-1, keepdims=True) + EPS)
    return (y * g.astype(jnp.float32)).astype(x.dtype)


def swiglu(x, w_gate, w_up, w_down):
    return (jax.nn.silu(x @ w_gate) * (x @ w_up)) @ w_down


def rope(x, pos):
    half = MLA_ROPE // 2
    inv_freq = ROPE_THETA ** (-jnp.arange(half, dtype=jnp.float32) / half)
    ang = pos.astype(jnp.float32)[:, None] * inv_freq[None, :]
    cos = jnp.cos(ang)[None, :, None, :]
    sin = jnp.sin(ang)[None, :, None, :]
    xf = x.astype(jnp.float32)
    x1, x2 = xf[..., :half], xf[..., half:]
    return jnp.concatenate([x1 * cos - x2 * sin, x1 * sin + x2 * cos], axis=-1).astype(x.dtype)


def sweep_query_blocks(fn, q_pos, *qs):
    nq = q_pos.shape[0]
    blk = QBLOCK if nq % QBLOCK == 0 else nq
    nb = nq // blk

    def split(a):
        return jnp.moveaxis(a.reshape(a.shape[0], nb, blk, *a.shape[2:]), 1, 0)

    out = lax.map(lambda a: fn(*a), (q_pos.reshape(nb, blk),) + tuple(split(a) for a in qs))
    out = jnp.moveaxis(out, 0, 1)
    return out.reshape(out.shape[0], nq, *out.shape[3:])


def mla_block(qp, q_lat, q_rope, latent, krope, k_pos):
    s = (jnp.einsum('bqhc,bkc->bhqk', q_lat, latent, preferred_element_type=jnp.float32)
         + jnp.einsum('bqhr,bkr->bhqk', q_rope, krope, preferred_element_type=jnp.float32)) * MLA_SCALE
    visible = (k_pos[None, :] // CHUNK) <= (qp[:, None] // CHUNK)
    s = jnp.where(visible[None, None], s, -jnp.inf)
    p = jax.nn.softmax(s, axis=-1).astype(latent.dtype)
    return jnp.einsum('bhqk,bkc->bqhc', p, latent)


def sb_block(qp, q, k, v, k_pos):
    z = jnp.einsum('bqhd,bkhd->bhqk', q, k, preferred_element_type=jnp.float32) * SB_SCALE
    before = (k_pos[None, :] < qp[:, None])[None, None]
    log_beta = jax.nn.log_sigmoid(z)
    log_rest = jnp.where(before, jax.nn.log_sigmoid(-z), 0.0)
    tail = lax.cumsum(log_rest, axis=3, reverse=True) - log_rest
    a = jnp.where(before, jnp.exp(log_beta + tail), 0.0).astype(v.dtype)
    return jnp.einsum('bhqk,bkhd->bqhd', a, v)


def token_mix(u, pos, past_latent, past_krope, past_k, past_v,
              w_in, g_q, w_uq, g_kv, w_uk, w_uv, g_mla_out, g_sb_out, w_out):
    B, S, _ = u.shape
    sbw = SB_HEADS * SB_DIM
    i1 = MLA_Q_LORA
    i2 = i1 + MLA_KV_LORA
    i3 = i2 + MLA_ROPE
    c_q, c_kv, k_r, q_sb, k_sb, v_sb = jnp.split(u @ w_in, [i1, i2, i3, i3 + sbw, i3 + 2 * sbw], axis=-1)
    q = jnp.einsum('bsc,chd->bshd', rmsnorm(c_q, g_q), w_uq)
    q_rope = rope(q[..., MLA_NOPE:], pos)
    q_lat = jnp.einsum('bshd,chd->bshc', q[..., :MLA_NOPE], w_uk)
    latent_new = rmsnorm(c_kv, g_kv)
    krope_new = rope(k_r[:, :, None, :], pos)[:, :, 0, :]
    q_sb = q_sb.reshape(B, S, SB_HEADS, SB_DIM)
    k_new = k_sb.reshape(B, S, SB_HEADS, SB_DIM)
    v_new = v_sb.reshape(B, S, SB_HEADS, SB_DIM)
    if past_latent is None:
        latent, krope, keys, vals, k_pos = latent_new, krope_new, k_new, v_new, pos
    else:
        n_past = past_latent.shape[1]
        latent = jnp.concatenate([past_latent, latent_new], axis=1)
        krope = jnp.concatenate([past_krope, krope_new], axis=1)
        keys = jnp.concatenate([past_k, k_new], axis=1)
        vals = jnp.concatenate([past_v, v_new], axis=1)
        k_pos = jnp.concatenate([jnp.arange(n_past, dtype=pos.dtype), pos])
    o_lat = sweep_query_blocks(lambda qp, ql, qr: mla_block(qp, ql, qr, latent, krope, k_pos), pos, q_lat, q_rope)
    o_mla = jnp.einsum('bshc,chd->bshd', o_lat, w_uv)
    o_sb = sweep_query_blocks(lambda qp, qq: sb_block(qp, qq, keys, vals, k_pos), pos, q_sb)
    o = jnp.concatenate([rmsnorm(o_mla, g_mla_out).reshape(B, S, MLA_HEADS * MLA_V),
                         rmsnorm(o_sb, g_sb_out).reshape(B, S, SB_HEADS * SB_DIM)], axis=-1)
    return o @ w_out, (latent_new, krope_new, k_new, v_new)


def encoder_layer(x, pos, past_latent, past_krope, past_k, past_v,
                  g_pre_ff1, w_gate1, w_up1, w_down1, g_post_ff1,
                  g_pre_mix, w_in, g_q, w_uq, g_kv, w_uk, w_uv, g_mla_out, g_sb_out, w_out, g_post_mix,
                  g_pre_ff2, w_gate2, w_up2, w_down2, g_post_ff2, g_final):
    h = x + 0.5 * rmsnorm(swiglu(rmsnorm(x, g_pre_ff1), w_gate1, w_up1, w_down1), g_post_ff1)
    m, rows = token_mix(rmsnorm(h, g_pre_mix), pos, past_latent, past_krope, past_k, past_v,
                        w_in, g_q, w_uq, g_kv, w_uk, w_uv, g_mla_out, g_sb_out, w_out)
    h = h + rmsnorm(m, g_post_mix)
    h = h + 0.5 * rmsnorm(swiglu(rmsnorm(h, g_pre_ff2), w_gate2, w_up2, w_down2), g_post_ff2)
    return rmsnorm(h, g_final), rows


def setup_inputs(seed: int = 0) -> dict:
    key = jax.random.key(seed)
    ks = iter(jax.random.split(key, 40))
    L = DEPTH

    def nrm(shape, fan_in):
        return jax.random.normal(next(ks), shape, jnp.float32) * fan_in ** -0.5

    def gain(shape):
        return 1.0 + 0.05 * jax.random.normal(next(ks), shape, jnp.float32)

    def unit(shape):
        return jax.random.normal(next(ks), shape, jnp.float32)

    return {
        'x_prompt': unit((BATCH, SEQ, D_MODEL)),
        'x_sample': unit((DEC_BATCH, DEC_SEQ, D_MODEL)),
        'cache_mla_latent': unit((L, DEC_BATCH, PAST_LEN, MLA_KV_LORA)),
        'cache_mla_krope': unit((L, DEC_BATCH, PAST_LEN, MLA_ROPE)),
        'cache_sb_k': unit((L, DEC_BATCH, PAST_LEN, SB_HEADS, SB_DIM)),
        'cache_sb_v': unit((L, DEC_BATCH, PAST_LEN, SB_HEADS, SB_DIM)),
        'g_pre_ff1': gain((L, D_MODEL)),
        'w_gate1': nrm((L, D_MODEL, D_FF), D_MODEL),
        'w_up1': nrm((L, D_MODEL, D_FF), D_MODEL),
        'w_down1': nrm((L, D_FF, D_MODEL), D_FF),
        'g_post_ff1': gain((L, D_MODEL)),
        'g_pre_mix': gain((L, D_MODEL)),
        'w_in': nrm((L, D_MODEL, IN_WIDTH), D_MODEL),
        'g_q': gain((L, MLA_Q_LORA)),
        'w_uq': nrm((L, MLA_Q_LORA, MLA_HEADS, MLA_NOPE + MLA_ROPE), MLA_Q_LORA),
        'g_kv': gain((L, MLA_KV_LORA)),
        'w_uk': nrm((L, MLA_KV_LORA, MLA_HEADS, MLA_NOPE), MLA_KV_LORA),
        'w_uv': nrm((L, MLA_KV_LORA, MLA_HEADS, MLA_V), MLA_KV_LORA),
        'g_mla_out': gain((L, MLA_HEADS, MLA_V)),
        'g_sb_out': gain((L, SB_HEADS, SB_DIM)),
        'w_out': nrm((L, MIX_WIDTH, D_MODEL), MIX_WIDTH),
        'g_post_mix': gain((L, D_MODEL)),
        'g_pre_ff2': gain((L, D_MODEL)),
        'w_gate2': nrm((L, D_MODEL, D_FF), D_MODEL),
        'w_up2': nrm((L, D_MODEL, D_FF), D_MODEL),
        'w_down2': nrm((L, D_FF, D_MODEL), D_FF),
        'g_post_ff2': gain((L, D_MODEL)),
        'g_final': gain((L, D_MODEL)),
    }


def reference(x_prompt, x_sample, cache_mla_latent, cache_mla_krope, cache_sb_k, cache_sb_v,
              g_pre_ff1, w_gate1, w_up1, w_down1, g_post_ff1,
              g_pre_mix, w_in, g_q, w_uq, g_kv, w_uk, w_uv, g_mla_out, g_sb_out, w_out, g_post_mix,
              g_pre_ff2, w_gate2, w_up2, w_down2, g_post_ff2, g_final):
    weights = (g_pre_ff1, w_gate1, w_up1, w_down1, g_post_ff1,
               g_pre_mix, w_in, g_q, w_uq, g_kv, w_uk, w_uv, g_mla_out, g_sb_out, w_out, g_post_mix,
               g_pre_ff2, w_gate2, w_up2, w_down2, g_post_ff2, g_final)
    pos_p = jnp.arange(x_prompt.shape[1], dtype=jnp.int32)
    pos_s = cache_mla_latent.shape[2] + jnp.arange(x_sample.shape[1], dtype=jnp.int32)
    hp, hs = x_prompt, x_sample
    rows_p, rows_s = [], []
    for l in range(DEPTH):
        wl = [w[l] for w in weights]
        hp, rp = encoder_layer(hp, pos_p, None, None, None, None, *wl)
        hs, rs = encoder_layer(hs, pos_s, cache_mla_latent[l], cache_mla_krope[l],
                               cache_sb_k[l], cache_sb_v[l], *wl)
        rows_p.append(rp)
        rows_s.append(rs)

    def stack(rows, i):
        return jnp.stack([r[i] for r in rows])

    return (hp, hs,
            stack(rows_p, 0), stack(rows_p, 1), stack(rows_p, 2), stack(rows_p, 3),
            stack(rows_s, 0), stack(rows_s, 1), stack(rows_s, 2), stack(rows_s, 3))
```

```python
import numpy as np
import ml_dtypes
import concourse.bass as bass
import concourse.mybir as mybir
from concourse.bass_utils import run_bass_kernel_spmd

F32 = mybir.dt.float32
BF16 = mybir.dt.bfloat16
ALU = mybir.AluOpType
AF = mybir.ActivationFunctionType
AX = mybir.AxisListType

SAME_ENGINE_SYNC = True


class Buf:
    __slots__ = ("name", "w", "r", "dsem")

    def __init__(self, name):
        self.name = name
        self.w = None
        self.r = {}
        self.dsem = None


class Op:
    __slots__ = ("eng", "fn", "deps", "dsem", "need", "tok", "idx", "grp")

    def __init__(self, eng, fn, dsem):
        self.eng = eng
        self.fn = fn
        self.deps = []
        self.dsem = dsem
        self.need = dsem is not None
        self.tok = None
        self.grp = None


class DSem:
    def __init__(self, name):
        self.name = name
        self.h = None
        self.n = 0


class Sched:
    ENGS = ("pe", "act", "dve", "pool", "sp")

    def __init__(self):
        self.ops = {e: [] for e in self.ENGS}
        self.dsems = []
        self.dmas = []
        self.nops = 0

    def dsem(self, name):
        d = DSem(name)
        self.dsems.append(d)
        return d

    stopped = False

    def add(self, eng, fn, reads=(), writes=(), dsem=None, group=None):
        if self.stopped:
            return None
        op = Op(eng, fn, dsem)
        op.grp = group
        self.nops += 1
        op.idx = self.nops
        best = {}

        def consider(d):
            if d is None or d is op:
                return
            k = id(d.dsem) if d.dsem is not None else d.eng
            o = best.get(k)
            if o is None or d.idx > o.idx:
                best[k] = d

        for b in reads:
            consider(b.w)
        for b in writes:
            consider(b.w)
            for d in b.r.values():
                consider(d)
        op.deps = list(best.values())
        if fn is not None:
            k = id(dsem) if dsem is not None else eng
            for b in reads:
                b.r[k] = op
        for b in writes:
            b.w = op
            b.r = {}
        self.ops[eng].append(op)
        if dsem is not None:
            self.dmas.append(op)
        return op

    def finalize(self, nc, final_waits):
        for e in self.ENGS:
            for op in self.ops[e]:
                for d in op.deps:
                    if d.dsem is not None:
                        continue
                    if d.eng == op.eng and (d.eng == "pe" or d.eng == "sp" or not SAME_ENGINE_SYNC):
                        continue
                    d.need = True
        stack = []
        esem = {}
        import contextlib
        with contextlib.ExitStack() as es:
            for e in self.ENGS:
                esem[e] = es.enter_context(nc.semaphore("s_" + e))
            for d in self.dsems:
                d.h = es.enter_context(nc.semaphore("d_" + d.name))
                d.n = 0
            for e in self.ENGS:
                n = 0
                for op in self.ops[e]:
                    if op.dsem is not None:
                        continue
                    if op.need:
                        n += 1
                        op.tok = (esem[e], n, e)
            gmax = {}
            for op in self.dmas:
                op.dsem.n += 16
                op.tok = (op.dsem.h, op.dsem.n, "dma:" + op.dsem.name)
                if op.grp is not None:
                    gmax[op.grp] = op.tok
            for op in self.dmas:
                if op.grp is not None:
                    assert gmax[op.grp][0] is op.tok[0]
                    op.tok = gmax[op.grp]
            block = es.enter_context(nc.Block())

            def make(e):
                def body(eng):
                    known = {}
                    ownn = 0
                    for op in self.ops[e]:
                        for d in op.deps:
                            if d.dsem is None and d.eng == e:
                                if e in ("pe", "sp") or not SAME_ENGINE_SYNC:
                                    continue
                            sem, val, who = d.tok
                            k = id(sem)
                            if known.get(k, 0) >= val:
                                continue
                            known[k] = val
                            eng.wait_ge(sem, val)
                        if op.fn is None:
                            continue
                        ins = op.fn(eng)
                        if op.dsem is not None:
                            ins.then_inc(op.dsem.h, 16)
                        elif op.need:
                            ins.then_inc(esem[e], 1)
                    if e == "sp":
                        for d in final_waits:
                            eng.wait_ge(d.h, d.n)
                return body

            block.tensor(make("pe"))
            block.scalar(make("act"))
            block.vector(make("dve"))
            block.gpsimd(make("pool"))
            block.sync(make("sp"))


D = 1024
MC = 8
HEADS = 8
MLA_SCALE = 96.0 ** -0.5
SB_SCALE = 0.125
EPS = 1e-6
NEG = -30000.0


class StopBuild(Exception):
    pass


class Cfg:
    stop = None
    vv_eng = "dve"

    def __init__(self, nslot=16, nstream=4, past=4096, dff=2816):
        self.NSLOT = nslot
        self.NSTREAM = nstream
        self.PAST = past
        self.DFF = dff
        self.FC = dff // 128
        self.NPT = past // 512
        self.NOWN = nslot // 2
        self.NKT = nslot + nstream * (self.NPT + 1)


def build_program(cfg):
    NSLOT, NSTREAM, PAST, DFF, FC, NPT, NOWN, NKT = (cfg.NSLOT, cfg.NSTREAM, cfg.PAST, cfg.DFF, cfg.FC,
                                                     cfg.NPT, cfg.NOWN, cfg.NKT)
    import contextlib
    nc = bass.Bass("TRN2", target_bir_lowering=False)
    S = Sched()

    def din(name, shape):
        return nc.dram_tensor(name, list(shape), F32, kind="ExternalInput").ap()

    def dout(name, shape):
        return nc.dram_tensor(name, list(shape), F32, kind="ExternalOutput").ap()

    def dscr(name, shape):
        return nc.dram_tensor(name, list(shape), BF16).ap()

    NP_TOK = NSLOT * 512
    NS_TOK = NSTREAM * 64
    NS_PAD = NSTREAM * 128
    xp = din("xp", [NP_TOK, D])
    xs = din("xs", [NS_PAD, D])
    c_lat = din("c_lat", [NSTREAM * PAST, 128])
    c_kr = din("c_kr", [NSTREAM * PAST, 32])
    c_k = din("c_k", [NSTREAM * PAST, 512])
    c_v = din("c_v", [NSTREAM * PAST, 512])
    w_gate = [din("w_gate1", [D, DFF]), din("w_gate2", [D, DFF])]
    w_up = [din("w_up1", [D, DFF]), din("w_up2", [D, DFF])]
    w_down = [din("w_down1", [DFF, D]), din("w_down2", [DFF, D])]
    w_in = din("w_in", [D, 1952])
    w_uq = din("w_uq", [256, 768])
    w_uk = din("w_uk", [128, 512])
    w_uv = din("w_uv", [128, 512])
    w_out = din("w_out", [D, D])
    gpre_d = din("gpre", [128, 3 * 8])
    gq_d = din("gq", [128, 2])
    gout_d = din("gout", [128, 8])
    gpost_d = din("gpost", [128, 4 * D])
    gkv_d = din("gkv", [128, 128])
    cc_p = din("cc_p", [NP_TOK, 32])
    ss_p = din("ss_p", [NP_TOK, 32])
    cc_s = din("cc_s", [NS_PAD, 32])
    ss_s = din("ss_s", [NS_PAD, 32])
    ccT_p = din("ccT_p", [32, NP_TOK])
    ssT_p = din("ssT_p", [32, NP_TOK])
    ccT_s = din("ccT_s", [32, NS_PAD])
    ssT_s = din("ssT_s", [32, NS_PAD])
    kmask_d = din("kmask", [1, NP_TOK])
    ident_d = din("ident", [128, 128])
    negU_d = din("negU", [128, 128])
    negL_d = din("negL", [128, 128])
    msb_d = din("msb", [128, 896])
    mmla_d = din("mmla", [128, 896])
    y_p = dout("y_p", [NOWN * 512, D])
    lat_p = dout("lat_p", [NOWN * 512, 128])
    kr_p = dout("kr_p", [NOWN * 512, 32])
    k_p = dout("k_p", [NOWN * 512, 512])
    v_p = dout("v_p", [NOWN * 512, 512])
    y_s = dout("y_s", [NS_TOK, D])
    lat_s = dout("lat_s", [NS_TOK, 128])
    kr_s = dout("kr_s", [NS_TOK, 32])
    k_s = dout("k_s", [NS_TOK, 512])
    v_s = dout("v_s", [NS_TOK, 512])
    Wgu = [dscr("Wgu1", [FC, 128, 2048]), dscr("Wgu2", [FC, 128, 2048])]
    Wd = [dscr("Wd1", [FC, 128, 1024]), dscr("Wd2", [FC, 128, 1024])]
    WO = dscr("WO", [8, 128, 1024])
    KT = dscr("KT", [NKT, 4, 128, 512])
    VV = dscr("VV", [NKT, 8, 128, 256])
    LT = dscr("LT", [NKT, 128, 512])
    LA = dscr("LA", [NKT, 128, 512])
    KR = dscr("KR", [NKT, 33, 512])
    B_Wgu = [[Buf("Wgu%d_%d" % (l, f)) for f in range(FC)] for l in range(2)]
    B_Wd = [[Buf("Wd%d_%d" % (l, f)) for f in range(FC)] for l in range(2)]
    B_WO = [Buf("WO%d" % k) for k in range(8)]
    B_KT = [Buf("KT%d" % k) for k in range(NKT)]
    B_VV = [Buf("VV%d" % k) for k in range(NKT)]
    B_LT = [Buf("LT%d" % k) for k in range(NKT)]
    B_LA = [Buf("LA%d" % k) for k in range(NKT)]
    B_KR = [Buf("KR%d" % k) for k in range(NKT)]

    es = contextlib.ExitStack()
    with es:
        def sb(name, shape, dt):
            return es.enter_context(nc.sbuf_tensor("s_" + name, list(shape), dt))

        PS = es.enter_context(nc.psum_tensor("PS", [128, 8, 512], F32))
        B_PS = [Buf("ps%d" % i) for i in range(8)]

        Win = sb("Win", [128, 8, 1952], BF16); B_Win = Buf("Win")
        Wuq = sb("Wuq", [128, 2, 768], BF16); B_Wuq = Buf("Wuq")
        Wuqs = sb("Wuqs", [128, 2, 8, 32], BF16); B_Wuqs = Buf("Wuqs")
        WukT = sb("WukT", [128, 8, 128], BF16); B_WukT = Buf("WukT")
        Wuv = sb("Wuv", [128, 8, 64], BF16); B_Wuv = Buf("Wuv")
        gpre = sb("gpre_t", [128, 3, 8], F32); B_gpre = Buf("gpre")
        gq = sb("gq_t", [128, 2], F32); B_gq = Buf("gq")
        gout = sb("gout_t", [128, 8], F32); B_gout = Buf("gout")
        gkv = sb("gkv_t", [128, 128], F32); B_gkv = Buf("gkv")
        ident_f = sb("ident_f", [128, 128], F32)
        ident_b = sb("ident_b", [128, 128], BF16)
        negU = sb("negU", [128, 128], BF16)
        negL = sb("negL", [128, 128], BF16)
        ones_b = sb("ones_b", [128, 128], BF16)
        ones_f = sb("ones_f", [128, 128], F32)
        msb = sb("msb", [128, 896], BF16)
        mmla = sb("mmla", [128, 896], BF16)
        MSB8 = sb("MSB8", [128, 8, 64], BF16)
        MMLA8 = sb("MMLA8", [128, 8, 64], BF16)
        ZERO = sb("ZERO", [128, 256], BF16)
        B_const = Buf("const")
        B_identf = Buf("identf")
        X = sb("X", [128, 4, D], F32); B_X = [Buf("X%d" % i) for i in range(4)]
        HT = sb("HT", [128, FC, 512], BF16) if FC >= 19 else sb("HT", [128, 19, 512], BF16)
        NHB = max(FC, 19)
        B_HT = [Buf("HT%d" % i) for i in range(NHB)]
        XN = HT[:, 0:8, :].rearrange("p a b -> p (a b)").rearrange("p (s d) -> p s d", d=D)
        UT = sb("UT", [128, 8, 512], BF16); B_UT = Buf("UT")
        TMP = sb("TMP", [128, D], BF16); B_TMP = Buf("TMP")
        NWG = 3
        WGB = [sb("WGB%d" % i, [128, 8, 2, 128], BF16) for i in range(NWG)]; B_WGB = [Buf("WGB%d" % i) for i in range(NWG)]
        NWD = 3
        WDB = [sb("WDB%d" % i, [128, 1024], BF16) for i in range(NWD)]; B_WDB = [Buf("WDB%d" % i) for i in range(NWD)]
        st = sb("stats", [128, 64], F32); B_st = Buf("stats")
        LAT32 = sb("LAT32", [128, 128], F32); B_LAT32 = Buf("LAT32")
        KR32 = sb("KR32", [128, 32], F32); B_KR32 = Buf("KR32")
        KRB = sb("KRB", [128, 32], BF16); B_KRB = Buf("KRB")
        RT = sb("RT", [128, 64], F32); B_RT = Buf("RT")
        KTs = sb("KTs", [128, 4, 512], BF16); B_KTs = Buf("KTs")
        VVs = sb("VVs", [128, 8, 4, 64], BF16); B_VVs = Buf("VVs")
        LTs = sb("LTs", [128, 512], BF16); B_LTs = Buf("LTs")
        LAs = sb("LAs", [128, 4, 128], BF16); B_LAs = Buf("LAs")
        KRs = sb("KRs", [64, 512], BF16); B_KRs = Buf("KRs")
        CQN = sb("CQN", [128, 4, 256], BF16); B_CQN = Buf("CQN")
        CQT = sb("CQT", [128, 2, 512], BF16); B_CQT = Buf("CQT")
        CS = sb("CS", [128, 4, 32], F32); SS_ = sb("SSt", [128, 4, 32], F32); B_CS = Buf("CS")
        CCT = sb("CCT", [32, 512], F32); SST = sb("SST", [32, 512], F32); B_CCT = Buf("CCT")
        QTP = sb("QTP", [128, 8, 512], BF16); B_QT = Buf("QTP")
        QLT = sb("QLT", [128, 8, 512], BF16); B_QLT = [Buf("QLT%d" % h) for h in range(8)]
        QRT = sb("QRT", [128, 8, 512], BF16); B_QRT = [Buf("QRT%d" % h) for h in range(8)]
        KMX = sb("KMX", [128, 1 + NSTREAM], F32); B_KMX = Buf("KMX")
        KMB = sb("KMB", [128, 4], F32); B_KMB = Buf("KMB")
        NKB = 3
        ktb = [sb("ktb%d" % i, [128, 512], BF16) for i in range(NKB)]
        vvb = [sb("vvb%d" % i, [128, 4, 128], BF16) for i in range(NKB)]
        ltb = [sb("ltb%d" % i, [128, 512], BF16) for i in range(NKB)]
        lab = [sb("lab%d" % i, [128, 4, 128], BF16) for i in range(NKB)]
        krb = [sb("krb%d" % i, [128, 512], BF16) for i in range(NKB)]
        B_ktb = [Buf("ktb%d" % i) for i in range(NKB)]; B_vvb = [Buf("vvb%d" % i) for i in range(NKB)]; B_vvb1 = [Buf("vvc%d" % i) for i in range(NKB)]
        B_mla = [Buf("mlab%d" % i) for i in range(NKB)]
        B_ltb = [Buf("ltb%d" % i) for i in range(NKB)]; B_lab = [Buf("lab%d" % i) for i in range(NKB)]
        B_krb = [Buf("krb%d" % i) for i in range(NKB)]
        Pb = [sb("Pb%d" % i, [128, 512], BF16) for i in range(2)]; B_Pb = [Buf("Pb0"), Buf("Pb1")]
        E32 = [sb("E32_%d" % i, [128, 512], F32) for i in range(3)]; B_E32 = [Buf("E0"), Buf("E1"), Buf("E2")]
        SPB = [sb("SPB%d" % i, [128, 512], BF16) for i in range(3)]; B_SPB = [Buf("SP0"), Buf("SP1"), Buf("SP2")]
        G32 = sb("G32", [128, 512], F32); B_G32 = Buf("G32")
        ATb = [sb("AT%d" % i, [128, 512], BF16) for i in range(2)]; B_AT = [Buf("AT0"), Buf("AT1")]
        OLT = sb("OLT", [128, 512], BF16); B_OLT = Buf("OLT")
        RDEN = sb("RDEN", [128, 512], F32); B_RDEN = Buf("RDEN")
        ON = sb("ON", [128, 512], F32); B_ON = Buf("ON")
        SQO = sb("SQO", [128, 512], BF16); B_SQO = Buf("SQO")
        RS = sb("RS", [128, 512], F32); B_RS = Buf("RS")
        OT = sb("OT", [128, 8, 512], BF16); B_OT = [Buf("OT%d" % i) for i in range(8)]
        GPB = sb("GPB", [128, D], F32); B_GPB = Buf("GPB")
        SG = E32; B_SG = B_E32
        RA = E32[0][0:32, :]; RB = E32[1][0:32, :]; B_RA = B_E32[0]; B_RB = B_E32[1]
        QN = SPB[0][0:64, :]; B_QN = B_SPB[0]
        SQ1 = Pb[0]; SQ2 = Pb[1]; B_SQ1 = B_Pb[0]; B_SQ2 = B_Pb[1]
        MT = RS; B_MT = B_RS
        KM32 = RS; B_KM32 = B_RS
        K32 = G32; B_K32 = B_G32
        V32 = ON; B_V32 = B_ON


        rr = {"n": 0}

        def ew_engine():
            rr["n"] += 1
            return ("dve", "pool", "act")[rr["n"] % 3]

        def ev_engine():
            rr["n"] += 1
            return ("dve", "act")[rr["n"] % 2]

        def copy_op(eng, out, in_, reads, writes):
            if eng == "act":
                S.add("act", lambda e: e.activation(out=out, in_=in_, func=AF.Copy), reads, writes)
            else:
                S.add(eng, lambda e: e.tensor_copy(out=out, in_=in_), reads, writes)

        def scale_op(eng, out, in_, sc, reads, writes):
            if eng == "act":
                S.add("act", lambda e: e.activation(out=out, in_=in_, func=AF.Copy, scale=sc), reads, writes)
            else:
                S.add(eng, lambda e: e.tensor_scalar(out=out, in0=in_, scalar1=sc, scalar2=None, op0=ALU.mult),
                      reads, writes)

        out_sems = []

        def dma(q, out, in_, reads, writes, semb=None, group=None, final=False):
            b = semb if semb is not None else writes[0]
            if b.dsem is None:
                b.dsem = {}
            if q not in b.dsem:
                b.dsem[q] = S.dsem(b.name + "_" + q)
            ds = b.dsem[q]
            if final and ds not in out_sems:
                out_sems.append(ds)
            S.add(q, lambda e: e.dma_start(out=out, in_=in_), reads, writes, dsem=ds, group=group)

        def barrier(q, bufs):
            S.add(q, None, bufs, [])

        def mm(out, lhsT, rhs, start, stop, reads, writes, **kw):
            S.add("pe", lambda e: e.matmul(out, lhsT=lhsT, rhs=rhs, start=start, stop=stop, **kw), reads, writes)

        def tr(out, in_, ident, reads, writes):
            S.add("pe", lambda e: e.transpose(out=out, in_=in_, identity=ident), reads, writes)

        def chk(name):
            if cfg.stop == name:
                S.stopped = True

        Xf = X[:].rearrange("p a b -> p (a b)")

        def load_const(dst_b, src_d, ncol, q):
            dma("sp", Xf[:, q * 1024:q * 1024 + ncol], src_d, [], [B_X[q]])
            copy_op("dve", dst_b, Xf[:, q * 1024:q * 1024 + ncol], [B_X[q]], [B_const])

        dma("sp", ident_f[:], ident_d, [], [B_identf])
        load_const(ident_b[:], ident_d, 128, 0)
        load_const(negU[:], negU_d, 128, 1)
        load_const(negL[:], negL_d, 128, 2)
        load_const(msb[:], msb_d, 896, 3)
        load_const(mmla[:], mmla_d, 896, 0)
        for hh in range(8):
            copy_op("dve", MSB8[:, hh, :], msb[:, 384:448], [B_const], [B_const])
            copy_op("dve", MMLA8[:, hh, :], mmla[:, 384:448], [B_const], [B_const])
        S.add("pool", lambda e: e.memset(ZERO[:], 0.0), [], [B_const])
        S.add("pool", lambda e: e.memset(ones_b[:], 1.0), [], [B_const])
        S.add("pool", lambda e: e.memset(ones_f[:], 1.0), [], [B_const])
        S.add("pool", lambda e: e.memset(KMX[:], 0.0), [], [B_KMX])
        for i in range(NKB):
            S.add("pool", lambda e, i=i: e.memset(krb[i][:], 0.0), [], [B_krb[i]])
            S.add("pool", lambda e, i=i: e.memset(krb[i][64:65, :], 1.0), [], [B_krb[i]])
        S.add("pool", lambda e: e.memset(QTP[:], 0.0), [], [B_QT])
        S.add("pool", lambda e: e.memset(QRT[:], 0.0), [], B_QRT)
        S.add("pool", lambda e: e.memset(QRT[32:33, :, :], 1.0), [], B_QRT)
        S.add("pool", lambda e: e.memset(KRs[:], 0.0), [], [B_KRs])
        S.add("pool", lambda e: e.memset(OT[:], 0.0), [], B_OT)
        dma("sp", gpre[:].rearrange("p a b -> p (a b)"), gpre_d, [], [B_gpre])
        dma("sp", gq[:], gq_d, [], [B_gq])
        dma("sp", gout[:], gout_d, [], [B_gout])
        dma("sp", gkv[:], gkv_d, [], [B_gkv])

        stg_n = {"n": 0}

        def stage(src_ap, ncol):
            h = stg_n["n"] % 2
            stg_n["n"] += 1
            v = Xf[:, h * 2048:h * 2048 + ncol]
            bufs = [B_X[2 * h], B_X[2 * h + 1]]
            return h, v, bufs

        for mc in range(8):
            h, v, bufs = stage(None, 1952)
            dma("sp", v, w_in[mc * 128:(mc + 1) * 128, :], [], bufs)
            scale_op(ew_engine(), Win[:, mc, :], v, gpre[:, 1, mc:mc + 1], bufs + [B_gpre], [B_Win])
        for cc in range(2):
            h, v, bufs = stage(None, 768)
            dma("sp", v, w_uq[cc * 128:(cc + 1) * 128, :], [], bufs)
            scale_op("dve", Wuq[:, cc, :], v, gq[:, cc:cc + 1], bufs + [B_gq], [B_Wuq])
            wv = Wuq[:, cc, :].rearrange("p (h e) -> p h e", e=96)
            copy_op("dve", Wuqs[:, cc, :, 0:16], wv[:, :, 80:96], [B_Wuq], [B_Wuqs])
            copy_op("dve", Wuqs[:, cc, :, 16:32], wv[:, :, 64:80], [B_Wuq], [B_Wuqs])
        h, v, bufs = stage(None, 512)
        dma("sp", v, w_uv, [], bufs)
        copy_op("dve", Wuv[:].rearrange("p h d -> p (h d)"), v, bufs, [B_Wuv])
        h, v, bufs = stage(None, 512)
        dma("sp", v, w_uk, [], bufs)
        for hh in range(8):
            bk = 4 + (hh % 4)
            tr(PS[0:64, bk, 0:128], v[:, hh * 64:(hh + 1) * 64], ident_f[:], bufs + [B_identf], [B_PS[bk]])
            copy_op(ev_engine(), WukT[0:64, hh, :], PS[0:64, bk, 0:128], [B_PS[bk]], [B_WukT])
        for l in range(2):
            gsel = 0 if l == 0 else 2
            for fc in range(FC):
                wslot = WGB[fc % NWG]; bws = B_WGB[fc % NWG]
                for gi, wsrc in enumerate((w_gate[l], w_up[l])):
                    h, v, bufs = stage(None, 1024)
                    src = wsrc.rearrange("(mc p) f -> p mc f", p=128)[:, :, fc * 128:(fc + 1) * 128]
                    v3 = v.rearrange("p (a b) -> p a b", b=128)
                    dma("sp", v3, src, [], bufs)
                    gb = gpre[:, gsel, :].unsqueeze(2).to_broadcast([128, 8, 128])
                    eng = ("dve", "pool")[(fc + gi) % 2]
                    S.add(eng, lambda e, o=wslot[:, :, gi, :], i0=v3, g=gb: e.tensor_tensor(out=o, in0=i0, in1=g,
                                                                                          op=ALU.mult),
                          bufs + [B_gpre], [bws])
                dma("pool", Wgu[l][fc], wslot[:].rearrange("p a b c -> p (a b c)"), [bws], [B_Wgu[l][fc]], semb=bws)
                dslot = WDB[fc % NWD]; bds = B_WDB[fc % NWD]
                h, v, bufs = stage(None, 1024)
                dma("sp", v, w_down[l][fc * 128:(fc + 1) * 128, :], [], bufs)
                copy_op("act", dslot[:], v, bufs, [bds])
                dma("pool", Wd[l][fc], dslot[:], [bds], [B_Wd[l][fc]], semb=bds)
        for kc in range(8):
            dslot = WDB[kc % NWD]; bds = B_WDB[kc % NWD]
            h, v, bufs = stage(None, 1024)
            dma("sp", v, w_out[kc * 128:(kc + 1) * 128, :], [], bufs)
            scale_op("act", dslot[:], v, gout[:, kc:kc + 1], bufs + [B_gout], [bds])
            dma("pool", WO[kc], dslot[:], [bds], [B_WO[kc]], semb=bds)

        chk("prep")
        st_n = {"n": 0}

        def stat_cols(n):
            c = st_n["n"]
            if c + n > 64:
                c = 0
            st_n["n"] = c + n
            return c

        def rstd_from_ss(c, n, inv_dim):
            S.add("act", lambda e: e.activation(out=st[:, c:c + n], in_=st[:, c:c + n], func=AF.Ln, bias=EPS,
                                                scale=inv_dim), [B_st], [B_st])
            S.add("act", lambda e: e.activation(out=st[:, c:c + n], in_=st[:, c:c + n], func=AF.Exp, scale=-0.5),
                  [B_st], [B_st])

        def prenorm_to_UT(NS):
            for s in range(NS):
                c = stat_cols(1)
                S.add("act", lambda e, s=s, c=c: e.activation(out=TMP[:], in_=X[:, s, :], func=AF.Square,
                                                              accum_out=st[:, c:c + 1]), [B_X[s]], [B_TMP, B_st])
                rstd_from_ss(c, 1, 1.0 / D)
                scale_op(("dve", "act")[s % 2], XN[:, s, :], X[:, s, :], st[:, c:c + 1],
                         [B_X[s], B_st], [B_HT[2 * s], B_HT[2 * s + 1]])
                bk = 2 * s
                pv = PS[:, bk, :].bitcast(BF16)
                for mc in range(8):
                    tr(pv[:, mc * 128:(mc + 1) * 128], XN[:, s, mc * 128:(mc + 1) * 128], ident_b[:],
                       [B_HT[2 * s], B_HT[2 * s + 1], B_const], [B_PS[bk]])
                copy_op(("act", "dve")[s % 2], UT[:, :, s * 128:(s + 1) * 128],
                        pv.rearrange("p (m t) -> p m t", t=128), [B_PS[bk]], [B_UT])

        wg_n = {"n": 0}
        wd_n = {"n": 0}

        def down_stage(LHS, B_LHS, KC, Wscr, B_Wscr, NS, gsel):
            coef = 1.0 if gsel == 1 else 0.5
            dma("sp", GPB[:], gpost_d[:, gsel * D:(gsel + 1) * D], [], [B_GPB])
            subs = list(range(NS))
            for kc in range(KC):
                i = wd_n["n"] % NWD
                wd_n["n"] += 1
                dma("sp", WDB[i][:], Wscr[kc], [B_Wscr[kc]], [B_WDB[i]])
                for si, s in enumerate(subs):
                    for nh in range(2):
                        bk = 2 * si + nh
                        mm(PS[:, bk, :], LHS[:, kc, s * 128:(s + 1) * 128], WDB[i][:, nh * 512:(nh + 1) * 512],
                           kc == 0, kc == KC - 1, [B_LHS[kc], B_WDB[i]], [B_PS[bk]])
            for si, s in enumerate(subs):
                c = stat_cols(1)
                pf = PS[:, 2 * si:2 * si + 2, :].rearrange("p a b -> p (a b)")
                pbufs = [B_PS[2 * si], B_PS[2 * si + 1]]
                S.add("act", lambda e, pf=pf, c=c: e.activation(out=TMP[:], in_=pf, func=AF.Square,
                                                                accum_out=st[:, c:c + 1]), pbufs, [B_TMP, B_st])
                rstd_from_ss(c, 1, 1.0 / D)
                S.add("dve", lambda e, pf=pf, c=c: e.scalar_tensor_tensor(
                    out=pf, in0=pf, scalar=st[:, c:c + 1], in1=GPB[:], op0=ALU.mult, op1=ALU.mult),
                    pbufs + [B_st, B_GPB], pbufs)
                S.add("dve", lambda e, s=s, pf=pf: e.scalar_tensor_tensor(out=X[:, s, :], in0=pf, scalar=coef,
                                                                          in1=X[:, s, :], op0=ALU.mult,
                                                                          op1=ALU.add), pbufs + [B_X[s]], [B_X[s]])

        def ffn(l, NS):
            N = NS * 128
            prenorm_to_UT(NS)
            for fc in range(FC):
                i = wg_n["n"] % NWG
                wg_n["n"] += 1
                dma("sp", WGB[i][:].rearrange("p a b c -> p (a b c)"), Wgu[l][fc], [B_Wgu[l][fc]], [B_WGB[i]])
                bg = 4 + 2 * (fc % 2)
                bu = bg + 1
                for mc in range(8):
                    mm(PS[:, bg, 0:N], WGB[i][:, mc, 0, :], UT[:, mc, 0:N], mc == 0, mc == 7, [B_WGB[i], B_UT],
                       [B_PS[bg]])
                for mc in range(8):
                    mm(PS[:, bu, 0:N], WGB[i][:, mc, 1, :], UT[:, mc, 0:N], mc == 0, mc == 7, [B_WGB[i], B_UT],
                       [B_PS[bu]])
                sg = SG[fc % 2]; bsg = B_SG[fc % 2]
                S.add("act", lambda e, sg=sg, bg=bg: e.activation(out=sg[:, 0:N], in_=PS[:, bg, 0:N], func=AF.Silu),
                      [B_PS[bg]], [bsg])
                S.add("dve", lambda e, sg=sg, bu=bu, fc=fc: e.tensor_tensor(out=HT[:, fc, 0:N], in0=PS[:, bu, 0:N],
                                                                           in1=sg[:, 0:N], op=ALU.mult),
                      [B_PS[bu], bsg], [B_HT[fc]])
            down_stage(HT, B_HT, FC, Wd[l], B_Wd[l], NS, 0 if l == 0 else 2)

        HTf = HT[:, 0:19, :].rearrange("p a b -> p (a b)").bitcast(F32)
        CK32 = HTf[:, 0:2048].rearrange("p (b f) -> p b f", f=512)
        CV32 = HTf[:, 2048:4096].rearrange("p (b f) -> p b f", f=512)
        CL32 = HTf[:, 4096:4608].rearrange("p (b f) -> p b f", f=128)
        CR32 = HTf[:, 4608:4736].rearrange("p (b f) -> p b f", f=32)
        B_CK = B_HT[0:8]; B_CV = B_HT[8:16]; B_CL = B_HT[16:18]; B_CR = [B_HT[18]]

        def kv_scratch_write(kts, nblk_each):
            if nblk_each == 4:
                kt = kts[0][0]
                dma("pool", KT[kt].rearrange("h p j -> p h j"), KTs[:], [B_KTs], [B_KT[kt]], semb=B_KTs)
                dma("pool", VV[kt].rearrange("h p x -> p h x"), VVs[:].rearrange("p h b d -> p h (b d)"), [B_VVs],
                    [B_VV[kt]], semb=B_VVs)
                dma("pool", LT[kt], LTs[:], [B_LTs], [B_LT[kt]], semb=B_LTs)
                dma("pool", LA[kt], LAs[:].rearrange("p b c -> p (b c)"), [B_LAs], [B_LA[kt]], semb=B_LAs)
                dma("pool", KR[kt], KRs[0:33, :], [B_KRs], [B_KR[kt]], semb=B_KRs)
            else:
                gid = ("kvw", kts[0][0])
                for kt, s in kts:
                    dma("pool", KT[kt].rearrange("h p j -> p h j")[:, :, 0:128], KTs[:, :, s * 128:(s + 1) * 128],
                        [B_KTs], [B_KT[kt]], semb=B_KTs, group=gid + (0,))
                    dma("pool", VV[kt].rearrange("h p x -> p h x")[:, :, 0:64], VVs[:, :, s, :], [B_VVs], [B_VV[kt]],
                        semb=B_VVs, group=gid + (1,))
                    dma("pool", LT[kt][:, 0:128], LTs[:, s * 128:(s + 1) * 128], [B_LTs], [B_LT[kt]], semb=B_LTs,
                        group=gid + (2,))
                    dma("pool", LA[kt][:, 0:128], LAs[:, s, :], [B_LAs], [B_LA[kt]], semb=B_LAs, group=gid + (3,))
                    dma("pool", KR[kt][:, 0:128], KRs[0:33, s * 128:(s + 1) * 128], [B_KRs], [B_KR[kt]], semb=B_KRs,
                        group=gid + (4,))

        def kmax_update(col, a_ap, b_ap, reads):
            c = stat_cols(2)
            S.add("act", lambda e: e.activation(out=TMP[:, 0:128], in_=a_ap, func=AF.Square,
                                                accum_out=st[:, c:c + 1]), reads, [B_TMP, B_st])
            S.add("act", lambda e: e.activation(out=TMP[:, 0:32], in_=b_ap, func=AF.Square,
                                                accum_out=st[:, c + 1:c + 2]), reads, [B_TMP, B_st])
            S.add("dve", lambda e: e.tensor_tensor(out=st[:, c:c + 1], in0=st[:, c:c + 1], in1=st[:, c + 1:c + 2],
                                                   op=ALU.add), [B_st], [B_st])
            S.add("dve", lambda e: e.tensor_tensor(out=KMX[:, col:col + 1], in0=KMX[:, col:col + 1],
                                                   in1=st[:, c:c + 1], op=ALU.max), [B_st, B_KMX], [B_KMX])

        for sI in range(NSTREAM):
            for pt in range(NPT):
                kt = NSLOT + sI * (NPT + 1) + pt
                r0 = sI * PAST + pt * 512
                dma("sp", CK32, c_k[r0:r0 + 512, :].rearrange("(b p) f -> p b f", p=128), [], B_CK)
                dma("sp", CV32, c_v[r0:r0 + 512, :].rearrange("(b p) f -> p b f", p=128), [], B_CV)
                dma("sp", CL32, c_lat[r0:r0 + 512, :].rearrange("(b p) f -> p b f", p=128), [], B_CL)
                dma("sp", CR32, c_kr[r0:r0 + 512, :].rearrange("(b p) f -> p b f", p=128), [], B_CR)
                for hp in range(4):
                    bk = 4 + hp
                    for b in range(4):
                        tr(PS[:, bk, b * 128:(b + 1) * 128], CK32[:, b, hp * 128:(hp + 1) * 128], ident_f[:],
                           B_CK + [B_identf], [B_PS[bk]])
                    copy_op(ev_engine(), KTs[:, hp, :], PS[:, bk, :], [B_PS[bk]], [B_KTs])
                copy_op("pool", VVs[:].rearrange("p h b d -> p b h d"),
                        CV32.rearrange("p b (h d) -> p b h d", d=64), B_CV, [B_VVs])
                copy_op("dve", LAs[:], CL32, B_CL, [B_LAs])
                for b in range(4):
                    tr(PS[:, 0, b * 128:(b + 1) * 128], CL32[:, b, :], ident_f[:], B_CL + [B_identf], [B_PS[0]])
                copy_op("act", LTs[:], PS[:, 0, :], [B_PS[0]], [B_LTs])
                for b in range(4):
                    tr(PS[0:32, 1, b * 128:(b + 1) * 128], CR32[:, b, :], ident_f[:], B_CR + [B_identf], [B_PS[1]])
                copy_op("dve", KRs[0:32, :], PS[0:32, 1, :], [B_PS[1]], [B_KRs])
                for b in range(4):
                    kmax_update(1 + sI, CL32[:, b, :], CR32[:, b, :], B_CL + B_CR)
                kv_scratch_write([(kt, 0)], 4)
        chk("cache")
        barrier("sp", B_KT[NSLOT:] + B_VV[NSLOT:] + B_LT[NSLOT:] + B_LA[NSLOT:] + B_KR[NSLOT:]
                + [b for l in range(2) for b in B_Wgu[l]] + [b for l in range(2) for b in B_Wd[l]] + B_WO)

        def proj_stage(T):
            own = T["own"]
            prenorm_to_UT(4)
            t0 = T["tab0"]
            ccd, ssd = T["cc"], T["ss"]
            dma("sp", CS[:], ccd[t0:t0 + 512, :].rearrange("(s p) e -> p s e", p=128), [], [B_CS])
            dma("sp", SS_[:], ssd[t0:t0 + 512, :].rearrange("(s p) e -> p s e", p=128), [], [Buf("SSt") if False else B_CS])
            if T["prompt"]:
                dma("sp", KM32[32:33, :], kmask_d[0:1, t0:t0 + 512], [], [B_KM32])
            nv = T["nvalid"]
            for s in range(4):
                bb = 4 * (s % 2)
                for (bk, c0, c1) in ((bb, 0, 416), (bb + 1, 928, 1440), (bb + 2, 1440, 1952)):
                    for mc in range(8):
                        mm(PS[:, bk, 0:c1 - c0], UT[:, mc, s * 128:(s + 1) * 128], Win[:, mc, c0:c1], mc == 0, mc == 7,
                           [B_UT, B_Win], [B_PS[bk]])
                orow = T["orow0"] + s * T["ostride"]
                c = stat_cols(2)
                S.add("act", lambda e, c=c, bb=bb: e.activation(out=TMP[:, 0:128], in_=PS[:, bb, 256:384], func=AF.Square,
                                                         accum_out=st[:, c:c + 1]), [B_PS[bb]], [B_TMP, B_st])
                rstd_from_ss(c, 1, 1.0 / 128)
                if own:
                    S.add("act", lambda e, c=c, bb=bb: e.activation(out=TMP[:, 0:256], in_=PS[:, bb, 0:256], func=AF.Square,
                                                             accum_out=st[:, c + 1:c + 2]), [B_PS[bb]], [B_TMP, B_st])
                    rstd_from_ss(c + 1, 1, 1.0 / 256)
                S.add("dve", lambda e, c=c, bb=bb: e.scalar_tensor_tensor(out=LAT32[:], in0=PS[:, bb, 256:384],
                                                                   scalar=st[:, c:c + 1], in1=gkv[:], op0=ALU.mult,
                                                                   op1=ALU.mult), [B_PS[bb], B_st, B_gkv], [B_LAT32])
                if own:
                    dma("pool", T["lat_out"][orow:orow + nv, :], LAT32[0:nv, :], [B_LAT32], [Buf("o")], semb=B_LAT32, final=True)
                if own:
                    chk("po_lat")
                copy_op("pool", LAs[:, s, :], LAT32[:], [B_LAT32], [B_LAs])
                pv = PS[:, bb + 3, :].bitcast(BF16)
                tr(pv[:, 0:128], LAs[:, s, :], ident_b[:], [B_LAs, B_const], [B_PS[bb + 3]])
                copy_op("act", LTs[:, s * 128:(s + 1) * 128], pv[:, 0:128], [B_PS[bb + 3]], [B_LTs])
                S.add("dve", lambda e, s=s, bb=bb: e.tensor_tensor(out=RT[:, 0:32], in0=PS[:, bb, 384:416], in1=CS[:, s, :],
                                                            op=ALU.mult), [B_PS[bb], B_CS], [B_RT])
                S.add("dve", lambda e, s=s, bb=bb: e.tensor_tensor(out=RT[:, 32:48], in0=PS[:, bb, 400:416],
                                                            in1=SS_[:, s, 0:16], op=ALU.mult), [B_PS[bb], B_CS], [B_RT])
                S.add("dve", lambda e, s=s, bb=bb: e.tensor_tensor(out=RT[:, 48:64], in0=PS[:, bb, 384:400],
                                                            in1=SS_[:, s, 16:32], op=ALU.mult), [B_PS[bb], B_CS], [B_RT])
                S.add("dve", lambda e: e.tensor_tensor(out=KR32[:], in0=RT[:, 0:32], in1=RT[:, 32:64], op=ALU.add),
                      [B_RT], [B_KR32])
                if own:
                    dma("pool", T["kr_out"][orow:orow + nv, :], KR32[0:nv, :], [B_KR32], [Buf("o")], semb=B_KR32, final=True)
                if own:
                    chk("po_kr")
                copy_op("pool", KRB[:], KR32[:], [B_KR32], [B_KRB])
                tr(pv[0:32, 128:256], KRB[:], ident_b[:], [B_KRB, B_const], [B_PS[bb + 3]])
                copy_op("dve", KRs[0:32, s * 128:(s + 1) * 128], pv[0:32, 128:256], [B_PS[bb + 3]], [B_KRs])
                kmax_update(T["kmx_cols"][s], LAT32[:], KR32[:], [B_LAT32, B_KR32])
                if own:
                    copy_op("act", K32[:], PS[:, bb + 1, :], [B_PS[bb + 1]], [B_K32])
                    dma("pool", T["k_out"][orow:orow + nv, :], K32[0:nv, :], [B_K32], [Buf("o")], semb=B_K32, final=True)
                    copy_op("dve", V32[:], PS[:, bb + 2, :], [B_PS[bb + 2]], [B_V32])
                    dma("pool", T["v_out"][orow:orow + nv, :], V32[0:nv, :], [B_V32], [Buf("o")], semb=B_V32, final=True)
                if own:
                    chk("po_kv")
                copy_op(cfg.vv_eng or ev_engine(), VVs[:, :, s, :], PS[:, bb + 2, :].rearrange("p (h d) -> p h d", d=64), [B_PS[bb + 2]],
                        [B_VVs])
                if own:
                    chk("po_vv")
                    scale_op("dve", CQN[:, s, :], PS[:, bb, 0:256], st[:, c + 1:c + 2], [B_PS[bb], B_st], [B_CQN])
                    chk("po_cqn")
                    for cc in range(2):
                        tr(pv[:, 256 + cc * 128:384 + cc * 128], CQN[:, s, cc * 128:(cc + 1) * 128], ident_b[:],
                           [B_CQN, B_const], [B_PS[bb + 3]])
                    chk("po_cqtr")
                    copy_op("act", CQT[:, :, s * 128:(s + 1) * 128],
                            pv[:, 256:512].rearrange("p (c j) -> p c j", j=128), [B_PS[bb + 3]], [B_CQT])
                    chk("po_cq0")
            if own:
                chk("po_cq")
            if T["prompt"]:
                copy_op("dve", KRs[32:33, :], KM32[32:33, :], [B_KM32], [B_KRs])
            else:
                S.add("pool", lambda e: e.memset(KRs[32:33, :], 0.0), [], [B_KRs])
            for hp in range(4):
                bk = 4 + hp
                for mc in range(8):
                    mm(PS[:, bk, :], Win[:, mc, 928 + hp * 128:928 + (hp + 1) * 128], UT[:, mc, :], mc == 0, mc == 7,
                       [B_Win, B_UT], [B_PS[bk]])
                copy_op(ev_engine(), KTs[:, hp, :], PS[:, bk, :], [B_PS[bk]], [B_KTs])
            if own:
                for hp in range(4):
                    bk = 4 + hp
                    for mc in range(8):
                        mm(PS[:, bk, :], Win[:, mc, 416 + hp * 128:416 + (hp + 1) * 128], UT[:, mc, :], mc == 0,
                           mc == 7, [B_Win, B_UT], [B_PS[bk]])
                    scale_op("dve", QTP[0:64, 2 * hp, :], PS[0:64, bk, :], SB_SCALE, [B_PS[bk]], [B_QT])
                    scale_op("act", QTP[64:128, 2 * hp + 1, :], PS[64:128, bk, :], SB_SCALE, [B_PS[bk]], [B_QT])
            if T["prompt"]:
                kv_scratch_write([(T["kts"][0], 0)], 4)
            else:
                kv_scratch_write([(T["kts"][s], s) for s in range(4)], 1)

        def kmax_bcast(T):
            cols = sorted(set(T["kmx_cols"]))
            for i, col in enumerate(cols):
                S.add("pe", lambda e, col=col: e.transpose(out=PS[0:1, 0, 0:128], in_=KMX[:, col:col + 1],
                                                           identity=ident_f[:]), [B_KMX, B_identf], [B_PS[0]])
                S.add("dve", lambda e: e.tensor_reduce(out=MT[0:1, 0:1], in_=PS[0:1, 0, 0:128], axis=AX.X,
                                                       op=ALU.max), [B_PS[0]], [B_MT])
                S.add("act", lambda e: e.activation(out=MT[0:1, 0:1], in_=MT[0:1, 0:1], func=AF.Ln, bias=1e-18),
                      [B_MT], [B_MT])
                S.add("act", lambda e: e.activation(out=MT[0:1, 0:1], in_=MT[0:1, 0:1], func=AF.Exp, scale=0.5),
                      [B_MT], [B_MT])
                S.add("dve", lambda e: e.tensor_scalar(out=SQ2[0:1, 0:2], in0=MT[0:1, 0:1].to_broadcast([1, 2]),
                                                       scalar1=-1.02, scalar2=None, op0=ALU.mult), [B_MT], [B_SQ2])
                mm(PS[:, 0, 0:2], ones_b[0:1, 0:128], SQ2[0:1, 0:2], True, True, [B_SQ2, B_const], [B_PS[0]])
                copy_op("dve", KMB[:, i:i + 1], PS[:, 0, 0:1], [B_PS[0]], [B_KMB])

        def qside(T):
            t0 = T["tab0"]
            dma("sp", CCT[:], T["ccT"][:, t0:t0 + 512], [], [B_CCT])
            dma("sp", SST[:], T["ssT"][:, t0:t0 + 512], [], [B_CCT])
            chk("qs_tab")
            kmax_bcast(T)
            chk("qs_kmax")
            for h in range(8):
                b0 = 4 * (h % 2)
                for cc in range(2):
                    mm(PS[0:64, b0, :], Wuq[:, cc, h * 96:h * 96 + 64], CQT[:, cc, :], cc == 0, cc == 1,
                       [B_Wuq, B_CQT], [B_PS[b0]])
                scale_op("act", QN[:], PS[0:64, b0, :], MLA_SCALE, [B_PS[b0]], [B_QN])
                mm(PS[:, b0 + 1, :], WukT[0:64, h, :], QN[:], True, True, [B_WukT, B_QN], [B_PS[b0 + 1]])
                copy_op("dve", QLT[:, h, :], PS[:, b0 + 1, :], [B_PS[b0 + 1]], [B_QLT[h]])
                for cc in range(2):
                    mm(PS[0:32, b0 + 2, :], Wuq[:, cc, h * 96 + 64:h * 96 + 96], CQT[:, cc, :], cc == 0, cc == 1,
                       [B_Wuq, B_CQT], [B_PS[b0 + 2]])
                for cc in range(2):
                    mm(PS[0:32, b0 + 3, :], Wuqs[:, cc, h, :], CQT[:, cc, :], cc == 0, cc == 1, [B_Wuqs, B_CQT],
                       [B_PS[b0 + 3]])
                S.add("dve", lambda e, b=b0 + 2: e.scalar_tensor_tensor(out=RA[:], in0=PS[0:32, b, :], scalar=MLA_SCALE,
                                                                        in1=CCT[:], op0=ALU.mult, op1=ALU.mult),
                      [B_PS[b0 + 2], B_CCT], [B_RA])
                S.add("dve", lambda e, b=b0 + 3: e.scalar_tensor_tensor(out=RB[:], in0=PS[0:32, b, :], scalar=MLA_SCALE,
                                                                        in1=SST[:], op0=ALU.mult, op1=ALU.mult),
                      [B_PS[b0 + 3], B_CCT], [B_RB])
                S.add("dve", lambda e, h=h: e.tensor_tensor(out=QRT[0:32, h, :], in0=RA[:], in1=RB[:], op=ALU.add),
                      [B_RA, B_RB], [B_QRT[h]])
                chk("qs_rope")
                S.add("act", lambda e, h=h: e.activation(out=SQ1[:], in_=QLT[:, h, :], func=AF.Square), [B_QLT[h]],
                      [B_SQ1])
                S.add("act", lambda e, h=h: e.activation(out=SQ2[0:32, :], in_=QRT[0:32, h, :], func=AF.Square),
                      [B_QRT[h]], [B_SQ2])
                mm(PS[:, b0, :], ones_b[:, 0:128], SQ1[:], True, False, [B_SQ1, B_const], [B_PS[b0]])
                mm(PS[:, b0, :], ones_b[0:32, 0:128], SQ2[0:32, :], False, True, [B_SQ2, B_const], [B_PS[b0]])
                S.add("act", lambda e, b=b0: e.activation(out=MT[64:65, :], in_=PS[64:65, b, :], func=AF.Ln,
                                                          bias=1e-18), [B_PS[b0]], [B_MT])
                S.add("act", lambda e: e.activation(out=MT[64:65, :], in_=MT[64:65, :], func=AF.Exp, scale=0.5),
                      [B_MT], [B_MT])
                if T["prompt"]:
                    S.add("dve", lambda e, h=h: e.tensor_scalar(out=QRT[64:65, h, :], in0=MT[64:65, :],
                                                                scalar1=KMB[64:65, 0:1], scalar2=None, op0=ALU.mult),
                          [B_MT, B_KMB], [B_QRT[h]])
                else:
                    for s in range(4):
                        S.add("dve", lambda e, h=h, s=s: e.tensor_scalar(
                            out=QRT[64:65, h, s * 128:(s + 1) * 128], in0=MT[64:65, s * 128:(s + 1) * 128],
                            scalar1=KMB[64:65, s:s + 1], scalar2=None, op0=ALU.mult), [B_MT, B_KMB], [B_QRT[h]])

        ld_n = {"n": 0}

        def headnorm(src, src_bufs, dst, dst_bufs, pb, N, bank):
            S.add("act", lambda e: e.activation(out=SQO[pb:pb + 64, 0:N], in_=src, func=AF.Square), src_bufs, [B_SQO])
            mm(PS[pb:pb + 64, bank, 0:N], ones_b[pb:pb + 64, 0:64], SQO[pb:pb + 64, 0:N], True, True,
               [B_SQO, B_const], [B_PS[bank]])
            S.add("act", lambda e: e.activation(out=RS[pb:pb + 64, 0:N], in_=PS[pb:pb + 64, bank, 0:N], func=AF.Ln,
                                                bias=EPS, scale=1.0 / 64), [B_PS[bank]], [B_RS])
            S.add("act", lambda e: e.activation(out=RS[pb:pb + 64, 0:N], in_=RS[pb:pb + 64, 0:N], func=AF.Exp,
                                                scale=-0.5), [B_RS], [B_RS])
            S.add("dve", lambda e: e.tensor_tensor(out=dst, in0=src, in1=RS[pb:pb + 64, 0:N], op=ALU.mult),
                  src_bufs + [B_RS], dst_bufs)

        def attention_head(g, h):
            pb = (h % 2) * 64
            hp = h // 2
            qc0, N = g["qc0"], g["N"]
            ktl = g["ktl"]
            slots = {}

            def load(n):
                kt, nblk, diag = ktl[n]
                i = ld_n["n"] % NKB
                ld_n["n"] += 1
                slots[n] = i
                W = nblk * 128
                dma("sp", ktb[i][:, 0:W], KT[kt][hp, :, 0:W], [B_KT[kt]], [B_ktb[i]])
                dma("sp", vvb[i][:, 0:nblk, 0:64],
                    VV[kt][2 * hp][:, 0:nblk * 64].rearrange("p (b d) -> p b d", d=64), [B_VV[kt]], [B_vvb[i]])
                dma("sp", vvb[i][:, 0:nblk, 64:128],
                    VV[kt][2 * hp + 1][:, 0:nblk * 64].rearrange("p (b d) -> p b d", d=64), [B_VV[kt]],
                    [B_vvb1[i]])
                dma("sp", ltb[i][:, 0:W], LT[kt][:, 0:W], [B_LT[kt]], [B_ltb[i]])
                dma("sp", lab[i][:, 0:nblk, :], LA[kt][:, 0:W].rearrange("p (b c) -> p b c", c=128), [B_LA[kt]],
                    [B_lab[i]])
                dma("sp", krb[i][0:33, 0:W], KR[kt][:, 0:W], [B_KR[kt]], [B_krb[i]])

            units = []
            for n, (kt, nblk, diag) in enumerate(ktl):
                for kb in reversed(range(nblk)):
                    units.append((n, kb, diag, kb == 0))
            NU = len(units)
            for n in range(min(NKB, len(ktl))):
                load(n)
            q_lat = QLT[:, h, qc0:qc0 + N]
            q_rope = QRT[0:65, h, qc0:qc0 + N]
            q_sb = QTP[:, h, qc0:qc0 + N]

            def stage_S(u):
                n, kb, diag, lastkb = units[u]
                i = slots[n]
                off = 384 - kb * 128
                a = u % 2
                mm(PS[:, 2 + a, 0:N], ktb[i][:, kb * 128:(kb + 1) * 128], q_sb, True, not diag,
                   [B_ktb[i], B_QT], [B_PS[2 + a]])
                if diag:
                    mm(PS[:, 2 + a, 0:N], ident_b[:], msb[:, off:off + N], False, True, [B_const], [B_PS[2 + a]])
                mm(PS[:, a, 0:N], ltb[i][:, kb * 128:(kb + 1) * 128], q_lat, True, False, [B_ltb[i], B_QLT[h]],
                   [B_PS[a]])
                mm(PS[:, a, 0:N], krb[i][0:65, kb * 128:(kb + 1) * 128], q_rope, False, not diag,
                   [B_krb[i], B_QRT[h]], [B_PS[a]])
                if diag:
                    mm(PS[:, a, 0:N], ident_b[:], mmla[:, off:off + N], False, True, [B_const], [B_PS[a]])

            def st_T(v):
                mm(PS[:, 6, 0:N], negU[:], SPB[v % 3][:, 0:N], v == 0, True, [B_SPB[v % 3], B_const], [B_PS[6]],
                   skip_group_check=(v != 0))

            def st_L(v):
                mm(PS[:, 6, 0:N], negL[:], SPB[v % 3][:, 0:N], False, True, [B_SPB[v % 3], B_const], [B_PS[6]],
                   skip_group_check=True)

            def st_EG_V(v):
                S.add("act", lambda e: e.activation(out=G32[:, 0:N], in_=PS[:, 6, 0:N], func=AF.Exp), [B_PS[6]],
                      [B_G32])
                S.add("dve", lambda e, v=v: e.tensor_tensor(out=ATb[v % 2][:, 0:N], in0=E32[v % 3][:, 0:N],
                                                            in1=G32[:, 0:N], op=ALU.mult), [B_E32[v % 3], B_G32],
                      [B_AT[v % 2]])

            def st_O(v):
                n1, kb1, _, _ = units[v]
                i1 = slots[n1]
                mm(PS[:, 7, 0:N], vvb[i1][:, kb1, :], ATb[v % 2][:, 0:N], v == 0, v == NU - 1,
                   [B_vvb[i1], B_vvb1[i1], B_AT[v % 2]], [B_PS[7]])
                if units[v][3] and n1 + NKB < len(ktl):
                    load(n1 + NKB)

            stage_S(0)
            for u in range(NU):
                n, kb, diag, lastkb = units[u]
                i = slots[n]
                a = u % 2
                first, last = (u == 0), (u == NU - 1)
                if u >= 2:
                    st_L(u - 2)
                if u >= 1:
                    st_T(u - 1)
                S.add("act", lambda e, a=a, u=u: e.activation(out=E32[u % 3][:, 0:N], in_=PS[:, 2 + a, 0:N],
                                                              func=AF.Exp), [B_PS[2 + a]], [B_E32[u % 3]])
                S.add("act", lambda e, a=a: e.activation(out=Pb[a][:, 0:N], in_=PS[:, a, 0:N], func=AF.Exp),
                      [B_PS[a]], [B_Pb[a]])
                S.add("act", lambda e, u=u: e.activation(out=SPB[u % 3][:, 0:N], in_=E32[u % 3][:, 0:N], func=AF.Ln,
                                                         bias=1.0), [B_E32[u % 3]], [B_SPB[u % 3]])
                if u + 1 < NU:
                    stage_S(u + 1)
                if u >= 2:
                    st_O(u - 2)
                mm(PS[:, 4, 0:N], lab[i][:, kb, :], Pb[a][:, 0:N], first, last, [B_lab[i], B_Pb[a]], [B_PS[4]])
                mm(PS[:, 5, 0:N], ones_b[:], Pb[a][:, 0:N], first, last, [B_Pb[a], B_const], [B_PS[5]])
                if u >= 1:
                    st_EG_V(u - 1)
            if NU >= 2:
                st_L(NU - 2)
            st_T(NU - 1)
            st_EG_V(NU - 1)
            if NU >= 2:
                st_O(NU - 2)
            st_O(NU - 1)
            S.add("dve", lambda e: e.reciprocal(out=RDEN[:, 0:N], in_=PS[:, 5, 0:N]), [B_PS[5]], [B_RDEN])
            copy_op("act", OLT[:, 0:N], PS[:, 4, 0:N], [B_PS[4]], [B_OLT])
            mm(PS[pb:pb + 64, 5, 0:N], Wuv[:, h, :], OLT[:, 0:N], True, True, [B_Wuv, B_OLT], [B_PS[5]])
            S.add("dve", lambda e: e.tensor_tensor(out=ON[pb:pb + 64, 0:N], in0=PS[pb:pb + 64, 5, 0:N],
                                                   in1=RDEN[pb:pb + 64, 0:N], op=ALU.mult), [B_PS[5], B_RDEN], [B_ON])
            headnorm(ON[pb:pb + 64, 0:N], [B_ON], OT[pb:pb + 64, hp, qc0:qc0 + N], [B_OT[hp]], pb, N, 4)
            headnorm(PS[pb:pb + 64, 7, 0:N], [B_PS[7]], OT[pb:pb + 64, 4 + hp, qc0:qc0 + N], [B_OT[4 + hp]], pb, N, 6)

        def attention_stream(g):
            qc0 = g["qc0"]
            ktl = g["ktl"]
            NQ = 64
            NKS = 2
            slots = {}
            KTA = [HT[:, 8 * i:8 * i + 4, :] for i in range(2)]
            VVA = [HT[:, 8 * i + 4:8 * i + 8, :].rearrange("p a b -> p (a b)").rearrange("p (h x) -> p h x", x=256)
                   for i in range(2)]
            B_KTA = [B_HT[8 * i:8 * i + 4] for i in range(2)]
            B_VVA = [B_HT[8 * i + 4:8 * i + 8] for i in range(2)]

            def load(n):
                kt, nblk, diag = ktl[n]
                i = ld_n["n"] % NKS
                ld_n["n"] += 1
                slots[n] = i
                W = nblk * 128
                dma("sp", KTA[i][:, :, 0:W], KT[kt].rearrange("h p j -> p h j")[:, :, 0:W], [B_KT[kt]], B_KTA[i])
                dma("sp", VVA[i][:, :, 0:nblk * 64], VV[kt].rearrange("h p x -> p h x")[:, :, 0:nblk * 64],
                    [B_VV[kt]], B_VVA[i])
                dma("sp", ltb[i][:, 0:W], LT[kt][:, 0:W], [B_LT[kt]], [B_ltb[i]])
                dma("sp", lab[i][:, 0:nblk, :], LA[kt][:, 0:W].rearrange("p (b c) -> p b c", c=128), [B_LA[kt]],
                    [B_lab[i]])
                dma("sp", krb[i][0:33, 0:W], KR[kt][:, 0:W], [B_KR[kt]], [B_krb[i]])

            units = []
            for n, (kt, nblk, diag) in enumerate(ktl):
                for kb in reversed(range(nblk)):
                    units.append((n, kb, diag, kb == 0))
            NU = len(units)
            for n in range(min(NKS, len(ktl))):
                load(n)
            N = 512

            def stage_S(u):
                n, kb, diag, lastkb = units[u]
                i = slots[n]
                a = u % 2
                ks = slice(kb * 128, (kb + 1) * 128)
                if diag:
                    mm(PS[:, 2 + a, :], ident_b[:], MSB8[:].rearrange("p h q -> p (h q)"), True, True, [B_const],
                       [B_PS[2 + a]])

                def zmm(hh, last):
                    pbh = (hh % 2) * 64
                    mm(PS[:, 2 + a, hh * 64:(hh + 1) * 64], KTA[i][:, hh // 2, ks],
                       QTP[:, hh, qc0:qc0 + NQ], not diag, True, B_KTA[i] + [B_QT], [B_PS[2 + a]],
                       skip_group_check=diag)

                for hh in (0, 2, 4, 6):
                    zmm(hh, False)
                if diag:
                    mm(PS[:, a, :], ident_b[:], MMLA8[:].rearrange("p h q -> p (h q)"), True, True, [B_const],
                       [B_PS[a]])
                for hh in range(8):
                    mm(PS[:, a, hh * 64:(hh + 1) * 64], ltb[i][:, ks], QLT[:, hh, qc0:qc0 + NQ], not diag, diag,
                       [B_ltb[i], B_QLT[hh]], [B_PS[a]], skip_group_check=diag)
                    mm(PS[:, a, hh * 64:(hh + 1) * 64], krb[i][0:65, ks], QRT[0:65, hh, qc0:qc0 + NQ], False,
                       True, [B_krb[i], B_QRT[hh]], [B_PS[a]], skip_group_check=diag)
                for hh in (1, 3, 5, 7):
                    zmm(hh, hh == 7)

            mm(PS[:, 7, 0:256], ident_b[:], ZERO[:], True, True, [B_const], [B_PS[7]])
            def st_T(v):
                mm(PS[:, 6, :], negU[:], SPB[v % 3][:], v == 0, True, [B_SPB[v % 3], B_const], [B_PS[6]],
                   skip_group_check=(v != 0))

            def st_L(v):
                mm(PS[:, 6, :], negL[:], SPB[v % 3][:], False, True, [B_SPB[v % 3], B_const], [B_PS[6]],
                   skip_group_check=True)

            def st_EG_V(v):
                S.add("act", lambda e: e.activation(out=G32[:], in_=PS[:, 6, :], func=AF.Exp), [B_PS[6]], [B_G32])
                S.add("dve", lambda e, v=v: e.tensor_tensor(out=ATb[v % 2][:], in0=E32[v % 3][:], in1=G32[:],
                                                            op=ALU.mult), [B_E32[v % 3], B_G32], [B_AT[v % 2]])

            def st_O(v):
                n1, kb1, _, lk1 = units[v]
                i1 = slots[n1]
                for hh in (0, 2, 4, 6, 1, 3, 5, 7):
                    pbh = (hh % 2) * 64
                    mm(PS[pbh:pbh + 64, 7, (hh // 2) * 64:(hh // 2 + 1) * 64],
                       VVA[i1][:, hh, kb1 * 64:(kb1 + 1) * 64], ATb[v % 2][:, hh * 64:(hh + 1) * 64], False, True,
                       B_VVA[i1] + [B_AT[v % 2]], [B_PS[7]], skip_group_check=True)
                if lk1 and n1 + NKS < len(ktl):
                    load(n1 + NKS)

            stage_S(0)
            for u in range(NU):
                n, kb, diag, lastkb = units[u]
                i = slots[n]
                a = u % 2
                first, last = (u == 0), (u == NU - 1)
                if u >= 2:
                    st_L(u - 2)
                if u >= 1:
                    st_T(u - 1)
                S.add("act", lambda e, a=a, u=u: e.activation(out=E32[u % 3][:], in_=PS[:, 2 + a, :], func=AF.Exp),
                      [B_PS[2 + a]], [B_E32[u % 3]])
                S.add("act", lambda e, a=a: e.activation(out=Pb[a][:], in_=PS[:, a, :], func=AF.Exp), [B_PS[a]],
                      [B_Pb[a]])
                S.add("act", lambda e, u=u: e.activation(out=SPB[u % 3][:], in_=E32[u % 3][:], func=AF.Ln, bias=1.0),
                      [B_E32[u % 3]], [B_SPB[u % 3]])
                if u + 1 < NU:
                    stage_S(u + 1)
                if u >= 2:
                    st_O(u - 2)
                mm(PS[:, 4, :], lab[i][:, kb, :], Pb[a][:], first, last, [B_lab[i], B_Pb[a]], [B_PS[4]])
                mm(PS[:, 5, :], ones_b[:], Pb[a][:], first, last, [B_Pb[a], B_const], [B_PS[5]])
                if u >= 1:
                    st_EG_V(u - 1)
            if NU >= 2:
                st_L(NU - 2)
            st_T(NU - 1)
            st_EG_V(NU - 1)
            if NU >= 2:
                st_O(NU - 2)
            st_O(NU - 1)
            S.add("dve", lambda e: e.reciprocal(out=RDEN[:], in_=PS[:, 5, :]), [B_PS[5]], [B_RDEN])
            copy_op("act", OLT[:], PS[:, 4, :], [B_PS[4]], [B_OLT])
            for hh in (0, 2, 4, 6, 1, 3, 5, 7):
                pbh = (hh % 2) * 64
                mm(PS[pbh:pbh + 64, 5, (hh // 2) * 64:(hh // 2 + 1) * 64], Wuv[:, hh, :],
                   OLT[:, hh * 64:(hh + 1) * 64], True, True, [B_Wuv, B_OLT], [B_PS[5]], skip_group_check=True)
            for hh in range(8):
                pbh = (hh % 2) * 64
                S.add("dve", lambda e, hh=hh, pbh=pbh: e.tensor_tensor(
                    out=ON[pbh:pbh + 64, (hh // 2) * 64:(hh // 2 + 1) * 64],
                    in0=PS[pbh:pbh + 64, 5, (hh // 2) * 64:(hh // 2 + 1) * 64],
                    in1=RDEN[pbh:pbh + 64, hh * 64:(hh + 1) * 64], op=ALU.mult), [B_PS[5], B_RDEN], [B_ON])

            def headnorm_all(src, src_bufs, dst, dst_bufs, bank):
                S.add("act", lambda e: e.activation(out=SQO[:, 0:256], in_=src, func=AF.Square), src_bufs, [B_SQO])
                for pbh in (0, 64):
                    mm(PS[pbh:pbh + 64, bank, 0:256], ones_b[pbh:pbh + 64, 0:64], SQO[pbh:pbh + 64, 0:256], True,
                       True, [B_SQO, B_const], [B_PS[bank]], skip_group_check=True)
                S.add("act", lambda e: e.activation(out=RS[:, 0:256], in_=PS[:, bank, 0:256], func=AF.Ln, bias=EPS,
                                                    scale=1.0 / 64), [B_PS[bank]], [B_RS])
                S.add("act", lambda e: e.activation(out=RS[:, 0:256], in_=RS[:, 0:256], func=AF.Exp, scale=-0.5),
                      [B_RS], [B_RS])
                S.add("dve", lambda e: e.tensor_tensor(out=dst, in0=src.rearrange("p (h q) -> p h q", q=64),
                                                       in1=RS[:, 0:256].rearrange("p (h q) -> p h q", q=64),
                                                       op=ALU.mult), src_bufs + [B_RS], dst_bufs)

            headnorm_all(ON[:, 0:256], [B_ON], OT[:, 0:4, qc0:qc0 + NQ], B_OT[0:4], 4)
            headnorm_all(PS[:, 7, 0:256], [B_PS[7]], OT[:, 4:8, qc0:qc0 + NQ], B_OT[4:8], 6)

        def final_out(T):
            nv = T["nvalid"]
            YS = HT[:, 0:16, :].rearrange("p a b -> p (a b)").bitcast(F32).rearrange("p (s d) -> p s d", d=D)
            dma("sp", GPB[:], gpost_d[:, 3 * D:4 * D], [], [B_GPB])
            for s in range(4):
                c = stat_cols(1)
                S.add("act", lambda e, s=s, c=c: e.activation(out=TMP[:], in_=X[:, s, :], func=AF.Square,
                                                              accum_out=st[:, c:c + 1]), [B_X[s]], [B_TMP, B_st])
                rstd_from_ss(c, 1, 1.0 / D)
                S.add("dve", lambda e, s=s, c=c: e.scalar_tensor_tensor(out=YS[:, s, :], in0=X[:, s, :],
                                                                        scalar=st[:, c:c + 1], in1=GPB[:],
                                                                        op0=ALU.mult, op1=ALU.mult),
                      [B_X[s], B_st, B_GPB], B_HT[4 * s:4 * s + 4])
                orow = T["orow0"] + s * T["ostride"]
                dma("pool", T["y_out"][orow:orow + nv, :], YS[0:nv, s, :], B_HT[4 * s:4 * s + 4], [Buf("o")],
                    semb=B_HT[4 * s], final=True)

        def process_tile(T):
            dma("sp", X[:], T["x"].rearrange("(s p) d -> p s d", p=128), [], B_X)
            ffn(0, 4)
            chk("ffn1")
            proj_stage(T)
            chk("proj")
            chk("proj_%d" % T.get("idx", -1))
            if not T["own"]:
                return
            qside(T)
            chk("qside")
            for g in T["groups"]:
                if g["N"] == 64:
                    attention_stream(g)
                    continue
                for h in range(8):
                    attention_head(g, h)
                    chk("att1")
            chk("att")
            down_stage(OT, B_OT, 8, WO, B_WO, 4, 1)
            chk("wout")
            ffn(1, 4)
            final_out(T)

        for slot in range(NSLOT):
            own = (slot % 2 == 1)
            oi = slot // 2
            T = dict(idx=slot, own=own, prompt=True, x=xp[slot * 512:(slot + 1) * 512, :], tab0=slot * 512, cc=cc_p, ss=ss_p,
                     ccT=ccT_p, ssT=ssT_p, nvalid=128, orow0=oi * 512, ostride=128, kts=[slot], kmx_cols=[0] * 4,
                     lat_out=lat_p, kr_out=kr_p, k_out=k_p, v_out=v_p, y_out=y_p,
                     groups=[dict(qc0=0, N=512, ktl=[(slot, 4, True)] + [(k, 4, False) for k in
                                                                         reversed(range(slot))])])
            process_tile(T)
        if NSTREAM > 0:
            assert NSTREAM == 4
            kts = [NSLOT + s * (NPT + 1) + NPT for s in range(4)]
            groups = []
            for s in range(4):
                base = NSLOT + s * (NPT + 1)
                groups.append(dict(qc0=s * 128, N=64, ktl=[(kts[s], 1, True)] + [(base + k, 4, False) for k in
                                                                                reversed(range(NPT))]))
            T = dict(own=True, prompt=False, x=xs, tab0=0, cc=cc_s, ss=ss_s, ccT=ccT_s, ssT=ssT_s, nvalid=64, orow0=0,
                     ostride=64, kts=kts, kmx_cols=[1, 2, 3, 4], lat_out=lat_s, kr_out=kr_s, k_out=k_s, v_out=v_s,
                     y_out=y_s, groups=groups)
            process_tile(T)

        S.finalize(nc, out_sems)
    return nc, S


def _rope_tables(pos):
    half = 16
    inv_freq = (np.float32(10000.0) ** (-np.arange(half, dtype=np.float32) / np.float32(half))).astype(np.float32)
    ang = pos.astype(np.float32)[:, None] * inv_freq[None, :]
    c = np.cos(ang).astype(np.float32)
    s = np.sin(ang).astype(np.float32)
    cc = np.concatenate([c, c], axis=1)
    ss = np.concatenate([-s, s], axis=1)
    return np.ascontiguousarray(cc), np.ascontiguousarray(ss)


def _consts():
    j = np.arange(128)
    ident = np.eye(128, dtype=np.float32)
    negU = -(j[:, None] >= j[None, :]).astype(np.float32)
    negL = -(j[:, None] < j[None, :]).astype(np.float32)
    x = np.arange(896) - 384
    msb = np.where(j[:, None] >= x[None, :], NEG, 0.0).astype(np.float32)
    mmla = np.where((j[:, None] // 64) > (x[None, :] // 64), NEG, 0.0).astype(np.float32)
    return dict(ident=ident, negU=negU, negL=negL, msb=msb, mmla=mmla)


_PROG_CACHE = {}


def run_cores(inp, cfg, nseq):
    ncores = 2 * nseq
    NSLOT, NSTREAM, PAST = cfg.NSLOT, cfg.NSTREAM, cfg.PAST
    key = (NSLOT, NSTREAM, PAST, cfg.DFF)
    if key not in _PROG_CACHE:
        _PROG_CACHE[key] = build_program(cfg)
    nc, _ = _PROG_CACHE[key]
    f32 = lambda a: np.ascontiguousarray(np.asarray(a, dtype=np.float32))
    shared = dict(
        w_gate1=f32(inp["w_gate1"][0]), w_gate2=f32(inp["w_gate2"][0]), w_up1=f32(inp["w_up1"][0]),
        w_up2=f32(inp["w_up2"][0]), w_down1=f32(inp["w_down1"][0]), w_down2=f32(inp["w_down2"][0]),
        w_in=f32(inp["w_in"][0]), w_uq=f32(inp["w_uq"][0]).reshape(256, 768),
        w_uk=f32(inp["w_uk"][0]).reshape(128, 512), w_uv=f32(inp["w_uv"][0]).reshape(128, 512),
        w_out=f32(inp["w_out"][0]),
    )
    gp = np.stack([f32(inp["g_pre_ff1"][0]), f32(inp["g_pre_mix"][0]), f32(inp["g_pre_ff2"][0])])
    shared["gpre"] = np.ascontiguousarray(gp.reshape(3, 8, 128).transpose(2, 0, 1).reshape(128, 24))
    shared["gq"] = np.ascontiguousarray(f32(inp["g_q"][0]).reshape(2, 128).T)
    go = np.concatenate([f32(inp["g_mla_out"][0]).reshape(-1), f32(inp["g_sb_out"][0]).reshape(-1)])
    shared["gout"] = np.ascontiguousarray(go.reshape(8, 128).T)
    gpost = np.concatenate([f32(inp["g_post_ff1"][0]), f32(inp["g_post_mix"][0]), f32(inp["g_post_ff2"][0]),
                            f32(inp["g_final"][0])])
    shared["gpost"] = np.ascontiguousarray(np.broadcast_to(gpost[None, :], (128, 4 * D)))
    shared["gkv"] = np.ascontiguousarray(np.broadcast_to(f32(inp["g_kv"][0])[None, :], (128, 128)))
    shared.update(_consts())
    pos_s = np.zeros(NSTREAM * 128, dtype=np.float32)
    for s in range(NSTREAM):
        pos_s[s * 128:s * 128 + 64] = PAST + np.arange(64)
    cc_s, ss_s = _rope_tables(pos_s)
    shared.update(cc_s=cc_s, ss_s=ss_s, ccT_s=np.ascontiguousarray(cc_s.T), ssT_s=np.ascontiguousarray(ss_s.T))
    xprompt = f32(inp["x_prompt"])
    xsample = f32(inp["x_sample"])
    c_lat = f32(inp["cache_mla_latent"][0]); c_kr = f32(inp["cache_mla_krope"][0])
    c_k = f32(inp["cache_sb_k"][0]); c_v = f32(inp["cache_sb_v"][0])
    in_maps = []
    for core in range(ncores):
        seq, c = core // 2, core % 2
        m = dict(shared)
        if c == 0:
            m["xp"] = np.ascontiguousarray(xprompt[seq])
            pos = np.arange(NSLOT * 512, dtype=np.float32)
            km = np.zeros((1, NSLOT * 512), np.float32)
        else:
            m["xp"] = np.ascontiguousarray(np.concatenate([np.zeros((512, D), np.float32),
                                                           xprompt[seq][:(NSLOT - 1) * 512]], axis=0))
            pos = np.concatenate([np.zeros(512, np.float32), np.arange((NSLOT - 1) * 512, dtype=np.float32)])
            km = np.zeros((1, NSLOT * 512), np.float32)
            km[0, :512] = NEG
        cc, ss = _rope_tables(pos)
        m.update(cc_p=cc, ss_p=ss, ccT_p=np.ascontiguousarray(cc.T), ssT_p=np.ascontiguousarray(ss.T), kmask=km)
        st0 = core * NSTREAM
        xs = np.zeros((NSTREAM * 128, D), np.float32)
        for s in range(NSTREAM):
            xs[s * 128:s * 128 + 64] = xsample[st0 + s]
        m["xs"] = xs
        m["c_lat"] = np.ascontiguousarray(c_lat[st0:st0 + NSTREAM].reshape(NSTREAM * PAST, 128))
        m["c_kr"] = np.ascontiguousarray(c_kr[st0:st0 + NSTREAM].reshape(NSTREAM * PAST, 32))
        m["c_k"] = np.ascontiguousarray(c_k[st0:st0 + NSTREAM].reshape(NSTREAM * PAST, 512))
        m["c_v"] = np.ascontiguousarray(c_v[st0:st0 + NSTREAM].reshape(NSTREAM * PAST, 512))
        in_maps.append(m)
    res = run_bass_kernel_spmd(nc, in_maps, core_ids=list(range(ncores)))
    R = res.results
    SEQ = NSLOT * 512
    NB = ncores * NSTREAM
    y_p = np.zeros((nseq, SEQ, D), np.float32)
    lat_p = np.zeros((1, nseq, SEQ, 128), np.float32)
    kr_p = np.zeros((1, nseq, SEQ, 32), np.float32)
    k_p = np.zeros((1, nseq, SEQ, 8, 64), np.float32)
    v_p = np.zeros((1, nseq, SEQ, 8, 64), np.float32)
    y_s = np.zeros((NB, 64, D), np.float32)
    lat_s = np.zeros((1, NB, 64, 128), np.float32)
    kr_s = np.zeros((1, NB, 64, 32), np.float32)
    k_s = np.zeros((1, NB, 64, 8, 64), np.float32)
    v_s = np.zeros((1, NB, 64, 8, 64), np.float32)
    for core in range(ncores):
        seq, c = core // 2, core % 2
        r = R[core]
        for oi in range(cfg.NOWN):
            tile = 2 * oi + 1 if c == 0 else 2 * oi
            a, b = tile * 512, (tile + 1) * 512
            o0, o1 = oi * 512, (oi + 1) * 512
            y_p[seq, a:b] = r["y_p"][o0:o1]
            lat_p[0, seq, a:b] = r["lat_p"][o0:o1]
            kr_p[0, seq, a:b] = r["kr_p"][o0:o1]
            k_p[0, seq, a:b] = r["k_p"][o0:o1].reshape(512, 8, 64)
            v_p[0, seq, a:b] = r["v_p"][o0:o1].reshape(512, 8, 64)
        for s in range(NSTREAM):
            g = core * NSTREAM + s
            y_s[g] = r["y_s"][s * 64:(s + 1) * 64]
            lat_s[0, g] = r["lat_s"][s * 64:(s + 1) * 64]
            kr_s[0, g] = r["kr_s"][s * 64:(s + 1) * 64]
            k_s[0, g] = r["k_s"][s * 64:(s + 1) * 64].reshape(64, 8, 64)
            v_s[0, g] = r["v_s"][s * 64:(s + 1) * 64].reshape(64, 8, 64)
    return (y_p, y_s, lat_p, kr_p, k_p, v_p, lat_s, kr_s, k_s, v_s)


def kernel(**inputs):
    cfg = Cfg(nslot=16, nstream=4, past=4096, dff=2816)
    return run_cores(inputs, cfg, nseq=4)
```

```python
import numpy as np
import ml_dtypes
import concourse.bass as bass
import concourse.mybir as mybir
from concourse.bass_utils import run_bass_kernel_spmd

F32 = mybir.dt.float32
BF16 = mybir.dt.bfloat16
ALU = mybir.AluOpType
AF = mybir.ActivationFunctionType
AX = mybir.AxisListType

SAME_ENGINE_SYNC = True


class Buf:
    __slots__ = ("name", "w", "r", "dsem")

    def __init__(self, name):
        self.name = name
        self.w = None
        self.r = {}
        self.dsem = None


class Op:
    __slots__ = ("eng", "fn", "deps", "dsem", "need", "tok", "idx", "grp")

    def __init__(self, eng, fn, dsem):
        self.eng = eng
        self.fn = fn
        self.deps = []
        self.dsem = dsem
        self.need = dsem is not None
        self.tok = None
        self.grp = None


class DSem:
    def __init__(self, name):
        self.name = name
        self.h = None
        self.n = 0


class Sched:
    ENGS = ("pe", "act", "dve", "pool", "sp")

    def __init__(self):
        self.ops = {e: [] for e in self.ENGS}
        self.dsems = []
        self.dmas = []
        self.nops = 0

    def dsem(self, name):
        d = DSem(name)
        self.dsems.append(d)
        return d

    stopped = False

    def add(self, eng, fn, reads=(), writes=(), dsem=None, group=None):
        if self.stopped:
            return None
        op = Op(eng, fn, dsem)
        op.grp = group
        self.nops += 1
        op.idx = self.nops
        best = {}

        def consider(d):
            if d is None or d is op:
                return
            k = id(d.dsem) if d.dsem is not None else d.eng
            o = best.get(k)
            if o is None or d.idx > o.idx:
                best[k] = d

        for b in reads:
            consider(b.w)
        for b in writes:
            consider(b.w)
            for d in b.r.values():
                consider(d)
        op.deps = list(best.values())
        if fn is not None:
            k = id(dsem) if dsem is not None else eng
            for b in reads:
                b.r[k] = op
        for b in writes:
            b.w = op
            b.r = {}
        self.ops[eng].append(op)
        if dsem is not None:
            self.dmas.append(op)
        return op

    def finalize(self, nc, final_waits):
        for e in self.ENGS:
            for op in self.ops[e]:
                for d in op.deps:
                    if d.dsem is not None:
                        continue
                    if d.eng == op.eng and (d.eng == "pe" or d.eng == "sp" or not SAME_ENGINE_SYNC):
                        continue
                    d.need = True
        stack = []
        esem = {}
        import contextlib
        with contextlib.ExitStack() as es:
            for e in self.ENGS:
                esem[e] = es.enter_context(nc.semaphore("s_" + e))
            for d in self.dsems:
                d.h = es.enter_context(nc.semaphore("d_" + d.name))
                d.n = 0
            for e in self.ENGS:
                n = 0
                for op in self.ops[e]:
                    if op.dsem is not None:
                        continue
                    if op.need:
                        n += 1
                        op.tok = (esem[e], n, e)
            gmax = {}
            for op in self.dmas:
                op.dsem.n += 16
                op.tok = (op.dsem.h, op.dsem.n, "dma:" + op.dsem.name)
                if op.grp is not None:
                    gmax[op.grp] = op.tok
            for op in self.dmas:
                if op.grp is not None:
                    assert gmax[op.grp][0] is op.tok[0]
                    op.tok = gmax[op.grp]
            block = es.enter_context(nc.Block())

            def make(e):
                def body(eng):
                    known = {}
                    ownn = 0
                    for op in self.ops[e]:
                        for d in op.deps:
                            if d.dsem is None and d.eng == e:
                                if e in ("pe", "sp") or not SAME_ENGINE_SYNC:
                                    continue
                            sem, val, who = d.tok
                            k = id(sem)
                            if known.get(k, 0) >= val:
                                continue
                            known[k] = val
                            eng.wait_ge(sem, val)
                        if op.fn is None:
                            continue
                        ins = op.fn(eng)
                        if op.dsem is not None:
                            ins.then_inc(op.dsem.h, 16)
                        elif op.need:
                            ins.then_inc(esem[e], 1)
                    if e == "sp":
                        for d in final_waits:
                            eng.wait_ge(d.h, d.n)
                return body

            block.tensor(make("pe"))
            block.scalar(make("act"))
            block.vector(make("dve"))
            block.gpsimd(make("pool"))
            block.sync(make("sp"))


D = 1024
MC = 8
HEADS = 8
MLA_SCALE = 96.0 ** -0.5
SB_SCALE = 0.125
EPS = 1e-6
NEG = -30000.0


class StopBuild(Exception):
    pass


class Cfg:
    stop = None
    vv_eng = "dve"

    def __init__(self, nslot=16, nstream=4, past=4096, dff=2816):
        self.NSLOT = nslot
        self.NSTREAM = nstream
        self.PAST = past
        self.DFF = dff
        self.FC = dff // 128
        self.NPT = past // 512
        self.NOWN = nslot // 2
        self.NKT = nslot + nstream * (self.NPT + 1)


def build_program(cfg):
    NSLOT, NSTREAM, PAST, DFF, FC, NPT, NOWN, NKT = (cfg.NSLOT, cfg.NSTREAM, cfg.PAST, cfg.DFF, cfg.FC,
                                                     cfg.NPT, cfg.NOWN, cfg.NKT)
    import contextlib
    nc = bass.Bass("TRN2", target_bir_lowering=False)
    S = Sched()

    def din(name, shape):
        return nc.dram_tensor(name, list(shape), F32, kind="ExternalInput").ap()

    def dout(name, shape):
        return nc.dram_tensor(name, list(shape), F32, kind="ExternalOutput").ap()

    def dscr(name, shape):
        return nc.dram_tensor(name, list(shape), BF16).ap()

    NP_TOK = NSLOT * 512
    NS_TOK = NSTREAM * 64
    NS_PAD = NSTREAM * 128
    xp = din("xp", [NP_TOK, D])
    xs = din("xs", [NS_PAD, D])
    c_lat = din("c_lat", [NSTREAM * PAST, 128])
    c_kr = din("c_kr", [NSTREAM * PAST, 32])
    c_k = din("c_k", [NSTREAM * PAST, 512])
    c_v = din("c_v", [NSTREAM * PAST, 512])
    w_gate = [din("w_gate1", [D, DFF]), din("w_gate2", [D, DFF])]
    w_up = [din("w_up1", [D, DFF]), din("w_up2", [D, DFF])]
    w_down = [din("w_down1", [DFF, D]), din("w_down2", [DFF, D])]
    w_in = din("w_in", [D, 1952])
    w_uq = din("w_uq", [256, 768])
    w_uk = din("w_uk", [128, 512])
    w_uv = din("w_uv", [128, 512])
    w_out = din("w_out", [D, D])
    gpre_d = din("gpre", [128, 3 * 8])
    gq_d = din("gq", [128, 2])
    gout_d = din("gout", [128, 8])
    gpost_d = din("gpost", [128, 4 * D])
    gkv_d = din("gkv", [128, 128])
    cc_p = din("cc_p", [NP_TOK, 32])
    ss_p = din("ss_p", [NP_TOK, 32])
    cc_s = din("cc_s", [NS_PAD, 32])
    ss_s = din("ss_s", [NS_PAD, 32])
    ccT_p = din("ccT_p", [32, NP_TOK])
    ssT_p = din("ssT_p", [32, NP_TOK])
    ccT_s = din("ccT_s", [32, NS_PAD])
    ssT_s = din("ssT_s", [32, NS_PAD])
    kmask_d = din("kmask", [1, NP_TOK])
    ident_d = din("ident", [128, 128])
    negU_d = din("negU", [128, 128])
    negL_d = din("negL", [128, 128])
    msb_d = din("msb", [128, 896])
    mmla_d = din("mmla", [128, 896])
    y_p = dout("y_p", [NOWN * 512, D])
    lat_p = dout("lat_p", [NOWN * 512, 128])
    kr_p = dout("kr_p", [NOWN * 512, 32])
    k_p = dout("k_p", [NOWN * 512, 512])
    v_p = dout("v_p", [NOWN * 512, 512])
    y_s = dout("y_s", [NS_TOK, D])
    lat_s = dout("lat_s", [NS_TOK, 128])
    kr_s = dout("kr_s", [NS_TOK, 32])
    k_s = dout("k_s", [NS_TOK, 512])
    v_s = dout("v_s", [NS_TOK, 512])
    Wgu = [dscr("Wgu1", [FC, 128, 2048]), dscr("Wgu2", [FC, 128, 2048])]
    Wd = [dscr("Wd1", [FC, 128, 1024]), dscr("Wd2", [FC, 128, 1024])]
    WO = dscr("WO", [8, 128, 1024])
    KT = dscr("KT", [NKT, 4, 128, 512])
    VV = dscr("VV", [NKT, 8, 128, 256])
    LT = dscr("LT", [NKT, 128, 512])
    LA = dscr("LA", [NKT, 128, 512])
    KR = dscr("KR", [NKT, 33, 512])
    B_Wgu = [[Buf("Wgu%d_%d" % (l, f)) for f in range(FC)] for l in range(2)]
    B_Wd = [[Buf("Wd%d_%d" % (l, f)) for f in range(FC)] for l in range(2)]
    B_WO = [Buf("WO%d" % k) for k in range(8)]
    B_KT = [Buf("KT%d" % k) for k in range(NKT)]
    B_VV = [Buf("VV%d" % k) for k in range(NKT)]
    B_LT = [Buf("LT%d" % k) for k in range(NKT)]
    B_LA = [Buf("LA%d" % k) for k in range(NKT)]
    B_KR = [Buf("KR%d" % k) for k in range(NKT)]

    es = contextlib.ExitStack()
    with es:
        def sb(name, shape, dt):
            return es.enter_context(nc.sbuf_tensor("s_" + name, list(shape), dt))

        PS = es.enter_context(nc.psum_tensor("PS", [128, 8, 512], F32))
        B_PS = [Buf("ps%d" % i) for i in range(8)]

        Win = sb("Win", [128, 8, 1952], BF16); B_Win = Buf("Win")
        Wuq = sb("Wuq", [128, 2, 768], BF16); B_Wuq = Buf("Wuq")
        Wuqs = sb("Wuqs", [128, 2, 8, 32], BF16); B_Wuqs = Buf("Wuqs")
        WukT = sb("WukT", [128, 8, 128], BF16); B_WukT = Buf("WukT")
        Wuv = sb("Wuv", [128, 8, 64], BF16); B_Wuv = Buf("Wuv")
        gpre = sb("gpre_t", [128, 3, 8], F32); B_gpre = Buf("gpre")
        gq = sb("gq_t", [128, 2], F32); B_gq = Buf("gq")
        gout = sb("gout_t", [128, 8], F32); B_gout = Buf("gout")
        gkv = sb("gkv_t", [128, 128], F32); B_gkv = Buf("gkv")
        ident_f = sb("ident_f", [128, 128], F32)
        ident_b = sb("ident_b", [128, 128], BF16)
        negU = sb("negU", [128, 128], BF16)
        negL = sb("negL", [128, 128], BF16)
        ones_b = sb("ones_b", [128, 128], BF16)
        ones_f = sb("ones_f", [128, 128], F32)
        msb = sb("msb", [128, 896], BF16)
        mmla = sb("mmla", [128, 896], BF16)
        MSB8 = sb("MSB8", [128, 8, 64], BF16)
        MMLA8 = sb("MMLA8", [128, 8, 64], BF16)
        ZERO = sb("ZERO", [128, 256], BF16)
        B_const = Buf("const")
        B_identf = Buf("identf")
        X = sb("X", [128, 4, D], F32); B_X = [Buf("X%d" % i) for i in range(4)]
        HT = sb("HT", [128, FC, 512], BF16) if FC >= 19 else sb("HT", [128, 19, 512], BF16)
        NHB = max(FC, 19)
        B_HT = [Buf("HT%d" % i) for i in range(NHB)]
        XN = HT[:, 0:8, :].rearrange("p a b -> p (a b)").rearrange("p (s d) -> p s d", d=D)
        UT = sb("UT", [128, 8, 512], BF16); B_UT = Buf("UT")
        TMP = sb("TMP", [128, D], BF16); B_TMP = Buf("TMP")
        NWG = 3
        WGB = [sb("WGB%d" % i, [128, 8, 2, 128], BF16) for i in range(NWG)]; B_WGB = [Buf("WGB%d" % i) for i in range(NWG)]
        NWD = 3
        WDB = [sb("WDB%d" % i, [128, 1024], BF16) for i in range(NWD)]; B_WDB = [Buf("WDB%d" % i) for i in range(NWD)]
        st = sb("stats", [128, 64], F32); B_st = Buf("stats")
        LAT32 = sb("LAT32", [128, 128], F32); B_LAT32 = Buf("LAT32")
        KR32 = sb("KR32", [128, 32], F32); B_KR32 = Buf("KR32")
        KRB = sb("KRB", [128, 32], BF16); B_KRB = Buf("KRB")
        RT = sb("RT", [128, 64], F32); B_RT = Buf("RT")
        KTs = sb("KTs", [128, 4, 512], BF16); B_KTs = Buf("KTs")
        VVs = sb("VVs", [128, 8, 4, 64], BF16); B_VVs = Buf("VVs")
        LTs = sb("LTs", [128, 512], BF16); B_LTs = Buf("LTs")
        LAs = sb("LAs", [128, 4, 128], BF16); B_LAs = Buf("LAs")
        KRs = sb("KRs", [64, 512], BF16); B_KRs = Buf("KRs")
        CQN = sb("CQN", [128, 4, 256], BF16); B_CQN = Buf("CQN")
        CQT = sb("CQT", [128, 2, 512], BF16); B_CQT = Buf("CQT")
        CS = sb("CS", [128, 4, 32], F32); SS_ = sb("SSt", [128, 4, 32], F32); B_CS = Buf("CS")
        CCT = sb("CCT", [32, 512], F32); SST = sb("SST", [32, 512], F32); B_CCT = Buf("CCT")
        QTP = sb("QTP", [128, 8, 512], BF16); B_QT = Buf("QTP")
        QLT = sb("QLT", [128, 8, 512], BF16); B_QLT = [Buf("QLT%d" % h) for h in range(8)]
        QRT = sb("QRT", [128, 8, 512], BF16); B_QRT = [Buf("QRT%d" % h) for h in range(8)]
        KMX = sb("KMX", [128, 1 + NSTREAM], F32); B_KMX = Buf("KMX")
        KMB = sb("KMB", [128, 4], F32); B_KMB = Buf("KMB")
        NKB = 3
        ktb = [sb("ktb%d" % i, [128, 512], BF16) for i in range(NKB)]
        vvb = [sb("vvb%d" % i, [128, 4, 128], BF16) for i in range(NKB)]
        ltb = [sb("ltb%d" % i, [128, 512], BF16) for i in range(NKB)]
        lab = [sb("lab%d" % i, [128, 4, 128], BF16) for i in range(NKB)]
        krb = [sb("krb%d" % i, [128, 512], BF16) for i in range(NKB)]
        B_ktb = [Buf("ktb%d" % i) for i in range(NKB)]; B_vvb = [Buf("vvb%d" % i) for i in range(NKB)]; B_vvb1 = [Buf("vvc%d" % i) for i in range(NKB)]
        B_mla = [Buf("mlab%d" % i) for i in range(NKB)]
        B_ltb = [Buf("ltb%d" % i) for i in range(NKB)]; B_lab = [Buf("lab%d" % i) for i in range(NKB)]
        B_krb = [Buf("krb%d" % i) for i in range(NKB)]
        Pb = [sb("Pb%d" % i, [128, 512], BF16) for i in range(2)]; B_Pb = [Buf("Pb0"), Buf("Pb1")]
        E32 = [sb("E32_%d" % i, [128, 512], F32) for i in range(3)]; B_E32 = [Buf("E0"), Buf("E1"), Buf("E2")]
        SPB = [sb("SPB%d" % i, [128, 512], BF16) for i in range(3)]; B_SPB = [Buf("SP0"), Buf("SP1"), Buf("SP2")]
        G32 = sb("G32", [128, 512], F32); B_G32 = Buf("G32")
        ATb = [sb("AT%d" % i, [128, 512], BF16) for i in range(2)]; B_AT = [Buf("AT0"), Buf("AT1")]
        OLT = sb("OLT", [128, 512], BF16); B_OLT = Buf("OLT")
        RDEN = sb("RDEN", [128, 512], F32); B_RDEN = Buf("RDEN")
        ON = sb("ON", [128, 512], F32); B_ON = Buf("ON")
        SQO = sb("SQO", [128, 512], BF16); B_SQO = Buf("SQO")
        RS = sb("RS", [128, 512], F32); B_RS = Buf("RS")
        OT = sb("OT", [128, 8, 512], BF16); B_OT = [Buf("OT%d" % i) for i in range(8)]
        GPB = sb("GPB", [128, D], F32); B_GPB = Buf("GPB")
        SG = E32; B_SG = B_E32
        RA = E32[0][0:32, :]; RB = E32[1][0:32, :]; B_RA = B_E32[0]; B_RB = B_E32[1]
        QN = SPB[0][0:64, :]; B_QN = B_SPB[0]
        SQ1 = Pb[0]; SQ2 = Pb[1]; B_SQ1 = B_Pb[0]; B_SQ2 = B_Pb[1]
        MT = RS; B_MT = B_RS
        KM32 = RS; B_KM32 = B_RS
        K32 = G32; B_K32 = B_G32
        V32 = ON; B_V32 = B_ON


        rr = {"n": 0}

        def ew_engine():
            rr["n"] += 1
            return ("dve", "pool", "act")[rr["n"] % 3]

        def ev_engine():
            rr["n"] += 1
            return ("dve", "act")[rr["n"] % 2]

        def copy_op(eng, out, in_, reads, writes):
            if eng == "act":
                S.add("act", lambda e: e.activation(out=out, in_=in_, func=AF.Copy), reads, writes)
            else:
                S.add(eng, lambda e: e.tensor_copy(out=out, in_=in_), reads, writes)

        def scale_op(eng, out, in_, sc, reads, writes):
            if eng == "act":
                S.add("act", lambda e: e.activation(out=out, in_=in_, func=AF.Copy, scale=sc), reads, writes)
            else:
                S.add(eng, lambda e: e.tensor_scalar(out=out, in0=in_, scalar1=sc, scalar2=None, op0=ALU.mult),
                      reads, writes)

        out_sems = []

        def dma(q, out, in_, reads, writes, semb=None, group=None, final=False):
            b = semb if semb is not None else writes[0]
            if b.dsem is None:
                b.dsem = {}
            if q not in b.dsem:
                b.dsem[q] = S.dsem(b.name + "_" + q)
            ds = b.dsem[q]
            if final and ds not in out_sems:
                out_sems.append(ds)
            S.add(q, lambda e: e.dma_start(out=out, in_=in_), reads, writes, dsem=ds, group=group)

        def barrier(q, bufs):
            S.add(q, None, bufs, [])

        def mm(out, lhsT, rhs, start, stop, reads, writes, **kw):
            S.add("pe", lambda e: e.matmul(out, lhsT=lhsT, rhs=rhs, start=start, stop=stop, **kw), reads, writes)

        def tr(out, in_, ident, reads, writes):
            S.add("pe", lambda e: e.transpose(out=out, in_=in_, identity=ident), reads, writes)

        def chk(name):
            if cfg.stop == name:
                S.stopped = True

        Xf = X[:].rearrange("p a b -> p (a b)")

        def load_const(dst_b, src_d, ncol, q):
            dma("sp", Xf[:, q * 1024:q * 1024 + ncol], src_d, [], [B_X[q]])
            copy_op("dve", dst_b, Xf[:, q * 1024:q * 1024 + ncol], [B_X[q]], [B_const])

        dma("sp", ident_f[:], ident_d, [], [B_identf])
        load_const(ident_b[:], ident_d, 128, 0)
        load_const(negU[:], negU_d, 128, 1)
        load_const(negL[:], negL_d, 128, 2)
        load_const(msb[:], msb_d, 896, 3)
        load_const(mmla[:], mmla_d, 896, 0)
        for hh in range(8):
            copy_op("dve", MSB8[:, hh, :], msb[:, 384:448], [B_const], [B_const])
            copy_op("dve", MMLA8[:, hh, :], mmla[:, 384:448], [B_const], [B_const])
        S.add("pool", lambda e: e.memset(ZERO[:], 0.0), [], [B_const])
        S.add("pool", lambda e: e.memset(ones_b[:], 1.0), [], [B_const])
        S.add("pool", lambda e: e.memset(ones_f[:], 1.0), [], [B_const])
        S.add("pool", lambda e: e.memset(KMX[:], 0.0), [], [B_KMX])
        for i in range(NKB):
            S.add("pool", lambda e, i=i: e.memset(krb[i][:], 0.0), [], [B_krb[i]])
            S.add("pool", lambda e, i=i: e.memset(krb[i][64:65, :], 1.0), [], [B_krb[i]])
        S.add("pool", lambda e: e.memset(QTP[:], 0.0), [], [B_QT])
        S.add("pool", lambda e: e.memset(QRT[:], 0.0), [], B_QRT)
        S.add("pool", lambda e: e.memset(QRT[32:33, :, :], 1.0), [], B_QRT)
        S.add("pool", lambda e: e.memset(KRs[:], 0.0), [], [B_KRs])
        S.add("pool", lambda e: e.memset(OT[:], 0.0), [], B_OT)
        dma("sp", gpre[:].rearrange("p a b -> p (a b)"), gpre_d, [], [B_gpre])
        dma("sp", gq[:], gq_d, [], [B_gq])
        dma("sp", gout[:], gout_d, [], [B_gout])
        dma("sp", gkv[:], gkv_d, [], [B_gkv])

        stg_n = {"n": 0}

        def stage(src_ap, ncol):
            h = stg_n["n"] % 2
            stg_n["n"] += 1
            v = Xf[:, h * 2048:h * 2048 + ncol]
            bufs = [B_X[2 * h], B_X[2 * h + 1]]
            return h, v, bufs

        for mc in range(8):
            h, v, bufs = stage(None, 1952)
            dma("sp", v, w_in[mc * 128:(mc + 1) * 128, :], [], bufs)
            scale_op(ew_engine(), Win[:, mc, :], v, gpre[:, 1, mc:mc + 1], bufs + [B_gpre], [B_Win])
        for cc in range(2):
            h, v, bufs = stage(None, 768)
            dma("sp", v, w_uq[cc * 128:(cc + 1) * 128, :], [], bufs)
            scale_op("dve", Wuq[:, cc, :], v, gq[:, cc:cc + 1], bufs + [B_gq], [B_Wuq])
            wv = Wuq[:, cc, :].rearrange("p (h e) -> p h e", e=96)
            copy_op("dve", Wuqs[:, cc, :, 0:16], wv[:, :, 80:96], [B_Wuq], [B_Wuqs])
            copy_op("dve", Wuqs[:, cc, :, 16:32], wv[:, :, 64:80], [B_Wuq], [B_Wuqs])
        h, v, bufs = stage(None, 512)
        dma("sp", v, w_uv, [], bufs)
        copy_op("dve", Wuv[:].rearrange("p h d -> p (h d)"), v, bufs, [B_Wuv])
        h, v, bufs = stage(None, 512)
        dma("sp", v, w_uk, [], bufs)
        for hh in range(8):
            bk = 4 + (hh % 4)
            tr(PS[0:64, bk, 0:128], v[:, hh * 64:(hh + 1) * 64], ident_f[:], bufs + [B_identf], [B_PS[bk]])
            copy_op(ev_engine(), WukT[0:64, hh, :], PS[0:64, bk, 0:128], [B_PS[bk]], [B_WukT])
        def prep_w(l, fc):
            gsel = 0 if l == 0 else 2
            wslot = WGB[fc % NWG]; bws = B_WGB[fc % NWG]
            for gi, wsrc in enumerate((w_gate[l], w_up[l])):
                h, v, bufs = stage(None, 1024)
                src = wsrc.rearrange("(mc p) f -> p mc f", p=128)[:, :, fc * 128:(fc + 1) * 128]
                v3 = v.rearrange("p (a b) -> p a b", b=128)
                dma("sp", v3, src, [], bufs)
                gb = gpre[:, gsel, :].unsqueeze(2).to_broadcast([128, 8, 128])
                eng = ("dve", "pool")[(fc + gi) % 2]
                S.add(eng, lambda e, o=wslot[:, :, gi, :], i0=v3, g=gb: e.tensor_tensor(out=o, in0=i0, in1=g,
                                                                                      op=ALU.mult),
                      bufs + [B_gpre], [bws])
            dma("pool", Wgu[l][fc], wslot[:].rearrange("p a b c -> p (a b c)"), [bws], [B_Wgu[l][fc]], semb=bws)
            dslot = WDB[fc % NWD]; bds = B_WDB[fc % NWD]
            h, v, bufs = stage(None, 1024)
            dma("sp", v, w_down[l][fc * 128:(fc + 1) * 128, :], [], bufs)
            copy_op("act", dslot[:], v, bufs, [bds])
            dma("pool", Wd[l][fc], dslot[:], [bds], [B_Wd[l][fc]], semb=bds)
        def prep_o(kc):
            dslot = WDB[kc % NWD]; bds = B_WDB[kc % NWD]
            h, v, bufs = stage(None, 1024)
            dma("sp", v, w_out[kc * 128:(kc + 1) * 128, :], [], bufs)
            scale_op("act", dslot[:], v, gout[:, kc:kc + 1], bufs + [B_gout], [bds])
            dma("pool", WO[kc], dslot[:], [bds], [B_WO[kc]], semb=bds)

        chk("prep")
        st_n = {"n": 0}

        def stat_cols(n):
            c = st_n["n"]
            if c + n > 64:
                c = 0
            st_n["n"] = c + n
            return c

        def rstd_from_ss(c, n, inv_dim):
            S.add("act", lambda e: e.activation(out=st[:, c:c + n], in_=st[:, c:c + n], func=AF.Ln, bias=EPS,
                                                scale=inv_dim), [B_st], [B_st])
            S.add("act", lambda e: e.activation(out=st[:, c:c + n], in_=st[:, c:c + n], func=AF.Exp, scale=-0.5),
                  [B_st], [B_st])

        def prenorm_to_UT(NS):
            for s in range(NS):
                c = stat_cols(1)
                S.add("act", lambda e, s=s, c=c: e.activation(out=TMP[:], in_=X[:, s, :], func=AF.Square,
                                                              accum_out=st[:, c:c + 1]), [B_X[s]], [B_TMP, B_st])
                rstd_from_ss(c, 1, 1.0 / D)
                scale_op(("dve", "act")[s % 2], XN[:, s, :], X[:, s, :], st[:, c:c + 1],
                         [B_X[s], B_st], [B_HT[2 * s], B_HT[2 * s + 1]])
                bk = 2 * s
                pv = PS[:, bk, :].bitcast(BF16)
                for mc in range(8):
                    tr(pv[:, mc * 128:(mc + 1) * 128], XN[:, s, mc * 128:(mc + 1) * 128], ident_b[:],
                       [B_HT[2 * s], B_HT[2 * s + 1], B_const], [B_PS[bk]])
                copy_op(("act", "dve")[s % 2], UT[:, :, s * 128:(s + 1) * 128],
                        pv.rearrange("p (m t) -> p m t", t=128), [B_PS[bk]], [B_UT])

        wg_n = {"n": 0}
        wd_n = {"n": 0}

        def down_stage(LHS, B_LHS, KC, Wscr, B_Wscr, NS, gsel):
            coef = 1.0 if gsel == 1 else 0.5
            dma("sp", GPB[:], gpost_d[:, gsel * D:(gsel + 1) * D], [], [B_GPB])
            subs = list(range(NS))
            for kc in range(KC):
                i = wd_n["n"] % NWD
                wd_n["n"] += 1
                dma("sp", WDB[i][:], Wscr[kc], [B_Wscr[kc]], [B_WDB[i]])
                for si, s in enumerate(subs):
                    for nh in range(2):
                        bk = 2 * si + nh
                        mm(PS[:, bk, :], LHS[:, kc, s * 128:(s + 1) * 128], WDB[i][:, nh * 512:(nh + 1) * 512],
                           kc == 0, kc == KC - 1, [B_LHS[kc], B_WDB[i]], [B_PS[bk]])
            for si, s in enumerate(subs):
                c = stat_cols(1)
                pf = PS[:, 2 * si:2 * si + 2, :].rearrange("p a b -> p (a b)")
                pbufs = [B_PS[2 * si], B_PS[2 * si + 1]]
                S.add("act", lambda e, pf=pf, c=c: e.activation(out=TMP[:], in_=pf, func=AF.Square,
                                                                accum_out=st[:, c:c + 1]), pbufs, [B_TMP, B_st])
                rstd_from_ss(c, 1, 1.0 / D)
                S.add("dve", lambda e, pf=pf, c=c: e.scalar_tensor_tensor(
                    out=pf, in0=pf, scalar=st[:, c:c + 1], in1=GPB[:], op0=ALU.mult, op1=ALU.mult),
                    pbufs + [B_st, B_GPB], pbufs)
                S.add("dve", lambda e, s=s, pf=pf: e.scalar_tensor_tensor(out=X[:, s, :], in0=pf, scalar=coef,
                                                                          in1=X[:, s, :], op0=ALU.mult,
                                                                          op1=ALU.add), pbufs + [B_X[s]], [B_X[s]])

        def ffn(l, NS):
            N = NS * 128
            prenorm_to_UT(NS)
            for fc in range(FC):
                i = wg_n["n"] % NWG
                wg_n["n"] += 1
                dma("sp", WGB[i][:].rearrange("p a b c -> p (a b c)"), Wgu[l][fc], [B_Wgu[l][fc]], [B_WGB[i]])
                bg = 4 + 2 * (fc % 2)
                bu = bg + 1
                for mc in range(8):
                    mm(PS[:, bg, 0:N], WGB[i][:, mc, 0, :], UT[:, mc, 0:N], mc == 0, mc == 7, [B_WGB[i], B_UT],
                       [B_PS[bg]])
                for mc in range(8):
                    mm(PS[:, bu, 0:N], WGB[i][:, mc, 1, :], UT[:, mc, 0:N], mc == 0, mc == 7, [B_WGB[i], B_UT],
                       [B_PS[bu]])
                sg = SG[fc % 2]; bsg = B_SG[fc % 2]
                S.add("act", lambda e, sg=sg, bg=bg: e.activation(out=sg[:, 0:N], in_=PS[:, bg, 0:N], func=AF.Silu),
                      [B_PS[bg]], [bsg])
                S.add("dve", lambda e, sg=sg, bu=bu, fc=fc: e.tensor_tensor(out=HT[:, fc, 0:N], in0=PS[:, bu, 0:N],
                                                                           in1=sg[:, 0:N], op=ALU.mult),
                      [B_PS[bu], bsg], [B_HT[fc]])
            down_stage(HT, B_HT, FC, Wd[l], B_Wd[l], NS, 0 if l == 0 else 2)

        HTf = HT[:, 0:19, :].rearrange("p a b -> p (a b)").bitcast(F32)
        CK32 = HTf[:, 0:2048].rearrange("p (b f) -> p b f", f=512)
        CV32 = HTf[:, 2048:4096].rearrange("p (b f) -> p b f", f=512)
        CL32 = HTf[:, 4096:4608].rearrange("p (b f) -> p b f", f=128)
        CR32 = HTf[:, 4608:4736].rearrange("p (b f) -> p b f", f=32)
        B_CK = B_HT[0:8]; B_CV = B_HT[8:16]; B_CL = B_HT[16:18]; B_CR = [B_HT[18]]

        def kv_scratch_write(kts, nblk_each):
            if nblk_each == 4:
                kt = kts[0][0]
                dma("pool", KT[kt].rearrange("h p j -> p h j"), KTs[:], [B_KTs], [B_KT[kt]], semb=B_KTs)
                dma("pool", VV[kt].rearrange("h p x -> p h x"), VVs[:].rearrange("p h b d -> p h (b d)"), [B_VVs],
                    [B_VV[kt]], semb=B_VVs)
                dma("pool", LT[kt], LTs[:], [B_LTs], [B_LT[kt]], semb=B_LTs)
                dma("pool", LA[kt], LAs[:].rearrange("p b c -> p (b c)"), [B_LAs], [B_LA[kt]], semb=B_LAs)
                dma("pool", KR[kt], KRs[0:33, :], [B_KRs], [B_KR[kt]], semb=B_KRs)
            else:
                gid = ("kvw", kts[0][0])
                for kt, s in kts:
                    dma("pool", KT[kt].rearrange("h p j -> p h j")[:, :, 0:128], KTs[:, :, s * 128:(s + 1) * 128],
                        [B_KTs], [B_KT[kt]], semb=B_KTs, group=gid + (0,))
                    dma("pool", VV[kt].rearrange("h p x -> p h x")[:, :, 0:64], VVs[:, :, s, :], [B_VVs], [B_VV[kt]],
                        semb=B_VVs, group=gid + (1,))
                    dma("pool", LT[kt][:, 0:128], LTs[:, s * 128:(s + 1) * 128], [B_LTs], [B_LT[kt]], semb=B_LTs,
                        group=gid + (2,))
                    dma("pool", LA[kt][:, 0:128], LAs[:, s, :], [B_LAs], [B_LA[kt]], semb=B_LAs, group=gid + (3,))
                    dma("pool", KR[kt][:, 0:128], KRs[0:33, s * 128:(s + 1) * 128], [B_KRs], [B_KR[kt]], semb=B_KRs,
                        group=gid + (4,))

        def kmax_update(col, a_ap, b_ap, reads):
            c = stat_cols(2)
            S.add("act", lambda e: e.activation(out=TMP[:, 0:128], in_=a_ap, func=AF.Square,
                                                accum_out=st[:, c:c + 1]), reads, [B_TMP, B_st])
            S.add("act", lambda e: e.activation(out=TMP[:, 0:32], in_=b_ap, func=AF.Square,
                                                accum_out=st[:, c + 1:c + 2]), reads, [B_TMP, B_st])
            S.add("dve", lambda e: e.tensor_tensor(out=st[:, c:c + 1], in0=st[:, c:c + 1], in1=st[:, c + 1:c + 2],
                                                   op=ALU.add), [B_st], [B_st])
            S.add("dve", lambda e: e.tensor_tensor(out=KMX[:, col:col + 1], in0=KMX[:, col:col + 1],
                                                   in1=st[:, c:c + 1], op=ALU.max), [B_st, B_KMX], [B_KMX])

        def prep_c(sI, pt):
            kt = NSLOT + sI * (NPT + 1) + pt
            r0 = sI * PAST + pt * 512
            dma("sp", CK32, c_k[r0:r0 + 512, :].rearrange("(b p) f -> p b f", p=128), [], B_CK)
            dma("sp", CV32, c_v[r0:r0 + 512, :].rearrange("(b p) f -> p b f", p=128), [], B_CV)
            dma("sp", CL32, c_lat[r0:r0 + 512, :].rearrange("(b p) f -> p b f", p=128), [], B_CL)
            dma("sp", CR32, c_kr[r0:r0 + 512, :].rearrange("(b p) f -> p b f", p=128), [], B_CR)
            for hp in range(4):
                bk = 4 + hp
                for b in range(4):
                    tr(PS[:, bk, b * 128:(b + 1) * 128], CK32[:, b, hp * 128:(hp + 1) * 128], ident_f[:],
                       B_CK + [B_identf], [B_PS[bk]])
                copy_op(ev_engine(), KTs[:, hp, :], PS[:, bk, :], [B_PS[bk]], [B_KTs])
            copy_op("pool", VVs[:].rearrange("p h b d -> p b h d"),
                    CV32.rearrange("p b (h d) -> p b h d", d=64), B_CV, [B_VVs])
            copy_op("dve", LAs[:], CL32, B_CL, [B_LAs])
            for b in range(4):
                tr(PS[:, 0, b * 128:(b + 1) * 128], CL32[:, b, :], ident_f[:], B_CL + [B_identf], [B_PS[0]])
            copy_op("act", LTs[:], PS[:, 0, :], [B_PS[0]], [B_LTs])
            for b in range(4):
                tr(PS[0:32, 1, b * 128:(b + 1) * 128], CR32[:, b, :], ident_f[:], B_CR + [B_identf], [B_PS[1]])
            copy_op("dve", KRs[0:32, :], PS[0:32, 1, :], [B_PS[1]], [B_KRs])
            for b in range(4):
                kmax_update(1 + sI, CL32[:, b, :], CR32[:, b, :], B_CL + B_CR)
            kv_scratch_write([(kt, 0)], 4)

        tasks_w = [(prep_w, (l, fc)) for l in range(2) for fc in range(FC)] + [(prep_o, (kc,)) for kc in range(8)]
        tasks_c = [(prep_c, (sI, pt)) for sI in range(NSTREAM) for pt in range(NPT)]
        iw = ic = 0
        while iw < len(tasks_w) or ic < len(tasks_c):
            if iw < len(tasks_w) and (ic >= len(tasks_c) or iw * len(tasks_c) <= ic * len(tasks_w)):
                f, a_ = tasks_w[iw]; iw += 1
            else:
                f, a_ = tasks_c[ic]; ic += 1
            f(*a_)
        chk("cache")
        barrier("sp", B_KT[NSLOT:] + B_VV[NSLOT:] + B_LT[NSLOT:] + B_LA[NSLOT:] + B_KR[NSLOT:]
                + [b for l in range(2) for b in B_Wgu[l]] + [b for l in range(2) for b in B_Wd[l]] + B_WO)

        def proj_stage(T):
            own = T["own"]
            prenorm_to_UT(4)
            t0 = T["tab0"]
            ccd, ssd = T["cc"], T["ss"]
            dma("sp", CS[:], ccd[t0:t0 + 512, :].rearrange("(s p) e -> p s e", p=128), [], [B_CS])
            dma("sp", SS_[:], ssd[t0:t0 + 512, :].rearrange("(s p) e -> p s e", p=128), [], [Buf("SSt") if False else B_CS])
            if T["prompt"]:
                dma("sp", KM32[32:33, :], kmask_d[0:1, t0:t0 + 512], [], [B_KM32])
            nv = T["nvalid"]
            for s in range(4):
                bb = 4 * (s % 2)
                for (bk, c0, c1) in ((bb, 0, 416), (bb + 1, 928, 1440), (bb + 2, 1440, 1952)):
                    for mc in range(8):
                        mm(PS[:, bk, 0:c1 - c0], UT[:, mc, s * 128:(s + 1) * 128], Win[:, mc, c0:c1], mc == 0, mc == 7,
                           [B_UT, B_Win], [B_PS[bk]])
                orow = T["orow0"] + s * T["ostride"]
                c = stat_cols(2)
                S.add("act", lambda e, c=c, bb=bb: e.activation(out=TMP[:, 0:128], in_=PS[:, bb, 256:384], func=AF.Square,
                                                         accum_out=st[:, c:c + 1]), [B_PS[bb]], [B_TMP, B_st])
                rstd_from_ss(c, 1, 1.0 / 128)
                if own:
                    S.add("act", lambda e, c=c, bb=bb: e.activation(out=TMP[:, 0:256], in_=PS[:, bb, 0:256], func=AF.Square,
                                                             accum_out=st[:, c + 1:c + 2]), [B_PS[bb]], [B_TMP, B_st])
                    rstd_from_ss(c + 1, 1, 1.0 / 256)
                S.add("dve", lambda e, c=c, bb=bb: e.scalar_tensor_tensor(out=LAT32[:], in0=PS[:, bb, 256:384],
                                                                   scalar=st[:, c:c + 1], in1=gkv[:], op0=ALU.mult,
                                                                   op1=ALU.mult), [B_PS[bb], B_st, B_gkv], [B_LAT32])
                if own:
                    dma("pool", T["lat_out"][orow:orow + nv, :], LAT32[0:nv, :], [B_LAT32], [Buf("o")], semb=B_LAT32, final=True)
                if own:
                    chk("po_lat")
                copy_op("pool", LAs[:, s, :], LAT32[:], [B_LAT32], [B_LAs])
                pv = PS[:, bb + 3, :].bitcast(BF16)
                tr(pv[:, 0:128], LAs[:, s, :], ident_b[:], [B_LAs, B_const], [B_PS[bb + 3]])
                copy_op("act", LTs[:, s * 128:(s + 1) * 128], pv[:, 0:128], [B_PS[bb + 3]], [B_LTs])
                S.add("dve", lambda e, s=s, bb=bb: e.tensor_tensor(out=RT[:, 0:32], in0=PS[:, bb, 384:416], in1=CS[:, s, :],
                                                            op=ALU.mult), [B_PS[bb], B_CS], [B_RT])
                S.add("dve", lambda e, s=s, bb=bb: e.tensor_tensor(out=RT[:, 32:48], in0=PS[:, bb, 400:416],
                                                            in1=SS_[:, s, 0:16], op=ALU.mult), [B_PS[bb], B_CS], [B_RT])
                S.add("dve", lambda e, s=s, bb=bb: e.tensor_tensor(out=RT[:, 48:64], in0=PS[:, bb, 384:400],
                                                            in1=SS_[:, s, 16:32], op=ALU.mult), [B_PS[bb], B_CS], [B_RT])
                S.add("dve", lambda e: e.tensor_tensor(out=KR32[:], in0=RT[:, 0:32], in1=RT[:, 32:64], op=ALU.add),
                      [B_RT], [B_KR32])
                if own:
                    dma("pool", T["kr_out"][orow:orow + nv, :], KR32[0:nv, :], [B_KR32], [Buf("o")], semb=B_KR32, final=True)
                if own:
                    chk("po_kr")
                copy_op("pool", KRB[:], KR32[:], [B_KR32], [B_KRB])
                tr(pv[0:32, 128:256], KRB[:], ident_b[:], [B_KRB, B_const], [B_PS[bb + 3]])
                copy_op("dve", KRs[0:32, s * 128:(s + 1) * 128], pv[0:32, 128:256], [B_PS[bb + 3]], [B_KRs])
                kmax_update(T["kmx_cols"][s], LAT32[:], KR32[:], [B_LAT32, B_KR32])
                if own:
                    copy_op("act", K32[:], PS[:, bb + 1, :], [B_PS[bb + 1]], [B_K32])
                    dma("pool", T["k_out"][orow:orow + nv, :], K32[0:nv, :], [B_K32], [Buf("o")], semb=B_K32, final=True)
                    copy_op("dve", V32[:], PS[:, bb + 2, :], [B_PS[bb + 2]], [B_V32])
                    dma("pool", T["v_out"][orow:orow + nv, :], V32[0:nv, :], [B_V32], [Buf("o")], semb=B_V32, final=True)
                if own:
                    chk("po_kv")
                copy_op(cfg.vv_eng or ev_engine(), VVs[:, :, s, :], PS[:, bb + 2, :].rearrange("p (h d) -> p h d", d=64), [B_PS[bb + 2]],
                        [B_VVs])
                if own:
                    chk("po_vv")
                    scale_op("dve", CQN[:, s, :], PS[:, bb, 0:256], st[:, c + 1:c + 2], [B_PS[bb], B_st], [B_CQN])
                    chk("po_cqn")
                    for cc in range(2):
                        tr(pv[:, 256 + cc * 128:384 + cc * 128], CQN[:, s, cc * 128:(cc + 1) * 128], ident_b[:],
                           [B_CQN, B_const], [B_PS[bb + 3]])
                    chk("po_cqtr")
                    copy_op("act", CQT[:, :, s * 128:(s + 1) * 128],
                            pv[:, 256:512].rearrange("p (c j) -> p c j", j=128), [B_PS[bb + 3]], [B_CQT])
                    chk("po_cq0")
            if own:
                chk("po_cq")
            if T["prompt"]:
                copy_op("dve", KRs[32:33, :], KM32[32:33, :], [B_KM32], [B_KRs])
            else:
                S.add("pool", lambda e: e.memset(KRs[32:33, :], 0.0), [], [B_KRs])
            for hp in range(4):
                bk = 4 + hp
                for mc in range(8):
                    mm(PS[:, bk, :], Win[:, mc, 928 + hp * 128:928 + (hp + 1) * 128], UT[:, mc, :], mc == 0, mc == 7,
                       [B_Win, B_UT], [B_PS[bk]])
                copy_op(ev_engine(), KTs[:, hp, :], PS[:, bk, :], [B_PS[bk]], [B_KTs])
            if own:
                for hp in range(4):
                    bk = 4 + hp
                    for mc in range(8):
                        mm(PS[:, bk, :], Win[:, mc, 416 + hp * 128:416 + (hp + 1) * 128], UT[:, mc, :], mc == 0,
                           mc == 7, [B_Win, B_UT], [B_PS[bk]])
                    scale_op("dve", QTP[0:64, 2 * hp, :], PS[0:64, bk, :], SB_SCALE, [B_PS[bk]], [B_QT])
                    scale_op("act", QTP[64:128, 2 * hp + 1, :], PS[64:128, bk, :], SB_SCALE, [B_PS[bk]], [B_QT])
            if T["prompt"]:
                kv_scratch_write([(T["kts"][0], 0)], 4)
            else:
                kv_scratch_write([(T["kts"][s], s) for s in range(4)], 1)

        def kmax_bcast(T):
            cols = sorted(set(T["kmx_cols"]))
            for i, col in enumerate(cols):
                S.add("pe", lambda e, col=col: e.transpose(out=PS[0:1, 0, 0:128], in_=KMX[:, col:col + 1],
                                                           identity=ident_f[:]), [B_KMX, B_identf], [B_PS[0]])
                S.add("dve", lambda e: e.tensor_reduce(out=MT[0:1, 0:1], in_=PS[0:1, 0, 0:128], axis=AX.X,
                                                       op=ALU.max), [B_PS[0]], [B_MT])
                S.add("act", lambda e: e.activation(out=MT[0:1, 0:1], in_=MT[0:1, 0:1], func=AF.Ln, bias=1e-18),
                      [B_MT], [B_MT])
                S.add("act", lambda e: e.activation(out=MT[0:1, 0:1], in_=MT[0:1, 0:1], func=AF.Exp, scale=0.5),
                      [B_MT], [B_MT])
                S.add("dve", lambda e: e.tensor_scalar(out=SQ2[0:1, 0:2], in0=MT[0:1, 0:1].to_broadcast([1, 2]),
                                                       scalar1=-1.02, scalar2=None, op0=ALU.mult), [B_MT], [B_SQ2])
                mm(PS[:, 0, 0:2], ones_b[0:1, 0:128], SQ2[0:1, 0:2], True, True, [B_SQ2, B_const], [B_PS[0]])
                copy_op("dve", KMB[:, i:i + 1], PS[:, 0, 0:1], [B_PS[0]], [B_KMB])

        def qside(T):
            t0 = T["tab0"]
            dma("sp", CCT[:], T["ccT"][:, t0:t0 + 512], [], [B_CCT])
            dma("sp", SST[:], T["ssT"][:, t0:t0 + 512], [], [B_CCT])
            chk("qs_tab")
            kmax_bcast(T)
            chk("qs_kmax")
            for h in range(8):
                b0 = 4 * (h % 2)
                for cc in range(2):
                    mm(PS[0:64, b0, :], Wuq[:, cc, h * 96:h * 96 + 64], CQT[:, cc, :], cc == 0, cc == 1,
                       [B_Wuq, B_CQT], [B_PS[b0]])
                scale_op("act", QN[:], PS[0:64, b0, :], MLA_SCALE, [B_PS[b0]], [B_QN])
                mm(PS[:, b0 + 1, :], WukT[0:64, h, :], QN[:], True, True, [B_WukT, B_QN], [B_PS[b0 + 1]])
                copy_op("dve", QLT[:, h, :], PS[:, b0 + 1, :], [B_PS[b0 + 1]], [B_QLT[h]])
                for cc in range(2):
                    mm(PS[0:32, b0 + 2, :], Wuq[:, cc, h * 96 + 64:h * 96 + 96], CQT[:, cc, :], cc == 0, cc == 1,
                       [B_Wuq, B_CQT], [B_PS[b0 + 2]])
                for cc in range(2):
                    mm(PS[0:32, b0 + 3, :], Wuqs[:, cc, h, :], CQT[:, cc, :], cc == 0, cc == 1, [B_Wuqs, B_CQT],
                       [B_PS[b0 + 3]])
                S.add("dve", lambda e, b=b0 + 2: e.scalar_tensor_tensor(out=RA[:], in0=PS[0:32, b, :], scalar=MLA_SCALE,
                                                                        in1=CCT[:], op0=ALU.mult, op1=ALU.mult),
                      [B_PS[b0 + 2], B_CCT], [B_RA])
                S.add("dve", lambda e, b=b0 + 3: e.scalar_tensor_tensor(out=RB[:], in0=PS[0:32, b, :], scalar=MLA_SCALE,
                                                                        in1=SST[:], op0=ALU.mult, op1=ALU.mult),
                      [B_PS[b0 + 3], B_CCT], [B_RB])
                S.add("dve", lambda e, h=h: e.tensor_tensor(out=QRT[0:32, h, :], in0=RA[:], in1=RB[:], op=ALU.add),
                      [B_RA, B_RB], [B_QRT[h]])
                chk("qs_rope")
                S.add("act", lambda e, h=h: e.activation(out=SQ1[:], in_=QLT[:, h, :], func=AF.Square), [B_QLT[h]],
                      [B_SQ1])
                S.add("act", lambda e, h=h: e.activation(out=SQ2[0:32, :], in_=QRT[0:32, h, :], func=AF.Square),
                      [B_QRT[h]], [B_SQ2])
                mm(PS[:, b0, :], ones_b[:, 0:128], SQ1[:], True, False, [B_SQ1, B_const], [B_PS[b0]])
                mm(PS[:, b0, :], ones_b[0:32, 0:128], SQ2[0:32, :], False, True, [B_SQ2, B_const], [B_PS[b0]])
                S.add("act", lambda e, b=b0: e.activation(out=MT[64:65, :], in_=PS[64:65, b, :], func=AF.Ln,
                                                          bias=1e-18), [B_PS[b0]], [B_MT])
                S.add("act", lambda e: e.activation(out=MT[64:65, :], in_=MT[64:65, :], func=AF.Exp, scale=0.5),
                      [B_MT], [B_MT])
                if T["prompt"]:
                    S.add("dve", lambda e, h=h: e.tensor_scalar(out=QRT[64:65, h, :], in0=MT[64:65, :],
                                                                scalar1=KMB[64:65, 0:1], scalar2=None, op0=ALU.mult),
                          [B_MT, B_KMB], [B_QRT[h]])
                else:
                    for s in range(4):
                        S.add("dve", lambda e, h=h, s=s: e.tensor_scalar(
                            out=QRT[64:65, h, s * 128:(s + 1) * 128], in0=MT[64:65, s * 128:(s + 1) * 128],
                            scalar1=KMB[64:65, s:s + 1], scalar2=None, op0=ALU.mult), [B_MT, B_KMB], [B_QRT[h]])

        ld_n = {"n": 0}

        def headnorm(src, src_bufs, dst, dst_bufs, pb, N, bank):
            S.add("act", lambda e: e.activation(out=SQO[pb:pb + 64, 0:N], in_=src, func=AF.Square), src_bufs, [B_SQO])
            mm(PS[pb:pb + 64, bank, 0:N], ones_b[pb:pb + 64, 0:64], SQO[pb:pb + 64, 0:N], True, True,
               [B_SQO, B_const], [B_PS[bank]])
            S.add("act", lambda e: e.activation(out=RS[pb:pb + 64, 0:N], in_=PS[pb:pb + 64, bank, 0:N], func=AF.Ln,
                                                bias=EPS, scale=1.0 / 64), [B_PS[bank]], [B_RS])
            S.add("act", lambda e: e.activation(out=RS[pb:pb + 64, 0:N], in_=RS[pb:pb + 64, 0:N], func=AF.Exp,
                                                scale=-0.5), [B_RS], [B_RS])
            S.add("dve", lambda e: e.tensor_tensor(out=dst, in0=src, in1=RS[pb:pb + 64, 0:N], op=ALU.mult),
                  src_bufs + [B_RS], dst_bufs)

        def attention_head(g, h):
            pb = (h % 2) * 64
            hp = h // 2
            qc0, N = g["qc0"], g["N"]
            ktl = g["ktl"]
            slots = {}

            def load(n):
                kt, nblk, diag = ktl[n]
                i = ld_n["n"] % NKB
                ld_n["n"] += 1
                slots[n] = i
                W = nblk * 128
                dma("sp", ktb[i][:, 0:W], KT[kt][hp, :, 0:W], [B_KT[kt]], [B_ktb[i]])
                dma("sp", vvb[i][:, 0:nblk, 0:64],
                    VV[kt][2 * hp][:, 0:nblk * 64].rearrange("p (b d) -> p b d", d=64), [B_VV[kt]], [B_vvb[i]])
                dma("sp", vvb[i][:, 0:nblk, 64:128],
                    VV[kt][2 * hp + 1][:, 0:nblk * 64].rearrange("p (b d) -> p b d", d=64), [B_VV[kt]],
                    [B_vvb1[i]])
                dma("sp", ltb[i][:, 0:W], LT[kt][:, 0:W], [B_LT[kt]], [B_ltb[i]])
                dma("sp", lab[i][:, 0:nblk, :], LA[kt][:, 0:W].rearrange("p (b c) -> p b c", c=128), [B_LA[kt]],
                    [B_lab[i]])
                dma("sp", krb[i][0:33, 0:W], KR[kt][:, 0:W], [B_KR[kt]], [B_krb[i]])

            units = []
            for n, (kt, nblk, diag) in enumerate(ktl):
                for kb in reversed(range(nblk)):
                    units.append((n, kb, diag, kb == 0))
            NU = len(units)
            for n in range(min(NKB, len(ktl))):
                load(n)
            q_lat = QLT[:, h, qc0:qc0 + N]
            q_rope = QRT[0:65, h, qc0:qc0 + N]
            q_sb = QTP[:, h, qc0:qc0 + N]

            def stage_S(u):
                n, kb, diag, lastkb = units[u]
                i = slots[n]
                off = 384 - kb * 128
                a = u % 2
                mm(PS[:, 2 + a, 0:N], ktb[i][:, kb * 128:(kb + 1) * 128], q_sb, True, not diag,
                   [B_ktb[i], B_QT], [B_PS[2 + a]])
                if diag:
                    mm(PS[:, 2 + a, 0:N], ident_b[:], msb[:, off:off + N], False, True, [B_const], [B_PS[2 + a]])
                mm(PS[:, a, 0:N], ltb[i][:, kb * 128:(kb + 1) * 128], q_lat, True, False, [B_ltb[i], B_QLT[h]],
                   [B_PS[a]])
                mm(PS[:, a, 0:N], krb[i][0:65, kb * 128:(kb + 1) * 128], q_rope, False, not diag,
                   [B_krb[i], B_QRT[h]], [B_PS[a]])
                if diag:
                    mm(PS[:, a, 0:N], ident_b[:], mmla[:, off:off + N], False, True, [B_const], [B_PS[a]])

            def st_T(v):
                mm(PS[:, 6, 0:N], negU[:], SPB[v % 3][:, 0:N], v == 0, True, [B_SPB[v % 3], B_const], [B_PS[6]],
                   skip_group_check=(v != 0))

            def st_L(v):
                mm(PS[:, 6, 0:N], negL[:], SPB[v % 3][:, 0:N], False, True, [B_SPB[v % 3], B_const], [B_PS[6]],
                   skip_group_check=True)

            def st_EG_V(v):
                S.add("act", lambda e: e.activation(out=G32[:, 0:N], in_=PS[:, 6, 0:N], func=AF.Exp), [B_PS[6]],
                      [B_G32])
                S.add("dve", lambda e, v=v: e.tensor_tensor(out=ATb[v % 2][:, 0:N], in0=E32[v % 3][:, 0:N],
                                                            in1=G32[:, 0:N], op=ALU.mult), [B_E32[v % 3], B_G32],
                      [B_AT[v % 2]])

            def st_O(v):
                n1, kb1, _, _ = units[v]
                i1 = slots[n1]
                mm(PS[:, 7, 0:N], vvb[i1][:, kb1, :], ATb[v % 2][:, 0:N], v == 0, v == NU - 1,
                   [B_vvb[i1], B_vvb1[i1], B_AT[v % 2]], [B_PS[7]])
                if units[v][3] and n1 + NKB < len(ktl):
                    load(n1 + NKB)

            stage_S(0)
            for u in range(NU):
                n, kb, diag, lastkb = units[u]
                i = slots[n]
                a = u % 2
                first, last = (u == 0), (u == NU - 1)
                if u >= 2:
                    st_L(u - 2)
                if u >= 1:
                    st_T(u - 1)
                S.add("act", lambda e, a=a, u=u: e.activation(out=E32[u % 3][:, 0:N], in_=PS[:, 2 + a, 0:N],
                                                              func=AF.Exp), [B_PS[2 + a]], [B_E32[u % 3]])
                S.add("act", lambda e, a=a: e.activation(out=Pb[a][:, 0:N], in_=PS[:, a, 0:N], func=AF.Exp),
                      [B_PS[a]], [B_Pb[a]])
                S.add("act", lambda e, u=u: e.activation(out=SPB[u % 3][:, 0:N], in_=E32[u % 3][:, 0:N], func=AF.Ln,
                                                         bias=1.0), [B_E32[u % 3]], [B_SPB[u % 3]])
                if u + 1 < NU:
                    stage_S(u + 1)
                if u >= 2:
                    st_O(u - 2)
                mm(PS[:, 4, 0:N], lab[i][:, kb, :], Pb[a][:, 0:N], first, last, [B_lab[i], B_Pb[a]], [B_PS[4]])
                mm(PS[:, 5, 0:N], ones_b[:], Pb[a][:, 0:N], first, last, [B_Pb[a], B_const], [B_PS[5]])
                if u >= 1:
                    st_EG_V(u - 1)
            if NU >= 2:
                st_L(NU - 2)
            st_T(NU - 1)
            st_EG_V(NU - 1)
            if NU >= 2:
                st_O(NU - 2)
            st_O(NU - 1)
            S.add("dve", lambda e: e.reciprocal(out=RDEN[:, 0:N], in_=PS[:, 5, 0:N]), [B_PS[5]], [B_RDEN])
            copy_op("act", OLT[:, 0:N], PS[:, 4, 0:N], [B_PS[4]], [B_OLT])
            mm(PS[pb:pb + 64, 5, 0:N], Wuv[:, h, :], OLT[:, 0:N], True, True, [B_Wuv, B_OLT], [B_PS[5]])
            S.add("dve", lambda e: e.tensor_tensor(out=ON[pb:pb + 64, 0:N], in0=PS[pb:pb + 64, 5, 0:N],
                                                   in1=RDEN[pb:pb + 64, 0:N], op=ALU.mult), [B_PS[5], B_RDEN], [B_ON])
            headnorm(ON[pb:pb + 64, 0:N], [B_ON], OT[pb:pb + 64, hp, qc0:qc0 + N], [B_OT[hp]], pb, N, 4)
            headnorm(PS[pb:pb + 64, 7, 0:N], [B_PS[7]], OT[pb:pb + 64, 4 + hp, qc0:qc0 + N], [B_OT[4 + hp]], pb, N, 6)

        def attention_stream(g):
            qc0 = g["qc0"]
            ktl = g["ktl"]
            NQ = 64
            NKS = 2
            slots = {}
            KTA = [HT[:, 8 * i:8 * i + 4, :] for i in range(2)]
            VVA = [HT[:, 8 * i + 4:8 * i + 8, :].rearrange("p a b -> p (a b)").rearrange("p (h x) -> p h x", x=256)
                   for i in range(2)]
            B_KTA = [B_HT[8 * i:8 * i + 4] for i in range(2)]
            B_VVA = [B_HT[8 * i + 4:8 * i + 8] for i in range(2)]

            def load(n):
                kt, nblk, diag = ktl[n]
                i = ld_n["n"] % NKS
                ld_n["n"] += 1
                slots[n] = i
                W = nblk * 128
                dma("sp", KTA[i][:, :, 0:W], KT[kt].rearrange("h p j -> p h j")[:, :, 0:W], [B_KT[kt]], B_KTA[i])
                dma("sp", VVA[i][:, :, 0:nblk * 64], VV[kt].rearrange("h p x -> p h x")[:, :, 0:nblk * 64],
                    [B_VV[kt]], B_VVA[i])
                dma("sp", ltb[i][:, 0:W], LT[kt][:, 0:W], [B_LT[kt]], [B_ltb[i]])
                dma("sp", lab[i][:, 0:nblk, :], LA[kt][:, 0:W].rearrange("p (b c) -> p b c", c=128), [B_LA[kt]],
                    [B_lab[i]])
                dma("sp", krb[i][0:33, 0:W], KR[kt][:, 0:W], [B_KR[kt]], [B_krb[i]])

            units = []
            for n, (kt, nblk, diag) in enumerate(ktl):
                for kb in reversed(range(nblk)):
                    units.append((n, kb, diag, kb == 0))
            NU = len(units)
            for n in range(min(NKS, len(ktl))):
                load(n)
            N = 512

            def stage_S(u):
                n, kb, diag, lastkb = units[u]
                i = slots[n]
                a = u % 2
                ks = slice(kb * 128, (kb + 1) * 128)
                if diag:
                    mm(PS[:, 2 + a, :], ident_b[:], MSB8[:].rearrange("p h q -> p (h q)"), True, True, [B_const],
                       [B_PS[2 + a]])

                def zmm(hh, last):
                    pbh = (hh % 2) * 64
                    mm(PS[:, 2 + a, hh * 64:(hh + 1) * 64], KTA[i][:, hh // 2, ks],
                       QTP[:, hh, qc0:qc0 + NQ], not diag, True, B_KTA[i] + [B_QT], [B_PS[2 + a]],
                       skip_group_check=diag)

                for hh in (0, 2, 4, 6):
                    zmm(hh, False)
                if diag:
                    mm(PS[:, a, :], ident_b[:], MMLA8[:].rearrange("p h q -> p (h q)"), True, True, [B_const],
                       [B_PS[a]])
                for hh in range(8):
                    mm(PS[:, a, hh * 64:(hh + 1) * 64], ltb[i][:, ks], QLT[:, hh, qc0:qc0 + NQ], not diag, diag,
                       [B_ltb[i], B_QLT[hh]], [B_PS[a]], skip_group_check=diag)
                    mm(PS[:, a, hh * 64:(hh + 1) * 64], krb[i][0:65, ks], QRT[0:65, hh, qc0:qc0 + NQ], False,
                       True, [B_krb[i], B_QRT[hh]], [B_PS[a]], skip_group_check=diag)
                for hh in (1, 3, 5, 7):
                    zmm(hh, hh == 7)

            mm(PS[:, 7, 0:256], ident_b[:], ZERO[:], True, True, [B_const], [B_PS[7]])
            def st_T(v):
                mm(PS[:, 6, :], negU[:], SPB[v % 3][:], v == 0, True, [B_SPB[v % 3], B_const], [B_PS[6]],
                   skip_group_check=(v != 0))

            def st_L(v):
                mm(PS[:, 6, :], negL[:], SPB[v % 3][:], False, True, [B_SPB[v % 3], B_const], [B_PS[6]],
                   skip_group_check=True)

            def st_EG_V(v):
                S.add("act", lambda e: e.activation(out=G32[:], in_=PS[:, 6, :], func=AF.Exp), [B_PS[6]], [B_G32])
                S.add("dve", lambda e, v=v: e.tensor_tensor(out=ATb[v % 2][:], in0=E32[v % 3][:], in1=G32[:],
                                                            op=ALU.mult), [B_E32[v % 3], B_G32], [B_AT[v % 2]])

            def st_O(v):
                n1, kb1, _, lk1 = units[v]
                i1 = slots[n1]
                for hh in (0, 2, 4, 6, 1, 3, 5, 7):
                    pbh = (hh % 2) * 64
                    mm(PS[pbh:pbh + 64, 7, (hh // 2) * 64:(hh // 2 + 1) * 64],
                       VVA[i1][:, hh, kb1 * 64:(kb1 + 1) * 64], ATb[v % 2][:, hh * 64:(hh + 1) * 64], False, True,
                       B_VVA[i1] + [B_AT[v % 2]], [B_PS[7]], skip_group_check=True)
                if lk1 and n1 + NKS < len(ktl):
                    load(n1 + NKS)

            stage_S(0)
            for u in range(NU):
                n, kb, diag, lastkb = units[u]
                i = slots[n]
                a = u % 2
                first, last = (u == 0), (u == NU - 1)
                if u >= 2:
                    st_L(u - 2)
                if u >= 1:
                    st_T(u - 1)
                S.add("act", lambda e, a=a, u=u: e.activation(out=E32[u % 3][:], in_=PS[:, 2 + a, :], func=AF.Exp),
                      [B_PS[2 + a]], [B_E32[u % 3]])
                S.add("act", lambda e, a=a: e.activation(out=Pb[a][:], in_=PS[:, a, :], func=AF.Exp), [B_PS[a]],
                      [B_Pb[a]])
                S.add("act", lambda e, u=u: e.activation(out=SPB[u % 3][:], in_=E32[u % 3][:], func=AF.Ln, bias=1.0),
                      [B_E32[u % 3]], [B_SPB[u % 3]])
                if u + 1 < NU:
                    stage_S(u + 1)
                if u >= 2:
                    st_O(u - 2)
                mm(PS[:, 4, :], lab[i][:, kb, :], Pb[a][:], first, last, [B_lab[i], B_Pb[a]], [B_PS[4]])
                mm(PS[:, 5, :], ones_b[:], Pb[a][:], first, last, [B_Pb[a], B_const], [B_PS[5]])
                if u >= 1:
                    st_EG_V(u - 1)
            if NU >= 2:
                st_L(NU - 2)
            st_T(NU - 1)
            st_EG_V(NU - 1)
            if NU >= 2:
                st_O(NU - 2)
            st_O(NU - 1)
            S.add("dve", lambda e: e.reciprocal(out=RDEN[:], in_=PS[:, 5, :]), [B_PS[5]], [B_RDEN])
            copy_op("act", OLT[:], PS[:, 4, :], [B_PS[4]], [B_OLT])
            for hh in (0, 2, 4, 6, 1, 3, 5, 7):
                pbh = (hh % 2) * 64
                mm(PS[pbh:pbh + 64, 5, (hh // 2) * 64:(hh // 2 + 1) * 64], Wuv[:, hh, :],
                   OLT[:, hh * 64:(hh + 1) * 64], True, True, [B_Wuv, B_OLT], [B_PS[5]], skip_group_check=True)
            for hh in range(8):
                pbh = (hh % 2) * 64
                S.add("dve", lambda e, hh=hh, pbh=pbh: e.tensor_tensor(
                    out=ON[pbh:pbh + 64, (hh // 2) * 64:(hh // 2 + 1) * 64],
                    in0=PS[pbh:pbh + 64, 5, (hh // 2) * 64:(hh // 2 + 1) * 64],
                    in1=RDEN[pbh:pbh + 64, hh * 64:(hh + 1) * 64], op=ALU.mult), [B_PS[5], B_RDEN], [B_ON])

            def headnorm_all(src, src_bufs, dst, dst_bufs, bank):
                S.add("act", lambda e: e.activation(out=SQO[:, 0:256], in_=src, func=AF.Square), src_bufs, [B_SQO])
                for pbh in (0, 64):
                    mm(PS[pbh:pbh + 64, bank, 0:256], ones_b[pbh:pbh + 64, 0:64], SQO[pbh:pbh + 64, 0:256], True,
                       True, [B_SQO, B_const], [B_PS[bank]], skip_group_check=True)
                S.add("act", lambda e: e.activation(out=RS[:, 0:256], in_=PS[:, bank, 0:256], func=AF.Ln, bias=EPS,
                                                    scale=1.0 / 64), [B_PS[bank]], [B_RS])
                S.add("act", lambda e: e.activation(out=RS[:, 0:256], in_=RS[:, 0:256], func=AF.Exp, scale=-0.5),
                      [B_RS], [B_RS])
                S.add("dve", lambda e: e.tensor_tensor(out=dst, in0=src.rearrange("p (h q) -> p h q", q=64),
                                                       in1=RS[:, 0:256].rearrange("p (h q) -> p h q", q=64),
                                                       op=ALU.mult), src_bufs + [B_RS], dst_bufs)

            headnorm_all(ON[:, 0:256], [B_ON], OT[:, 0:4, qc0:qc0 + NQ], B_OT[0:4], 4)
            headnorm_all(PS[:, 7, 0:256], [B_PS[7]], OT[:, 4:8, qc0:qc0 + NQ], B_OT[4:8], 6)

        def final_out(T):
            nv = T["nvalid"]
            YS = HT[:, 0:16, :].rearrange("p a b -> p (a b)").bitcast(F32).rearrange("p (s d) -> p s d", d=D)
            dma("sp", GPB[:], gpost_d[:, 3 * D:4 * D], [], [B_GPB])
            for s in range(4):
                c = stat_cols(1)
                S.add("act", lambda e, s=s, c=c: e.activation(out=TMP[:], in_=X[:, s, :], func=AF.Square,
                                                              accum_out=st[:, c:c + 1]), [B_X[s]], [B_TMP, B_st])
                rstd_from_ss(c, 1, 1.0 / D)
                S.add("dve", lambda e, s=s, c=c: e.scalar_tensor_tensor(out=YS[:, s, :], in0=X[:, s, :],
                                                                        scalar=st[:, c:c + 1], in1=GPB[:],
                                                                        op0=ALU.mult, op1=ALU.mult),
                      [B_X[s], B_st, B_GPB], B_HT[4 * s:4 * s + 4])
                orow = T["orow0"] + s * T["ostride"]
                dma("pool", T["y_out"][orow:orow + nv, :], YS[0:nv, s, :], B_HT[4 * s:4 * s + 4], [Buf("o")],
                    semb=B_HT[4 * s], final=True)

        def process_tile(T):
            dma("sp", X[:], T["x"].rearrange("(s p) d -> p s d", p=128), [], B_X)
            ffn(0, 4)
            chk("ffn1")
            proj_stage(T)
            chk("proj")
            chk("proj_%d" % T.get("idx", -1))
            if not T["own"]:
                return
            qside(T)
            chk("qside")
            for g in T["groups"]:
                if g["N"] == 64:
                    attention_stream(g)
                    continue
                for h in range(8):
                    attention_head(g, h)
                    chk("att1")
            chk("att")
            down_stage(OT, B_OT, 8, WO, B_WO, 4, 1)
            chk("wout")
            ffn(1, 4)
            final_out(T)

        for slot in range(NSLOT):
            own = (slot % 2 == 1)
            oi = slot // 2
            T = dict(idx=slot, own=own, prompt=True, x=xp[slot * 512:(slot + 1) * 512, :], tab0=slot * 512, cc=cc_p, ss=ss_p,
                     ccT=ccT_p, ssT=ssT_p, nvalid=128, orow0=oi * 512, ostride=128, kts=[slot], kmx_cols=[0] * 4,
                     lat_out=lat_p, kr_out=kr_p, k_out=k_p, v_out=v_p, y_out=y_p,
                     groups=[dict(qc0=0, N=512, ktl=[(slot, 4, True)] + [(k, 4, False) for k in
                                                                         reversed(range(slot))])])
            process_tile(T)
        if NSTREAM > 0:
            assert NSTREAM == 4
            kts = [NSLOT + s * (NPT + 1) + NPT for s in range(4)]
            groups = []
            for s in range(4):
                base = NSLOT + s * (NPT + 1)
                groups.append(dict(qc0=s * 128, N=64, ktl=[(kts[s], 1, True)] + [(base + k, 4, False) for k in
                                                                                reversed(range(NPT))]))
            T = dict(own=True, prompt=False, x=xs, tab0=0, cc=cc_s, ss=ss_s, ccT=ccT_s, ssT=ssT_s, nvalid=64, orow0=0,
                     ostride=64, kts=kts, kmx_cols=[1, 2, 3, 4], lat_out=lat_s, kr_out=kr_s, k_out=k_s, v_out=v_s,
                     y_out=y_s, groups=groups)
            process_tile(T)

        S.finalize(nc, out_sems)
    return nc, S


def _rope_tables(pos):
    half = 16
    inv_freq = (np.float32(10000.0) ** (-np.arange(half, dtype=np.float32) / np.float32(half))).astype(np.float32)
    ang = pos.astype(np.float32)[:, None] * inv_freq[None, :]
    c = np.cos(ang).astype(np.float32)
    s = np.sin(ang).astype(np.float32)
    cc = np.concatenate([c, c], axis=1)
    ss = np.concatenate([-s, s], axis=1)
    return np.ascontiguousarray(cc), np.ascontiguousarray(ss)


def _consts():
    j = np.arange(128)
    ident = np.eye(128, dtype=np.float32)
    negU = -(j[:, None] >= j[None, :]).astype(np.float32)
    negL = -(j[:, None] < j[None, :]).astype(np.float32)
    x = np.arange(896) - 384
    msb = np.where(j[:, None] >= x[None, :], NEG, 0.0).astype(np.float32)
    mmla = np.where((j[:, None] // 64) > (x[None, :] // 64), NEG, 0.0).astype(np.float32)
    return dict(ident=ident, negU=negU, negL=negL, msb=msb, mmla=mmla)


_PROG_CACHE = {}


def run_cores(inp, cfg, nseq):
    ncores = 2 * nseq
    NSLOT, NSTREAM, PAST = cfg.NSLOT, cfg.NSTREAM, cfg.PAST
    key = (NSLOT, NSTREAM, PAST, cfg.DFF)
    if key not in _PROG_CACHE:
        _PROG_CACHE[key] = build_program(cfg)
    nc, _ = _PROG_CACHE[key]
    f32 = lambda a: np.ascontiguousarray(np.asarray(a, dtype=np.float32))
    shared = dict(
        w_gate1=f32(inp["w_gate1"][0]), w_gate2=f32(inp["w_gate2"][0]), w_up1=f32(inp["w_up1"][0]),
        w_up2=f32(inp["w_up2"][0]), w_down1=f32(inp["w_down1"][0]), w_down2=f32(inp["w_down2"][0]),
        w_in=f32(inp["w_in"][0]), w_uq=f32(inp["w_uq"][0]).reshape(256, 768),
        w_uk=f32(inp["w_uk"][0]).reshape(128, 512), w_uv=f32(inp["w_uv"][0]).reshape(128, 512),
        w_out=f32(inp["w_out"][0]),
    )
    gp = np.stack([f32(inp["g_pre_ff1"][0]), f32(inp["g_pre_mix"][0]), f32(inp["g_pre_ff2"][0])])
    shared["gpre"] = np.ascontiguousarray(gp.reshape(3, 8, 128).transpose(2, 0, 1).reshape(128, 24))
    shared["gq"] = np.ascontiguousarray(f32(inp["g_q"][0]).reshape(2, 128).T)
    go = np.concatenate([f32(inp["g_mla_out"][0]).reshape(-1), f32(inp["g_sb_out"][0]).reshape(-1)])
    shared["gout"] = np.ascontiguousarray(go.reshape(8, 128).T)
    gpost = np.concatenate([f32(inp["g_post_ff1"][0]), f32(inp["g_post_mix"][0]), f32(inp["g_post_ff2"][0]),
                            f32(inp["g_final"][0])])
    shared["gpost"] = np.ascontiguousarray(np.broadcast_to(gpost[None, :], (128, 4 * D)))
    shared["gkv"] = np.ascontiguousarray(np.broadcast_to(f32(inp["g_kv"][0])[None, :], (128, 128)))
    shared.update(_consts())
    pos_s = np.zeros(NSTREAM * 128, dtype=np.float32)
    for s in range(NSTREAM):
        pos_s[s * 128:s * 128 + 64] = PAST + np.arange(64)
    cc_s, ss_s = _rope_tables(pos_s)
    shared.update(cc_s=cc_s, ss_s=ss_s, ccT_s=np.ascontiguousarray(cc_s.T), ssT_s=np.ascontiguousarray(ss_s.T))
    xprompt = f32(inp["x_prompt"])
    xsample = f32(inp["x_sample"])
    c_lat = f32(inp["cache_mla_latent"][0]); c_kr = f32(inp["cache_mla_krope"][0])
    c_k = f32(inp["cache_sb_k"][0]); c_v = f32(inp["cache_sb_v"][0])
    in_maps = []
    for core in range(ncores):
        seq, c = core // 2, core % 2
        m = dict(shared)
        if c == 0:
            m["xp"] = np.ascontiguousarray(xprompt[seq])
            pos = np.arange(NSLOT * 512, dtype=np.float32)
            km = np.zeros((1, NSLOT * 512), np.float32)
        else:
            m["xp"] = np.ascontiguousarray(np.concatenate([np.zeros((512, D), np.float32),
                                                           xprompt[seq][:(NSLOT - 1) * 512]], axis=0))
            pos = np.concatenate([np.zeros(512, np.float32), np.arange((NSLOT - 1) * 512, dtype=np.float32)])
            km = np.zeros((1, NSLOT * 512), np.float32)
            km[0, :512] = NEG
        cc, ss = _rope_tables(pos)
        m.update(cc_p=cc, ss_p=ss, ccT_p=np.ascontiguousarray(cc.T), ssT_p=np.ascontiguousarray(ss.T), kmask=km)
        st0 = core * NSTREAM
        xs = np.zeros((NSTREAM * 128, D), np.float32)
        for s in range(NSTREAM):
            xs[s * 128:s * 128 + 64] = xsample[st0 + s]
        m["xs"] = xs
        m["c_lat"] = np.ascontiguousarray(c_lat[st0:st0 + NSTREAM].reshape(NSTREAM * PAST, 128))
        m["c_kr"] = np.ascontiguousarray(c_kr[st0:st0 + NSTREAM].reshape(NSTREAM * PAST, 32))
        m["c_k"] = np.ascontiguousarray(c_k[st0:st0 + NSTREAM].reshape(NSTREAM * PAST, 512))
        m["c_v"] = np.ascontiguousarray(c_v[st0:st0 + NSTREAM].reshape(NSTREAM * PAST, 512))
        in_maps.append(m)
    res = run_bass_kernel_spmd(nc, in_maps, core_ids=list(range(ncores)))
    R = res.results
    SEQ = NSLOT * 512
    NB = ncores * NSTREAM
    y_p = np.zeros((nseq, SEQ, D), np.float32)
    lat_p = np.zeros((1, nseq, SEQ, 128), np.float32)
    kr_p = np.zeros((1, nseq, SEQ, 32), np.float32)
    k_p = np.zeros((1, nseq, SEQ, 8, 64), np.float32)
    v_p = np.zeros((1, nseq, SEQ, 8, 64), np.float32)
    y_s = np.zeros((NB, 64, D), np.float32)
    lat_s = np.zeros((1, NB, 64, 128), np.float32)
    kr_s = np.zeros((1, NB, 64, 32), np.float32)
    k_s = np.zeros((1, NB, 64, 8, 64), np.float32)
    v_s = np.zeros((1, NB, 64, 8, 64), np.float32)
    for core in range(ncores):
        seq, c = core // 2, core % 2
        r = R[core]
        for oi in range(cfg.NOWN):
            tile = 2 * oi + 1 if c == 0 else 2 * oi
            a, b = tile * 512, (tile + 1) * 512
            o0, o1 = oi * 512, (oi + 1) * 512
            y_p[seq, a:b] = r["y_p"][o0:o1]
            lat_p[0, seq, a:b] = r["lat_p"][o0:o1]
            kr_p[0, seq, a:b] = r["kr_p"][o0:o1]
            k_p[0, seq, a:b] = r["k_p"][o0:o1].reshape(512, 8, 64)
            v_p[0, seq, a:b] = r["v_p"][o0:o1].reshape(512, 8, 64)
        for s in range(NSTREAM):
            g = core * NSTREAM + s
            y_s[g] = r["y_s"][s * 64:(s + 1) * 64]
            lat_s[0, g] = r["lat_s"][s * 64:(s + 1) * 64]
            kr_s[0, g] = r["kr_s"][s * 64:(s + 1) * 64]
            k_s[0, g] = r["k_s"][s * 64:(s + 1) * 64].reshape(64, 8, 64)
            v_s[0, g] = r["v_s"][s * 64:(s + 1) * 64].reshape(64, 8, 64)
    return (y_p, y_s, lat_p, kr_p, k_p, v_p, lat_s, kr_s, k_s, v_s)


def kernel(**inputs):
    cfg = Cfg(nslot=16, nstream=4, past=4096, dff=2816)
    return run_cores(inputs, cfg, nseq=4)
```

```python
import numpy as np
import ml_dtypes
import concourse.bass as bass
import concourse.mybir as mybir
from concourse.bass_utils import run_bass_kernel_spmd

F32 = mybir.dt.float32
BF16 = mybir.dt.bfloat16
ALU = mybir.AluOpType
AF = mybir.ActivationFunctionType
AX = mybir.AxisListType

SAME_ENGINE_SYNC = True


class Buf:
    __slots__ = ("name", "w", "r", "dsem")

    def __init__(self, name):
        self.name = name
        self.w = None
        self.r = {}
        self.dsem = None


class Op:
    __slots__ = ("eng", "fn", "deps", "dsem", "need", "tok", "idx", "grp")

    def __init__(self, eng, fn, dsem):
        self.eng = eng
        self.fn = fn
        self.deps = []
        self.dsem = dsem
        self.need = dsem is not None
        self.tok = None
        self.grp = None


class DSem:
    def __init__(self, name):
        self.name = name
        self.h = None
        self.n = 0


class Sched:
    ENGS = ("pe", "act", "dve", "pool", "sp")

    def __init__(self):
        self.ops = {e: [] for e in self.ENGS}
        self.dsems = []
        self.dmas = []
        self.nops = 0

    def dsem(self, name):
        d = DSem(name)
        self.dsems.append(d)
        return d

    stopped = False

    def add(self, eng, fn, reads=(), writes=(), dsem=None, group=None):
        if self.stopped:
            return None
        op = Op(eng, fn, dsem)
        op.grp = group
        self.nops += 1
        op.idx = self.nops
        best = {}

        def consider(d):
            if d is None or d is op:
                return
            k = id(d.dsem) if d.dsem is not None else d.eng
            o = best.get(k)
            if o is None or d.idx > o.idx:
                best[k] = d

        for b in reads:
            consider(b.w)
        for b in writes:
            consider(b.w)
            for d in b.r.values():
                consider(d)
        op.deps = list(best.values())
        if fn is not None:
            k = id(dsem) if dsem is not None else eng
            for b in reads:
                b.r[k] = op
        for b in writes:
            b.w = op
            b.r = {}
        self.ops[eng].append(op)
        if dsem is not None:
            self.dmas.append(op)
        return op

    def finalize(self, nc, final_waits):
        for e in self.ENGS:
            for op in self.ops[e]:
                for d in op.deps:
                    if d.dsem is not None:
                        continue
                    if d.eng == op.eng and (d.eng == "pe" or d.eng == "sp" or not SAME_ENGINE_SYNC):
                        continue
                    d.need = True
        stack = []
        esem = {}
        import contextlib
        with contextlib.ExitStack() as es:
            for e in self.ENGS:
                esem[e] = es.enter_context(nc.semaphore("s_" + e))
            for d in self.dsems:
                d.h = es.enter_context(nc.semaphore("d_" + d.name))
                d.n = 0
            for e in self.ENGS:
                n = 0
                for op in self.ops[e]:
                    if op.dsem is not None:
                        continue
                    if op.need:
                        n += 1
                        op.tok = (esem[e], n, e)
            gmax = {}
            for op in self.dmas:
                op.dsem.n += 16
                op.tok = (op.dsem.h, op.dsem.n, "dma:" + op.dsem.name)
                if op.grp is not None:
                    gmax[op.grp] = op.tok
            for op in self.dmas:
                if op.grp is not None:
                    assert gmax[op.grp][0] is op.tok[0]
                    op.tok = gmax[op.grp]
            block = es.enter_context(nc.Block())

            def make(e):
                def body(eng):
                    known = {}
                    ownn = 0
                    for op in self.ops[e]:
                        for d in op.deps:
                            if d.dsem is None and d.eng == e:
                                if e in ("pe", "sp") or not SAME_ENGINE_SYNC:
                                    continue
                            sem, val, who = d.tok
                            k = id(sem)
                            if known.get(k, 0) >= val:
                                continue
                            known[k] = val
                            eng.wait_ge(sem, val)
                        if op.fn is None:
                            continue
                        ins = op.fn(eng)
                        if op.dsem is not None:
                            ins.then_inc(op.dsem.h, 16)
                        elif op.need:
                            ins.then_inc(esem[e], 1)
                    if e == "sp":
                        for d in final_waits:
                            eng.wait_ge(d.h, d.n)
                return body

            block.tensor(make("pe"))
            block.scalar(make("act"))
            block.vector(make("dve"))
            block.gpsimd(make("pool"))
            block.sync(make("sp"))


D = 1024
MC = 8
HEADS = 8
MLA_SCALE = 96.0 ** -0.5
SB_SCALE = 0.125
EPS = 1e-6
NEG = -30000.0


class StopBuild(Exception):
    pass


class Cfg:
    stop = None
    vv_eng = "dve"

    def __init__(self, nslot=16, nstream=4, past=4096, dff=2816):
        self.NSLOT = nslot
        self.NSTREAM = nstream
        self.PAST = past
        self.DFF = dff
        self.FC = dff // 128
        self.NPT = past // 512
        self.NOWN = nslot // 2
        self.NKT = nslot + nstream * (self.NPT + 1)


def build_program(cfg):
    NSLOT, NSTREAM, PAST, DFF, FC, NPT, NOWN, NKT = (cfg.NSLOT, cfg.NSTREAM, cfg.PAST, cfg.DFF, cfg.FC,
                                                     cfg.NPT, cfg.NOWN, cfg.NKT)
    import contextlib
    nc = bass.Bass("TRN2", target_bir_lowering=False)
    S = Sched()

    def din(name, shape):
        return nc.dram_tensor(name, list(shape), F32, kind="ExternalInput").ap()

    def dout(name, shape):
        return nc.dram_tensor(name, list(shape), F32, kind="ExternalOutput").ap()

    def dscr(name, shape):
        return nc.dram_tensor(name, list(shape), BF16).ap()

    NP_TOK = NSLOT * 512
    NS_TOK = NSTREAM * 64
    NS_PAD = NSTREAM * 128
    xp = din("xp", [NP_TOK, D])
    xs = din("xs", [NS_PAD, D])
    c_lat = din("c_lat", [NSTREAM * PAST, 128])
    c_kr = din("c_kr", [NSTREAM * PAST, 32])
    c_k = din("c_k", [NSTREAM * PAST, 512])
    c_v = din("c_v", [NSTREAM * PAST, 512])
    w_gate = [din("w_gate1", [D, DFF]), din("w_gate2", [D, DFF])]
    w_up = [din("w_up1", [D, DFF]), din("w_up2", [D, DFF])]
    w_down = [din("w_down1", [DFF, D]), din("w_down2", [DFF, D])]
    w_in = din("w_in", [D, 1952])
    w_uq = din("w_uq", [256, 768])
    w_uk = din("w_uk", [128, 512])
    w_uv = din("w_uv", [128, 512])
    w_out = din("w_out", [D, D])
    gpre_d = din("gpre", [128, 3 * 8])
    gq_d = din("gq", [128, 2])
    gout_d = din("gout", [128, 8])
    gpost_d = din("gpost", [128, 4 * D])
    gkv_d = din("gkv", [128, 128])
    cc_p = din("cc_p", [NP_TOK, 32])
    ss_p = din("ss_p", [NP_TOK, 32])
    cc_s = din("cc_s", [NS_PAD, 32])
    ss_s = din("ss_s", [NS_PAD, 32])
    ccT_p = din("ccT_p", [32, NP_TOK])
    ssT_p = din("ssT_p", [32, NP_TOK])
    ccT_s = din("ccT_s", [32, NS_PAD])
    ssT_s = din("ssT_s", [32, NS_PAD])
    kmask_d = din("kmask", [1, NP_TOK])
    ident_d = din("ident", [128, 128])
    negU_d = din("negU", [128, 128])
    negL_d = din("negL", [128, 128])
    msb_d = din("msb", [128, 896])
    mmla_d = din("mmla", [128, 896])
    y_p = dout("y_p", [NOWN * 512, D])
    lat_p = dout("lat_p", [NOWN * 512, 128])
    kr_p = dout("kr_p", [NOWN * 512, 32])
    k_p = dout("k_p", [NOWN * 512, 512])
    v_p = dout("v_p", [NOWN * 512, 512])
    y_s = dout("y_s", [NS_TOK, D])
    lat_s = dout("lat_s", [NS_TOK, 128])
    kr_s = dout("kr_s", [NS_TOK, 32])
    k_s = dout("k_s", [NS_TOK, 512])
    v_s = dout("v_s", [NS_TOK, 512])
    Wgu = [dscr("Wgu1", [FC, 128, 2048]), dscr("Wgu2", [FC, 128, 2048])]
    Wd = [dscr("Wd1", [FC, 128, 1024]), dscr("Wd2", [FC, 128, 1024])]
    WO = dscr("WO", [8, 128, 1024])
    KT = dscr("KT", [NKT, 4, 128, 512])
    VV = dscr("VV", [NKT, 8, 128, 256])
    LT = dscr("LT", [NKT, 128, 512])
    LA = dscr("LA", [NKT, 128, 512])
    KR = dscr("KR", [NKT, 33, 512])
    B_Wgu = [[Buf("Wgu%d_%d" % (l, f)) for f in range(FC)] for l in range(2)]
    B_Wd = [[Buf("Wd%d_%d" % (l, f)) for f in range(FC)] for l in range(2)]
    B_WO = [Buf("WO%d" % k) for k in range(8)]
    B_KT = [Buf("KT%d" % k) for k in range(NKT)]
    B_VV = [Buf("VV%d" % k) for k in range(NKT)]
    B_LT = [Buf("LT%d" % k) for k in range(NKT)]
    B_LA = [Buf("LA%d" % k) for k in range(NKT)]
    B_KR = [Buf("KR%d" % k) for k in range(NKT)]

    es = contextlib.ExitStack()
    with es:
        def sb(name, shape, dt):
            return es.enter_context(nc.sbuf_tensor("s_" + name, list(shape), dt))

        PS = es.enter_context(nc.psum_tensor("PS", [128, 8, 512], F32))
        B_PS = [Buf("ps%d" % i) for i in range(8)]

        Win = sb("Win", [128, 8, 1952], BF16); B_Win = Buf("Win")
        Wuq = sb("Wuq", [128, 2, 768], BF16); B_Wuq = Buf("Wuq")
        Wuqs = sb("Wuqs", [128, 2, 8, 32], BF16); B_Wuqs = Buf("Wuqs")
        WukT = sb("WukT", [128, 8, 128], BF16); B_WukT = Buf("WukT")
        Wuv = sb("Wuv", [128, 8, 64], BF16); B_Wuv = Buf("Wuv")
        gpre = sb("gpre_t", [128, 3, 8], F32); B_gpre = Buf("gpre")
        gq = sb("gq_t", [128, 2], F32); B_gq = Buf("gq")
        gout = sb("gout_t", [128, 8], F32); B_gout = Buf("gout")
        gkv = sb("gkv_t", [128, 128], F32); B_gkv = Buf("gkv")
        ident_f = sb("ident_f", [128, 128], F32)
        ident_b = sb("ident_b", [128, 128], BF16)
        negU = sb("negU", [128, 128], BF16)
        negL = sb("negL", [128, 128], BF16)
        ones_b = sb("ones_b", [128, 128], BF16)
        ones_f = sb("ones_f", [128, 128], F32)
        msb = sb("msb", [128, 896], BF16)
        mmla = sb("mmla", [128, 896], BF16)
        MSB8 = sb("MSB8", [128, 8, 64], BF16)
        MMLA8 = sb("MMLA8", [128, 8, 64], BF16)
        B_const = Buf("const")
        B_identf = Buf("identf")
        X = sb("X", [128, 4, D], F32); B_X = [Buf("X%d" % i) for i in range(4)]
        HT = sb("HT", [128, FC, 512], BF16) if FC >= 19 else sb("HT", [128, 19, 512], BF16)
        NHB = max(FC, 19)
        B_HT = [Buf("HT%d" % i) for i in range(NHB)]
        XN = HT[:, 0:8, :].rearrange("p a b -> p (a b)").rearrange("p (s d) -> p s d", d=D)
        UT = sb("UT", [128, 8, 512], BF16); B_UT = Buf("UT")
        TMP = sb("TMP", [128, D], BF16); B_TMP = Buf("TMP")
        NWG = 3
        WGB = [sb("WGB%d" % i, [128, 8, 2, 128], BF16) for i in range(NWG)]; B_WGB = [Buf("WGB%d" % i) for i in range(NWG)]
        NWD = 3
        WDB = [sb("WDB%d" % i, [128, 1024], BF16) for i in range(NWD)]; B_WDB = [Buf("WDB%d" % i) for i in range(NWD)]
        st = sb("stats", [128, 64], F32); B_st = Buf("stats")
        LAT32s = [sb("LAT32_%d" % i, [128, 128], F32) for i in range(2)]; B_LAT32s = [Buf("LAT32a"), Buf("LAT32b")]
        KR32s = [sb("KR32_%d" % i, [128, 32], F32) for i in range(2)]; B_KR32s = [Buf("KR32a"), Buf("KR32b")]
        KRBs = [sb("KRB_%d" % i, [128, 32], BF16) for i in range(2)]; B_KRBs = [Buf("KRBa"), Buf("KRBb")]
        RTs = [sb("RT_%d" % i, [128, 64], F32) for i in range(2)]; B_RTs = [Buf("RTa"), Buf("RTb")]
        KTs = sb("KTs", [128, 4, 512], BF16); B_KTs = Buf("KTs")
        VVs = sb("VVs", [128, 8, 4, 64], BF16); B_VVs = Buf("VVs")
        LTs = sb("LTs", [128, 512], BF16); B_LTs = Buf("LTs")
        LAs = sb("LAs", [128, 4, 128], BF16); B_LAs = Buf("LAs")
        KRs = sb("KRs", [64, 512], BF16); B_KRs = Buf("KRs")
        CQN = sb("CQN", [128, 4, 256], BF16); B_CQN = Buf("CQN")
        CQT = sb("CQT", [128, 2, 512], BF16); B_CQT = Buf("CQT")
        CS = sb("CS", [128, 4, 32], F32); SS_ = sb("SSt", [128, 4, 32], F32); B_CS = Buf("CS")
        CCT = sb("CCT", [32, 512], F32); SST = sb("SST", [32, 512], F32); B_CCT = Buf("CCT")
        QTP = sb("QTP", [128, 8, 512], BF16); B_QT = Buf("QTP")
        QLT = sb("QLT", [128, 8, 512], BF16); B_QLT = [Buf("QLT%d" % h) for h in range(8)]
        QRT = sb("QRT", [128, 8, 512], BF16); B_QRT = [Buf("QRT%d" % h) for h in range(8)]
        KMX = sb("KMX", [128, 1 + NSTREAM], F32); B_KMX = Buf("KMX")
        KMB = sb("KMB", [128, 4], F32); B_KMB = Buf("KMB")
        NKB = 3
        ktb = [sb("ktb%d" % i, [128, 512], BF16) for i in range(NKB)]
        vvb = [sb("vvb%d" % i, [128, 4, 128], BF16) for i in range(NKB)]
        ltb = [sb("ltb%d" % i, [128, 512], BF16) for i in range(NKB)]
        lab = [sb("lab%d" % i, [128, 4, 128], BF16) for i in range(NKB)]
        krb = [sb("krb%d" % i, [128, 512], BF16) for i in range(NKB)]
        B_ktb = [Buf("ktb%d" % i) for i in range(NKB)]; B_vvb = [Buf("vvb%d" % i) for i in range(NKB)]; B_vvb1 = [Buf("vvc%d" % i) for i in range(NKB)]
        B_mla = [Buf("mlab%d" % i) for i in range(NKB)]
        B_ltb = [Buf("ltb%d" % i) for i in range(NKB)]; B_lab = [Buf("lab%d" % i) for i in range(NKB)]
        B_krb = [Buf("krb%d" % i) for i in range(NKB)]
        Pb = [sb("Pb%d" % i, [128, 512], BF16) for i in range(2)]; B_Pb = [Buf("Pb0"), Buf("Pb1")]
        E32 = [sb("E32_%d" % i, [128, 512], F32) for i in range(3)]; B_E32 = [Buf("E0"), Buf("E1"), Buf("E2")]
        SPB = [sb("SPB%d" % i, [128, 512], BF16) for i in range(3)]; B_SPB = [Buf("SP0"), Buf("SP1"), Buf("SP2")]
        G32 = sb("G32", [128, 512], F32); B_G32 = Buf("G32")
        ATb = [sb("AT%d" % i, [128, 512], BF16) for i in range(2)]; B_AT = [Buf("AT0"), Buf("AT1")]
        OLT = sb("OLT", [128, 512], BF16); B_OLT = Buf("OLT")
        RDEN = sb("RDEN", [128, 512], F32); B_RDEN = Buf("RDEN")
        ON = sb("ON", [128, 512], F32); B_ON = Buf("ON")
        SQO = sb("SQO", [128, 512], BF16); B_SQO = Buf("SQO")
        RS = sb("RS", [128, 512], F32); B_RS = Buf("RS")
        OT = sb("OT", [128, 8, 512], BF16); B_OT = [Buf("OT%d" % i) for i in range(8)]
        GPB = sb("GPB", [128, D], F32); B_GPB = Buf("GPB")
        SG = E32; B_SG = B_E32
        RA = E32[0][0:32, :]; RB = E32[1][0:32, :]; B_RA = B_E32[0]; B_RB = B_E32[1]
        QN = SPB[0][0:64, :]; B_QN = B_SPB[0]
        SQ1 = Pb[0]; SQ2 = Pb[1]; B_SQ1 = B_Pb[0]; B_SQ2 = B_Pb[1]
        MT = RS; B_MT = B_RS
        KM32 = RS; B_KM32 = B_RS
        K32s = [G32, RDEN]; B_K32s = [B_G32, B_RDEN]
        V32s = [ON, E32[2]]; B_V32s = [B_ON, B_E32[2]]


        rr = {"n": 0}

        def ew_engine():
            rr["n"] += 1
            return ("dve", "pool", "act")[rr["n"] % 3]

        def ev_engine():
            rr["n"] += 1
            return ("dve", "act")[rr["n"] % 2]

        def copy_op(eng, out, in_, reads, writes):
            if eng == "act":
                S.add("act", lambda e: e.activation(out=out, in_=in_, func=AF.Copy), reads, writes)
            else:
                S.add(eng, lambda e: e.tensor_copy(out=out, in_=in_), reads, writes)

        def scale_op(eng, out, in_, sc, reads, writes):
            if eng == "act":
                S.add("act", lambda e: e.activation(out=out, in_=in_, func=AF.Copy, scale=sc), reads, writes)
            else:
                S.add(eng, lambda e: e.tensor_scalar(out=out, in0=in_, scalar1=sc, scalar2=None, op0=ALU.mult),
                      reads, writes)

        out_sems = []

        def dma(q, out, in_, reads, writes, semb=None, group=None, final=False):
            b = semb if semb is not None else writes[0]
            if b.dsem is None:
                b.dsem = {}
            if q not in b.dsem:
                b.dsem[q] = S.dsem(b.name + "_" + q)
            ds = b.dsem[q]
            if final and ds not in out_sems:
                out_sems.append(ds)
            S.add(q, lambda e: e.dma_start(out=out, in_=in_), reads, writes, dsem=ds, group=group)

        def barrier(q, bufs):
            S.add(q, None, bufs, [])

        def mm(out, lhsT, rhs, start, stop, reads, writes, **kw):
            S.add("pe", lambda e: e.matmul(out, lhsT=lhsT, rhs=rhs, start=start, stop=stop, **kw), reads, writes)

        def tr(out, in_, ident, reads, writes):
            S.add("pe", lambda e: e.transpose(out=out, in_=in_, identity=ident), reads, writes)

        def chk(name):
            if cfg.stop == name:
                S.stopped = True

        Xf = X[:].rearrange("p a b -> p (a b)")

        def load_const(dst_b, src_d, ncol, q):
            dma("sp", Xf[:, q * 1024:q * 1024 + ncol], src_d, [], [B_X[q]])
            copy_op("dve", dst_b, Xf[:, q * 1024:q * 1024 + ncol], [B_X[q]], [B_const])

        dma("sp", ident_f[:], ident_d, [], [B_identf])
        load_const(ident_b[:], ident_d, 128, 0)
        load_const(negU[:], negU_d, 128, 1)
        load_const(negL[:], negL_d, 128, 2)
        load_const(msb[:], msb_d, 896, 3)
        load_const(mmla[:], mmla_d, 896, 0)
        for hh in range(8):
            copy_op("dve", MSB8[:, hh, :], msb[:, 384:448], [B_const], [B_const])
            copy_op("dve", MMLA8[:, hh, :], mmla[:, 384:448], [B_const], [B_const])
        S.add("pool", lambda e: e.memset(ones_b[:], 1.0), [], [B_const])
        S.add("pool", lambda e: e.memset(ones_f[:], 1.0), [], [B_const])
        S.add("pool", lambda e: e.memset(KMX[:], 0.0), [], [B_KMX])
        for i in range(NKB):
            S.add("pool", lambda e, i=i: e.memset(krb[i][:], 0.0), [], [B_krb[i]])
            S.add("pool", lambda e, i=i: e.memset(krb[i][64:65, :], 1.0), [], [B_krb[i]])
        S.add("pool", lambda e: e.memset(QTP[:], 0.0), [], [B_QT])
        S.add("pool", lambda e: e.memset(QRT[:], 0.0), [], B_QRT)
        S.add("pool", lambda e: e.memset(QRT[32:33, :, :], 1.0), [], B_QRT)
        S.add("pool", lambda e: e.memset(KRs[:], 0.0), [], [B_KRs])
        S.add("pool", lambda e: e.memset(OT[:], 0.0), [], B_OT)
        dma("sp", gpre[:].rearrange("p a b -> p (a b)"), gpre_d, [], [B_gpre])
        dma("sp", gq[:], gq_d, [], [B_gq])
        dma("sp", gout[:], gout_d, [], [B_gout])
        dma("sp", gkv[:], gkv_d, [], [B_gkv])

        stg_n = {"n": 0}

        def stage(src_ap, ncol):
            h = stg_n["n"] % 2
            stg_n["n"] += 1
            v = Xf[:, h * 2048:h * 2048 + ncol]
            bufs = [B_X[2 * h], B_X[2 * h + 1]]
            return h, v, bufs

        for mc in range(8):
            h, v, bufs = stage(None, 1952)
            dma("sp", v, w_in[mc * 128:(mc + 1) * 128, :], [], bufs)
            scale_op(ew_engine(), Win[:, mc, :], v, gpre[:, 1, mc:mc + 1], bufs + [B_gpre], [B_Win])
        for cc in range(2):
            h, v, bufs = stage(None, 768)
            dma("sp", v, w_uq[cc * 128:(cc + 1) * 128, :], [], bufs)
            scale_op("dve", Wuq[:, cc, :], v, gq[:, cc:cc + 1], bufs + [B_gq], [B_Wuq])
            wv = Wuq[:, cc, :].rearrange("p (h e) -> p h e", e=96)
            copy_op("dve", Wuqs[:, cc, :, 0:16], wv[:, :, 80:96], [B_Wuq], [B_Wuqs])
            copy_op("dve", Wuqs[:, cc, :, 16:32], wv[:, :, 64:80], [B_Wuq], [B_Wuqs])
        h, v, bufs = stage(None, 512)
        dma("sp", v, w_uv, [], bufs)
        copy_op("dve", Wuv[:].rearrange("p h d -> p (h d)"), v, bufs, [B_Wuv])
        h, v, bufs = stage(None, 512)
        dma("sp", v, w_uk, [], bufs)
        for hh in range(8):
            bk = 4 + (hh % 4)
            tr(PS[0:64, bk, 0:128], v[:, hh * 64:(hh + 1) * 64], ident_f[:], bufs + [B_identf], [B_PS[bk]])
            copy_op(ev_engine(), WukT[0:64, hh, :], PS[0:64, bk, 0:128], [B_PS[bk]], [B_WukT])
        for l in range(2):
            gsel = 0 if l == 0 else 2
            for fc in range(FC):
                wslot = WGB[fc % NWG]; bws = B_WGB[fc % NWG]
                for gi, wsrc in enumerate((w_gate[l], w_up[l])):
                    h, v, bufs = stage(None, 1024)
                    src = wsrc.rearrange("(mc p) f -> p mc f", p=128)[:, :, fc * 128:(fc + 1) * 128]
                    v3 = v.rearrange("p (a b) -> p a b", b=128)
                    dma("sp", v3, src, [], bufs)
                    gb = gpre[:, gsel, :].unsqueeze(2).to_broadcast([128, 8, 128])
                    eng = ("dve", "pool")[(fc + gi) % 2]
                    S.add(eng, lambda e, o=wslot[:, :, gi, :], i0=v3, g=gb: e.tensor_tensor(out=o, in0=i0, in1=g,
                                                                                          op=ALU.mult),
                          bufs + [B_gpre], [bws])
                dma("pool", Wgu[l][fc], wslot[:].rearrange("p a b c -> p (a b c)"), [bws], [B_Wgu[l][fc]], semb=bws)
                dslot = WDB[fc % NWD]; bds = B_WDB[fc % NWD]
                h, v, bufs = stage(None, 1024)
                dma("sp", v, w_down[l][fc * 128:(fc + 1) * 128, :], [], bufs)
                copy_op("act", dslot[:], v, bufs, [bds])
                dma("pool", Wd[l][fc], dslot[:], [bds], [B_Wd[l][fc]], semb=bds)
        for kc in range(8):
            dslot = WDB[kc % NWD]; bds = B_WDB[kc % NWD]
            h, v, bufs = stage(None, 1024)
            dma("sp", v, w_out[kc * 128:(kc + 1) * 128, :], [], bufs)
            scale_op("act", dslot[:], v, gout[:, kc:kc + 1], bufs + [B_gout], [bds])
            dma("pool", WO[kc], dslot[:], [bds], [B_WO[kc]], semb=bds)

        chk("prep")
        st_n = {"n": 0}

        def stat_cols(n):
            c = st_n["n"]
            if c + n > 64:
                c = 0
            st_n["n"] = c + n
            return c

        def rstd_from_ss(c, n, inv_dim):
            S.add("act", lambda e: e.activation(out=st[:, c:c + n], in_=st[:, c:c + n], func=AF.Ln, bias=EPS,
                                                scale=inv_dim), [B_st], [B_st])
            S.add("act", lambda e: e.activation(out=st[:, c:c + n], in_=st[:, c:c + n], func=AF.Exp, scale=-0.5),
                  [B_st], [B_st])

        def prenorm_to_UT(NS):
            for s in range(NS):
                c = stat_cols(1)
                S.add("act", lambda e, s=s, c=c: e.activation(out=TMP[:], in_=X[:, s, :], func=AF.Square,
                                                              accum_out=st[:, c:c + 1]), [B_X[s]], [B_TMP, B_st])
                rstd_from_ss(c, 1, 1.0 / D)
                scale_op(("dve", "act")[s % 2], XN[:, s, :], X[:, s, :], st[:, c:c + 1],
                         [B_X[s], B_st], [B_HT[2 * s], B_HT[2 * s + 1]])
                bk = 2 * s
                pv = PS[:, bk, :].bitcast(BF16)
                for mc in range(8):
                    tr(pv[:, mc * 128:(mc + 1) * 128], XN[:, s, mc * 128:(mc + 1) * 128], ident_b[:],
                       [B_HT[2 * s], B_HT[2 * s + 1], B_const], [B_PS[bk]])
                copy_op(("act", "dve")[s % 2], UT[:, :, s * 128:(s + 1) * 128],
                        pv.rearrange("p (m t) -> p m t", t=128), [B_PS[bk]], [B_UT])

        wg_n = {"n": 0}
        wd_n = {"n": 0}

        def down_stage(LHS, B_LHS, KC, Wscr, B_Wscr, NS, gsel):
            coef = 1.0 if gsel == 1 else 0.5
            dma("sp", GPB[:], gpost_d[:, gsel * D:(gsel + 1) * D], [], [B_GPB])
            subs = list(range(NS))
            for kc in range(KC):
                i = wd_n["n"] % NWD
                wd_n["n"] += 1
                dma("sp", WDB[i][:], Wscr[kc], [B_Wscr[kc]], [B_WDB[i]])
                for si, s in enumerate(subs):
                    for nh in range(2):
                        bk = 2 * si + nh
                        mm(PS[:, bk, :], LHS[:, kc, s * 128:(s + 1) * 128], WDB[i][:, nh * 512:(nh + 1) * 512],
                           kc == 0, kc == KC - 1, [B_LHS[kc], B_WDB[i]], [B_PS[bk]])
            for si, s in enumerate(subs):
                c = stat_cols(1)
                pf = PS[:, 2 * si:2 * si + 2, :].rearrange("p a b -> p (a b)")
                pbufs = [B_PS[2 * si], B_PS[2 * si + 1]]
                S.add("act", lambda e, pf=pf, c=c: e.activation(out=TMP[:], in_=pf, func=AF.Square,
                                                                accum_out=st[:, c:c + 1]), pbufs, [B_TMP, B_st])
                rstd_from_ss(c, 1, 1.0 / D)
                S.add("dve", lambda e, pf=pf, c=c: e.scalar_tensor_tensor(
                    out=pf, in0=pf, scalar=st[:, c:c + 1], in1=GPB[:], op0=ALU.mult, op1=ALU.mult),
                    pbufs + [B_st, B_GPB], pbufs)
                S.add("dve", lambda e, s=s, pf=pf: e.scalar_tensor_tensor(out=X[:, s, :], in0=pf, scalar=coef,
                                                                          in1=X[:, s, :], op0=ALU.mult,
                                                                          op1=ALU.add), pbufs + [B_X[s]], [B_X[s]])

        def ffn(l, NS):
            N = NS * 128
            prenorm_to_UT(NS)
            for fc in range(FC):
                i = wg_n["n"] % NWG
                wg_n["n"] += 1
                dma("sp", WGB[i][:].rearrange("p a b c -> p (a b c)"), Wgu[l][fc], [B_Wgu[l][fc]], [B_WGB[i]])
                bg = 4 + 2 * (fc % 2)
                bu = bg + 1
                for mc in range(8):
                    mm(PS[:, bg, 0:N], WGB[i][:, mc, 0, :], UT[:, mc, 0:N], mc == 0, mc == 7, [B_WGB[i], B_UT],
                       [B_PS[bg]])
                for mc in range(8):
                    mm(PS[:, bu, 0:N], WGB[i][:, mc, 1, :], UT[:, mc, 0:N], mc == 0, mc == 7, [B_WGB[i], B_UT],
                       [B_PS[bu]])
                sg = SG[fc % 2]; bsg = B_SG[fc % 2]
                S.add("act", lambda e, sg=sg, bg=bg: e.activation(out=sg[:, 0:N], in_=PS[:, bg, 0:N], func=AF.Silu),
                      [B_PS[bg]], [bsg])
                S.add("dve", lambda e, sg=sg, bu=bu, fc=fc: e.tensor_tensor(out=HT[:, fc, 0:N], in0=PS[:, bu, 0:N],
                                                                           in1=sg[:, 0:N], op=ALU.mult),
                      [B_PS[bu], bsg], [B_HT[fc]])
            down_stage(HT, B_HT, FC, Wd[l], B_Wd[l], NS, 0 if l == 0 else 2)

        HTf = HT[:, 0:19, :].rearrange("p a b -> p (a b)").bitcast(F32)
        CK32 = HTf[:, 0:2048].rearrange("p (b f) -> p b f", f=512)
        CV32 = HTf[:, 2048:4096].rearrange("p (b f) -> p b f", f=512)
        CL32 = HTf[:, 4096:4608].rearrange("p (b f) -> p b f", f=128)
        CR32 = HTf[:, 4608:4736].rearrange("p (b f) -> p b f", f=32)
        B_CK = B_HT[0:8]; B_CV = B_HT[8:16]; B_CL = B_HT[16:18]; B_CR = [B_HT[18]]

        def kv_scratch_write(kts, nblk_each):
            if nblk_each == 4:
                kt = kts[0][0]
                dma("pool", KT[kt].rearrange("h p j -> p h j"), KTs[:], [B_KTs], [B_KT[kt]], semb=B_KTs)
                dma("pool", VV[kt].rearrange("h p x -> p h x"), VVs[:].rearrange("p h b d -> p h (b d)"), [B_VVs],
                    [B_VV[kt]], semb=B_VVs)
                dma("pool", LT[kt], LTs[:], [B_LTs], [B_LT[kt]], semb=B_LTs)
                dma("pool", LA[kt], LAs[:].rearrange("p b c -> p (b c)"), [B_LAs], [B_LA[kt]], semb=B_LAs)
                dma("pool", KR[kt], KRs[0:33, :], [B_KRs], [B_KR[kt]], semb=B_KRs)
            else:
                gid = ("kvw", kts[0][0])
                for kt, s in kts:
                    dma("pool", KT[kt].rearrange("h p j -> p h j")[:, :, 0:128], KTs[:, :, s * 128:(s + 1) * 128],
                        [B_KTs], [B_KT[kt]], semb=B_KTs, group=gid + (0,))
                    dma("pool", VV[kt].rearrange("h p x -> p h x")[:, :, 0:64], VVs[:, :, s, :], [B_VVs], [B_VV[kt]],
                        semb=B_VVs, group=gid + (1,))
                    dma("pool", LT[kt][:, 0:128], LTs[:, s * 128:(s + 1) * 128], [B_LTs], [B_LT[kt]], semb=B_LTs,
                        group=gid + (2,))
                    dma("pool", LA[kt][:, 0:128], LAs[:, s, :], [B_LAs], [B_LA[kt]], semb=B_LAs, group=gid + (3,))
                    dma("pool", KR[kt][:, 0:128], KRs[0:33, s * 128:(s + 1) * 128], [B_KRs], [B_KR[kt]], semb=B_KRs,
                        group=gid + (4,))

        def kmax_update(col, a_ap, b_ap, reads):
            c = stat_cols(2)
            S.add("act", lambda e: e.activation(out=TMP[:, 0:128], in_=a_ap, func=AF.Square,
                                                accum_out=st[:, c:c + 1]), reads, [B_TMP, B_st])
            S.add("act", lambda e: e.activation(out=TMP[:, 0:32], in_=b_ap, func=AF.Square,
                                                accum_out=st[:, c + 1:c + 2]), reads, [B_TMP, B_st])
            S.add("dve", lambda e: e.tensor_tensor(out=st[:, c:c + 1], in0=st[:, c:c + 1], in1=st[:, c + 1:c + 2],
                                                   op=ALU.add), [B_st], [B_st])
            S.add("dve", lambda e: e.tensor_tensor(out=KMX[:, col:col + 1], in0=KMX[:, col:col + 1],
                                                   in1=st[:, c:c + 1], op=ALU.max), [B_st, B_KMX], [B_KMX])

        for sI in range(NSTREAM):
            for pt in range(NPT):
                kt = NSLOT + sI * (NPT + 1) + pt
                r0 = sI * PAST + pt * 512
                dma("sp", CK32, c_k[r0:r0 + 512, :].rearrange("(b p) f -> p b f", p=128), [], B_CK)
                dma("sp", CV32, c_v[r0:r0 + 512, :].rearrange("(b p) f -> p b f", p=128), [], B_CV)
                dma("sp", CL32, c_lat[r0:r0 + 512, :].rearrange("(b p) f -> p b f", p=128), [], B_CL)
                dma("sp", CR32, c_kr[r0:r0 + 512, :].rearrange("(b p) f -> p b f", p=128), [], B_CR)
                for hp in range(4):
                    bk = 4 + hp
                    for b in range(4):
                        tr(PS[:, bk, b * 128:(b + 1) * 128], CK32[:, b, hp * 128:(hp + 1) * 128], ident_f[:],
                           B_CK + [B_identf], [B_PS[bk]])
                    copy_op(ev_engine(), KTs[:, hp, :], PS[:, bk, :], [B_PS[bk]], [B_KTs])
                copy_op("pool", VVs[:].rearrange("p h b d -> p b h d"),
                        CV32.rearrange("p b (h d) -> p b h d", d=64), B_CV, [B_VVs])
                copy_op("dve", LAs[:], CL32, B_CL, [B_LAs])
                for b in range(4):
                    tr(PS[:, 0, b * 128:(b + 1) * 128], CL32[:, b, :], ident_f[:], B_CL + [B_identf], [B_PS[0]])
                copy_op("act", LTs[:], PS[:, 0, :], [B_PS[0]], [B_LTs])
                for b in range(4):
                    tr(PS[0:32, 1, b * 128:(b + 1) * 128], CR32[:, b, :], ident_f[:], B_CR + [B_identf], [B_PS[1]])
                copy_op("dve", KRs[0:32, :], PS[0:32, 1, :], [B_PS[1]], [B_KRs])
                for b in range(4):
                    kmax_update(1 + sI, CL32[:, b, :], CR32[:, b, :], B_CL + B_CR)
                kv_scratch_write([(kt, 0)], 4)
        chk("cache")
        barrier("sp", B_KT[NSLOT:] + B_VV[NSLOT:] + B_LT[NSLOT:] + B_LA[NSLOT:] + B_KR[NSLOT:]
                + [b for l in range(2) for b in B_Wgu[l]] + [b for l in range(2) for b in B_Wd[l]] + B_WO)

        def proj_stage(T):
            own = T["own"]
            prenorm_to_UT(4)
            t0 = T["tab0"]
            ccd, ssd = T["cc"], T["ss"]
            dma("sp", CS[:], ccd[t0:t0 + 512, :].rearrange("(s p) e -> p s e", p=128), [], [B_CS])
            dma("sp", SS_[:], ssd[t0:t0 + 512, :].rearrange("(s p) e -> p s e", p=128), [], [Buf("SSt") if False else B_CS])
            if T["prompt"]:
                dma("sp", KM32[32:33, :], kmask_d[0:1, t0:t0 + 512], [], [B_KM32])
            nv = T["nvalid"]
            for s in range(4):
                bb = 4 * (s % 2)
                par = s % 2
                LAT32, B_LAT32 = LAT32s[par], B_LAT32s[par]
                KR32, B_KR32 = KR32s[par], B_KR32s[par]
                KRB, B_KRB = KRBs[par], B_KRBs[par]
                RT, B_RT = RTs[par], B_RTs[par]
                K32, B_K32 = K32s[par], B_K32s[par]
                V32, B_V32 = V32s[par], B_V32s[par]
                for (bk, c0, c1) in ((bb, 0, 416), (bb + 1, 928, 1440), (bb + 2, 1440, 1952)):
                    for mc in range(8):
                        mm(PS[:, bk, 0:c1 - c0], UT[:, mc, s * 128:(s + 1) * 128], Win[:, mc, c0:c1], mc == 0, mc == 7,
                           [B_UT, B_Win], [B_PS[bk]])
                orow = T["orow0"] + s * T["ostride"]
                c = stat_cols(2)
                S.add("act", lambda e, c=c, bb=bb, LAT32=LAT32, KR32=KR32, RT=RT: e.activation(out=TMP[:, 0:128], in_=PS[:, bb, 256:384], func=AF.Square,
                                                         accum_out=st[:, c:c + 1]), [B_PS[bb]], [B_TMP, B_st])
                rstd_from_ss(c, 1, 1.0 / 128)
                if own:
                    S.add("act", lambda e, c=c, bb=bb, LAT32=LAT32, KR32=KR32, RT=RT: e.activation(out=TMP[:, 0:256], in_=PS[:, bb, 0:256], func=AF.Square,
                                                             accum_out=st[:, c + 1:c + 2]), [B_PS[bb]], [B_TMP, B_st])
                    rstd_from_ss(c + 1, 1, 1.0 / 256)
                S.add("dve", lambda e, c=c, bb=bb, LAT32=LAT32, KR32=KR32, RT=RT: e.scalar_tensor_tensor(out=LAT32[:], in0=PS[:, bb, 256:384],
                                                                   scalar=st[:, c:c + 1], in1=gkv[:], op0=ALU.mult,
                                                                   op1=ALU.mult), [B_PS[bb], B_st, B_gkv], [B_LAT32])
                if own:
                    dma("pool", T["lat_out"][orow:orow + nv, :], LAT32[0:nv, :], [B_LAT32], [Buf("o")], semb=B_LAT32, final=True)
                if own:
                    chk("po_lat")
                copy_op("pool", LAs[:, s, :], LAT32[:], [B_LAT32], [B_LAs])
                pv = PS[:, bb + 3, :].bitcast(BF16)
                tr(pv[:, 0:128], LAs[:, s, :], ident_b[:], [B_LAs, B_const], [B_PS[bb + 3]])
                copy_op("act", LTs[:, s * 128:(s + 1) * 128], pv[:, 0:128], [B_PS[bb + 3]], [B_LTs])
                S.add("dve", lambda e, s=s, bb=bb, LAT32=LAT32, KR32=KR32, RT=RT: e.tensor_tensor(out=RT[:, 0:32], in0=PS[:, bb, 384:416], in1=CS[:, s, :],
                                                            op=ALU.mult), [B_PS[bb], B_CS], [B_RT])
                S.add("dve", lambda e, s=s, bb=bb, LAT32=LAT32, KR32=KR32, RT=RT: e.tensor_tensor(out=RT[:, 32:48], in0=PS[:, bb, 400:416],
                                                            in1=SS_[:, s, 0:16], op=ALU.mult), [B_PS[bb], B_CS], [B_RT])
                S.add("dve", lambda e, s=s, bb=bb, LAT32=LAT32, KR32=KR32, RT=RT: e.tensor_tensor(out=RT[:, 48:64], in0=PS[:, bb, 384:400],
                                                            in1=SS_[:, s, 16:32], op=ALU.mult), [B_PS[bb], B_CS], [B_RT])
                S.add("dve", lambda e, LAT32=LAT32, KR32=KR32, RT=RT: e.tensor_tensor(out=KR32[:], in0=RT[:, 0:32], in1=RT[:, 32:64], op=ALU.add),
                      [B_RT], [B_KR32])
                if own:
                    dma("pool", T["kr_out"][orow:orow + nv, :], KR32[0:nv, :], [B_KR32], [Buf("o")], semb=B_KR32, final=True)
                if own:
                    chk("po_kr")
                copy_op("pool", KRB[:], KR32[:], [B_KR32], [B_KRB])
                tr(pv[0:32, 128:256], KRB[:], ident_b[:], [B_KRB, B_const], [B_PS[bb + 3]])
                copy_op("dve", KRs[0:32, s * 128:(s + 1) * 128], pv[0:32, 128:256], [B_PS[bb + 3]], [B_KRs])
                kmax_update(T["kmx_cols"][s], LAT32[:], KR32[:], [B_LAT32, B_KR32])
                if own:
                    copy_op("act", K32[:], PS[:, bb + 1, :], [B_PS[bb + 1]], [B_K32])
                    dma("pool", T["k_out"][orow:orow + nv, :], K32[0:nv, :], [B_K32], [Buf("o")], semb=B_K32, final=True)
                    copy_op("dve", V32[:], PS[:, bb + 2, :], [B_PS[bb + 2]], [B_V32])
                    dma("pool", T["v_out"][orow:orow + nv, :], V32[0:nv, :], [B_V32], [Buf("o")], semb=B_V32, final=True)
                if own:
                    chk("po_kv")
                copy_op(cfg.vv_eng or ev_engine(), VVs[:, :, s, :], PS[:, bb + 2, :].rearrange("p (h d) -> p h d", d=64), [B_PS[bb + 2]],
                        [B_VVs])
                if own:
                    chk("po_vv")
                    scale_op("dve", CQN[:, s, :], PS[:, bb, 0:256], st[:, c + 1:c + 2], [B_PS[bb], B_st], [B_CQN])
                    chk("po_cqn")
                    for cc in range(2):
                        tr(pv[:, 256 + cc * 128:384 + cc * 128], CQN[:, s, cc * 128:(cc + 1) * 128], ident_b[:],
                           [B_CQN, B_const], [B_PS[bb + 3]])
                    chk("po_cqtr")
                    copy_op("act", CQT[:, :, s * 128:(s + 1) * 128],
                            pv[:, 256:512].rearrange("p (c j) -> p c j", j=128), [B_PS[bb + 3]], [B_CQT])
                    chk("po_cq0")
            if own:
                chk("po_cq")
            if T["prompt"]:
                copy_op("dve", KRs[32:33, :], KM32[32:33, :], [B_KM32], [B_KRs])
            else:
                S.add("pool", lambda e: e.memset(KRs[32:33, :], 0.0), [], [B_KRs])
            for hp in range(4):
                bk = 4 + hp
                for mc in range(8):
                    mm(PS[:, bk, :], Win[:, mc, 928 + hp * 128:928 + (hp + 1) * 128], UT[:, mc, :], mc == 0, mc == 7,
                       [B_Win, B_UT], [B_PS[bk]])
                copy_op(ev_engine(), KTs[:, hp, :], PS[:, bk, :], [B_PS[bk]], [B_KTs])
            if own:
                for hp in range(4):
                    bk = 4 + hp
                    for mc in range(8):
                        mm(PS[:, bk, :], Win[:, mc, 416 + hp * 128:416 + (hp + 1) * 128], UT[:, mc, :], mc == 0,
                           mc == 7, [B_Win, B_UT], [B_PS[bk]])
                    scale_op("dve", QTP[0:64, 2 * hp, :], PS[0:64, bk, :], SB_SCALE, [B_PS[bk]], [B_QT])
                    scale_op("act", QTP[64:128, 2 * hp + 1, :], PS[64:128, bk, :], SB_SCALE, [B_PS[bk]], [B_QT])
            if T["prompt"]:
                kv_scratch_write([(T["kts"][0], 0)], 4)
            else:
                kv_scratch_write([(T["kts"][s], s) for s in range(4)], 1)

        def kmax_bcast(T):
            cols = sorted(set(T["kmx_cols"]))
            for i, col in enumerate(cols):
                S.add("pe", lambda e, col=col: e.transpose(out=PS[0:1, 0, 0:128], in_=KMX[:, col:col + 1],
                                                           identity=ident_f[:]), [B_KMX, B_identf], [B_PS[0]])
                S.add("dve", lambda e: e.tensor_reduce(out=MT[0:1, 0:1], in_=PS[0:1, 0, 0:128], axis=AX.X,
                                                       op=ALU.max), [B_PS[0]], [B_MT])
                S.add("act", lambda e: e.activation(out=MT[0:1, 0:1], in_=MT[0:1, 0:1], func=AF.Ln, bias=1e-18),
                      [B_MT], [B_MT])
                S.add("act", lambda e: e.activation(out=MT[0:1, 0:1], in_=MT[0:1, 0:1], func=AF.Exp, scale=0.5),
                      [B_MT], [B_MT])
                S.add("dve", lambda e: e.tensor_scalar(out=SQ2[0:1, 0:2], in0=MT[0:1, 0:1].to_broadcast([1, 2]),
                                                       scalar1=-1.02, scalar2=None, op0=ALU.mult), [B_MT], [B_SQ2])
                mm(PS[:, 0, 0:2], ones_b[0:1, 0:128], SQ2[0:1, 0:2], True, True, [B_SQ2, B_const], [B_PS[0]])
                copy_op("dve", KMB[:, i:i + 1], PS[:, 0, 0:1], [B_PS[0]], [B_KMB])

        def qside(T):
            t0 = T["tab0"]
            dma("sp", CCT[:], T["ccT"][:, t0:t0 + 512], [], [B_CCT])
            dma("sp", SST[:], T["ssT"][:, t0:t0 + 512], [], [B_CCT])
            chk("qs_tab")
            kmax_bcast(T)
            chk("qs_kmax")
            for h in range(8):
                b0 = 4 * (h % 2)
                par = h % 2
                QN, B_QN = (SPB[0][0:64, :], B_SPB[0]) if par == 0 else (SPB[1][0:64, :], B_SPB[1])
                SQ1, B_SQ1 = (Pb[0], B_Pb[0]) if par == 0 else (ATb[0], B_AT[0])
                SQ2, B_SQ2 = (Pb[1], B_Pb[1]) if par == 0 else (ATb[1], B_AT[1])
                RA, B_RA = (E32[0][0:32, :], B_E32[0]) if par == 0 else (E32[2][0:32, :], B_E32[2])
                RB, B_RB = (E32[1][0:32, :], B_E32[1]) if par == 0 else (G32[0:32, :], B_G32)
                MT, B_MT = (RS, B_RS) if par == 0 else (RDEN, B_RDEN)
                for cc in range(2):
                    mm(PS[0:64, b0, :], Wuq[:, cc, h * 96:h * 96 + 64], CQT[:, cc, :], cc == 0, cc == 1,
                       [B_Wuq, B_CQT], [B_PS[b0]])
                scale_op("act", QN[:], PS[0:64, b0, :], MLA_SCALE, [B_PS[b0]], [B_QN])
                mm(PS[:, b0 + 1, :], WukT[0:64, h, :], QN[:], True, True, [B_WukT, B_QN], [B_PS[b0 + 1]])
                copy_op("dve", QLT[:, h, :], PS[:, b0 + 1, :], [B_PS[b0 + 1]], [B_QLT[h]])
                for cc in range(2):
                    mm(PS[0:32, b0 + 2, :], Wuq[:, cc, h * 96 + 64:h * 96 + 96], CQT[:, cc, :], cc == 0, cc == 1,
                       [B_Wuq, B_CQT], [B_PS[b0 + 2]])
                for cc in range(2):
                    mm(PS[0:32, b0 + 3, :], Wuqs[:, cc, h, :], CQT[:, cc, :], cc == 0, cc == 1, [B_Wuqs, B_CQT],
                       [B_PS[b0 + 3]])
                S.add("dve", lambda e, b=b0 + 2, RA=RA, RB=RB, SQ1=SQ1, SQ2=SQ2, MT=MT: e.scalar_tensor_tensor(out=RA[:], in0=PS[0:32, b, :], scalar=MLA_SCALE,
                                                                        in1=CCT[:], op0=ALU.mult, op1=ALU.mult),
                      [B_PS[b0 + 2], B_CCT], [B_RA])
                S.add("dve", lambda e, b=b0 + 3, RA=RA, RB=RB, SQ1=SQ1, SQ2=SQ2, MT=MT: e.scalar_tensor_tensor(out=RB[:], in0=PS[0:32, b, :], scalar=MLA_SCALE,
                                                                        in1=SST[:], op0=ALU.mult, op1=ALU.mult),
                      [B_PS[b0 + 3], B_CCT], [B_RB])
                S.add("dve", lambda e, h=h, RA=RA, RB=RB, SQ1=SQ1, SQ2=SQ2, MT=MT: e.tensor_tensor(out=QRT[0:32, h, :], in0=RA[:], in1=RB[:], op=ALU.add),
                      [B_RA, B_RB], [B_QRT[h]])
                chk("qs_rope")
                S.add("act", lambda e, h=h, RA=RA, RB=RB, SQ1=SQ1, SQ2=SQ2, MT=MT: e.activation(out=SQ1[:], in_=QLT[:, h, :], func=AF.Square), [B_QLT[h]],
                      [B_SQ1])
                S.add("act", lambda e, h=h, RA=RA, RB=RB, SQ1=SQ1, SQ2=SQ2, MT=MT: e.activation(out=SQ2[0:32, :], in_=QRT[0:32, h, :], func=AF.Square),
                      [B_QRT[h]], [B_SQ2])
                mm(PS[:, b0, :], ones_b[:, 0:128], SQ1[:], True, False, [B_SQ1, B_const], [B_PS[b0]])
                mm(PS[:, b0, :], ones_b[0:32, 0:128], SQ2[0:32, :], False, True, [B_SQ2, B_const], [B_PS[b0]])
                S.add("act", lambda e, b=b0, RA=RA, RB=RB, SQ1=SQ1, SQ2=SQ2, MT=MT: e.activation(out=MT[64:65, :], in_=PS[64:65, b, :], func=AF.Ln,
                                                          bias=1e-18), [B_PS[b0]], [B_MT])
                S.add("act", lambda e, RA=RA, RB=RB, SQ1=SQ1, SQ2=SQ2, MT=MT: e.activation(out=MT[64:65, :], in_=MT[64:65, :], func=AF.Exp, scale=0.5),
                      [B_MT], [B_MT])
                if T["prompt"]:
                    S.add("dve", lambda e, h=h, RA=RA, RB=RB, SQ1=SQ1, SQ2=SQ2, MT=MT: e.tensor_scalar(out=QRT[64:65, h, :], in0=MT[64:65, :],
                                                                scalar1=KMB[64:65, 0:1], scalar2=None, op0=ALU.mult),
                          [B_MT, B_KMB], [B_QRT[h]])
                else:
                    for s in range(4):
                        S.add("dve", lambda e, h=h, s=s, RA=RA, RB=RB, SQ1=SQ1, SQ2=SQ2, MT=MT: e.tensor_scalar(
                            out=QRT[64:65, h, s * 128:(s + 1) * 128], in0=MT[64:65, s * 128:(s + 1) * 128],
                            scalar1=KMB[64:65, s:s + 1], scalar2=None, op0=ALU.mult), [B_MT, B_KMB], [B_QRT[h]])

        ld_n = {"n": 0}

        def headnorm(src, src_bufs, dst, dst_bufs, pb, N, bank):
            S.add("act", lambda e: e.activation(out=SQO[pb:pb + 64, 0:N], in_=src, func=AF.Square), src_bufs, [B_SQO])
            mm(PS[pb:pb + 64, bank, 0:N], ones_b[pb:pb + 64, 0:64], SQO[pb:pb + 64, 0:N], True, True,
               [B_SQO, B_const], [B_PS[bank]])
            S.add("act", lambda e: e.activation(out=RS[pb:pb + 64, 0:N], in_=PS[pb:pb + 64, bank, 0:N], func=AF.Ln,
                                                bias=EPS, scale=1.0 / 64), [B_PS[bank]], [B_RS])
            S.add("act", lambda e: e.activation(out=RS[pb:pb + 64, 0:N], in_=RS[pb:pb + 64, 0:N], func=AF.Exp,
                                                scale=-0.5), [B_RS], [B_RS])
            S.add("dve", lambda e: e.tensor_tensor(out=dst, in0=src, in1=RS[pb:pb + 64, 0:N], op=ALU.mult),
                  src_bufs + [B_RS], dst_bufs)

        def attention_head(g, h):
            pb = (h % 2) * 64
            hp = h // 2
            qc0, N = g["qc0"], g["N"]
            ktl = g["ktl"]
            slots = {}

            def load(n):
                kt, nblk, diag = ktl[n]
                i = ld_n["n"] % NKB
                ld_n["n"] += 1
                slots[n] = i
                W = nblk * 128
                dma("sp", ktb[i][:, 0:W], KT[kt][hp, :, 0:W], [B_KT[kt]], [B_ktb[i]])
                dma("sp", vvb[i][:, 0:nblk, 0:64],
                    VV[kt][2 * hp][:, 0:nblk * 64].rearrange("p (b d) -> p b d", d=64), [B_VV[kt]], [B_vvb[i]])
                dma("sp", vvb[i][:, 0:nblk, 64:128],
                    VV[kt][2 * hp + 1][:, 0:nblk * 64].rearrange("p (b d) -> p b d", d=64), [B_VV[kt]],
                    [B_vvb1[i]])
                dma("sp", ltb[i][:, 0:W], LT[kt][:, 0:W], [B_LT[kt]], [B_ltb[i]])
                dma("sp", lab[i][:, 0:nblk, :], LA[kt][:, 0:W].rearrange("p (b c) -> p b c", c=128), [B_LA[kt]],
                    [B_lab[i]])
                dma("sp", krb[i][0:33, 0:W], KR[kt][:, 0:W], [B_KR[kt]], [B_krb[i]])

            units = []
            for n, (kt, nblk, diag) in enumerate(ktl):
                for kb in reversed(range(nblk)):
                    units.append((n, kb, diag, kb == 0))
            NU = len(units)
            for n in range(min(NKB, len(ktl))):
                load(n)
            q_lat = QLT[:, h, qc0:qc0 + N]
            q_rope = QRT[0:65, h, qc0:qc0 + N]
            q_sb = QTP[:, h, qc0:qc0 + N]

            def stage_S(u):
                n, kb, diag, lastkb = units[u]
                i = slots[n]
                off = 384 - kb * 128
                a = u % 2
                mm(PS[:, 2 + a, 0:N], ktb[i][:, kb * 128:(kb + 1) * 128], q_sb, True, not diag,
                   [B_ktb[i], B_QT], [B_PS[2 + a]])
                if diag:
                    mm(PS[:, 2 + a, 0:N], ident_b[:], msb[:, off:off + N], False, True, [B_const], [B_PS[2 + a]])
                mm(PS[:, a, 0:N], ltb[i][:, kb * 128:(kb + 1) * 128], q_lat, True, False, [B_ltb[i], B_QLT[h]],
                   [B_PS[a]])
                mm(PS[:, a, 0:N], krb[i][0:65, kb * 128:(kb + 1) * 128], q_rope, False, not diag,
                   [B_krb[i], B_QRT[h]], [B_PS[a]])
                if diag:
                    mm(PS[:, a, 0:N], ident_b[:], mmla[:, off:off + N], False, True, [B_const], [B_PS[a]])

            def st_T(v):
                mm(PS[:, 6, 0:N], negU[:], SPB[v % 3][:, 0:N], v == 0, True, [B_SPB[v % 3], B_const], [B_PS[6]],
                   skip_group_check=(v != 0))

            def st_L(v):
                mm(PS[:, 6, 0:N], negL[:], SPB[v % 3][:, 0:N], False, True, [B_SPB[v % 3], B_const], [B_PS[6]],
                   skip_group_check=True)

            def st_EG_V(v):
                S.add("act", lambda e: e.activation(out=G32[:, 0:N], in_=PS[:, 6, 0:N], func=AF.Exp), [B_PS[6]],
                      [B_G32])
                S.add("dve", lambda e, v=v: e.tensor_tensor(out=ATb[v % 2][:, 0:N], in0=E32[v % 3][:, 0:N],
                                                            in1=G32[:, 0:N], op=ALU.mult), [B_E32[v % 3], B_G32],
                      [B_AT[v % 2]])

            def st_O(v):
                n1, kb1, _, _ = units[v]
                i1 = slots[n1]
                mm(PS[:, 7, 0:N], vvb[i1][:, kb1, :], ATb[v % 2][:, 0:N], v == 0, v == NU - 1,
                   [B_vvb[i1], B_vvb1[i1], B_AT[v % 2]], [B_PS[7]])
                if units[v][3] and n1 + NKB < len(ktl):
                    load(n1 + NKB)

            stage_S(0)
            for u in range(NU):
                n, kb, diag, lastkb = units[u]
                i = slots[n]
                a = u % 2
                first, last = (u == 0), (u == NU - 1)
                if u >= 2:
                    st_L(u - 2)
                if u >= 1:
                    st_T(u - 1)
                S.add("act", lambda e, a=a, u=u: e.activation(out=E32[u % 3][:, 0:N], in_=PS[:, 2 + a, 0:N],
                                                              func=AF.Exp), [B_PS[2 + a]], [B_E32[u % 3]])
                S.add("act", lambda e, a=a: e.activation(out=Pb[a][:, 0:N], in_=PS[:, a, 0:N], func=AF.Exp),
                      [B_PS[a]], [B_Pb[a]])
                S.add("act", lambda e, u=u: e.activation(out=SPB[u % 3][:, 0:N], in_=E32[u % 3][:, 0:N], func=AF.Ln,
                                                         bias=1.0), [B_E32[u % 3]], [B_SPB[u % 3]])
                if u + 1 < NU:
                    stage_S(u + 1)
                if u >= 2:
                    st_O(u - 2)
                mm(PS[:, 4, 0:N], lab[i][:, kb, :], Pb[a][:, 0:N], first, last, [B_lab[i], B_Pb[a]], [B_PS[4]])
                mm(PS[:, 5, 0:N], ones_b[:], Pb[a][:, 0:N], first, last, [B_Pb[a], B_const], [B_PS[5]])
                if u >= 1:
                    st_EG_V(u - 1)
            if NU >= 2:
                st_L(NU - 2)
            st_T(NU - 1)
            st_EG_V(NU - 1)
            if NU >= 2:
                st_O(NU - 2)
            st_O(NU - 1)
            S.add("dve", lambda e: e.tensor_copy(out=RDEN[pb:pb + 64, 0:N], in_=PS[pb:pb + 64, 5, 0:N]), [B_PS[5]],
                  [B_RDEN])
            S.add("dve", lambda e: e.scalar_tensor_tensor(out=ON[pb:pb + 64, 0:N], in0=RDEN[pb:pb + 64, 0:N],
                                                          scalar=EPS, in1=RDEN[pb:pb + 64, 0:N], op0=ALU.mult,
                                                          op1=ALU.mult), [B_RDEN], [B_ON])
            copy_op("act", OLT[:, 0:N], PS[:, 4, 0:N], [B_PS[4]], [B_OLT])
            mm(PS[pb:pb + 64, 5, 0:N], Wuv[:, h, :], OLT[:, 0:N], True, True, [B_Wuv, B_OLT], [B_PS[5]])
            S.add("act", lambda e: e.activation(out=SQO[pb:pb + 64, 0:N], in_=PS[pb:pb + 64, 5, 0:N],
                                                func=AF.Square), [B_PS[5]], [B_SQO])
            mm(PS[pb:pb + 64, 4, 0:N], ones_b[pb:pb + 64, 0:64], SQO[pb:pb + 64, 0:N], True, True,
               [B_SQO, B_const], [B_PS[4]])
            S.add("dve", lambda e: e.scalar_tensor_tensor(out=RS[pb:pb + 64, 0:N], in0=PS[pb:pb + 64, 4, 0:N],
                                                          scalar=1.0 / 64, in1=ON[pb:pb + 64, 0:N], op0=ALU.mult,
                                                          op1=ALU.add), [B_PS[4], B_ON], [B_RS])
            S.add("act", lambda e: e.activation(out=RS[pb:pb + 64, 0:N], in_=RS[pb:pb + 64, 0:N], func=AF.Ln),
                  [B_RS], [B_RS])
            S.add("act", lambda e: e.activation(out=RS[pb:pb + 64, 0:N], in_=RS[pb:pb + 64, 0:N], func=AF.Exp,
                                                scale=-0.5), [B_RS], [B_RS])
            S.add("dve", lambda e: e.tensor_tensor(out=OT[pb:pb + 64, hp, qc0:qc0 + N], in0=PS[pb:pb + 64, 5, 0:N],
                                                   in1=RS[pb:pb + 64, 0:N], op=ALU.mult), [B_PS[5], B_RS],
                  [B_OT[hp]])
            headnorm(PS[pb:pb + 64, 7, 0:N], [B_PS[7]], OT[pb:pb + 64, 4 + hp, qc0:qc0 + N], [B_OT[4 + hp]], pb, N, 6)

        def attention_stream(g):
            qc0 = g["qc0"]
            ktl = g["ktl"]
            NQ = 64
            NKS = 2
            slots = {}
            KTA = [HT[:, 8 * i:8 * i + 4, :] for i in range(2)]
            VVA = [HT[:, 8 * i + 4:8 * i + 8, :].rearrange("p a b -> p (a b)").rearrange("p (h x) -> p h x", x=256)
                   for i in range(2)]
            B_KTA = [B_HT[8 * i:8 * i + 4] for i in range(2)]
            B_VVA = [B_HT[8 * i + 4:8 * i + 8] for i in range(2)]

            def load(n):
                kt, nblk, diag = ktl[n]
                i = ld_n["n"] % NKS
                ld_n["n"] += 1
                slots[n] = i
                W = nblk * 128
                dma("sp", KTA[i][:, :, 0:W], KT[kt].rearrange("h p j -> p h j")[:, :, 0:W], [B_KT[kt]], B_KTA[i])
                dma("sp", VVA[i][:, :, 0:nblk * 64], VV[kt].rearrange("h p x -> p h x")[:, :, 0:nblk * 64],
                    [B_VV[kt]], B_VVA[i])
                dma("sp", ltb[i][:, 0:W], LT[kt][:, 0:W], [B_LT[kt]], [B_ltb[i]])
                dma("sp", lab[i][:, 0:nblk, :], LA[kt][:, 0:W].rearrange("p (b c) -> p b c", c=128), [B_LA[kt]],
                    [B_lab[i]])
                dma("sp", krb[i][0:33, 0:W], KR[kt][:, 0:W], [B_KR[kt]], [B_krb[i]])

            units = []
            for n, (kt, nblk, diag) in enumerate(ktl):
                for kb in reversed(range(nblk)):
                    units.append((n, kb, diag, kb == 0))
            NU = len(units)
            for n in range(min(NKS, len(ktl))):
                load(n)
            N = 512

            def stage_S(u):
                n, kb, diag, lastkb = units[u]
                i = slots[n]
                a = u % 2
                ks = slice(kb * 128, (kb + 1) * 128)
                if diag:
                    mm(PS[:, 2 + a, :], ident_b[:], MSB8[:].rearrange("p h q -> p (h q)"), True, True, [B_const],
                       [B_PS[2 + a]])

                def zmm(hh, last):
                    pbh = (hh % 2) * 64
                    mm(PS[:, 2 + a, hh * 64:(hh + 1) * 64], KTA[i][:, hh // 2, ks],
                       QTP[:, hh, qc0:qc0 + NQ], not diag, True, B_KTA[i] + [B_QT], [B_PS[2 + a]],
                       skip_group_check=diag)

                for hh in (0, 2, 4, 6):
                    zmm(hh, False)
                if diag:
                    mm(PS[:, a, :], ident_b[:], MMLA8[:].rearrange("p h q -> p (h q)"), True, True, [B_const],
                       [B_PS[a]])
                for hh in range(8):
                    mm(PS[:, a, hh * 64:(hh + 1) * 64], ltb[i][:, ks], QLT[:, hh, qc0:qc0 + NQ], not diag, diag,
                       [B_ltb[i], B_QLT[hh]], [B_PS[a]], skip_group_check=diag)
                    mm(PS[:, a, hh * 64:(hh + 1) * 64], krb[i][0:65, ks], QRT[0:65, hh, qc0:qc0 + NQ], False,
                       True, [B_krb[i], B_QRT[hh]], [B_PS[a]], skip_group_check=diag)
                for hh in (1, 3, 5, 7):
                    zmm(hh, hh == 7)

            mm(PS[:, 7, 0:256], ident_b[64:128, :], QTP[64:128, 0, 0:256], True, True, [B_const, B_QT], [B_PS[7]])
            def st_T(v):
                mm(PS[:, 6, :], negU[:], SPB[v % 3][:], v == 0, True, [B_SPB[v % 3], B_const], [B_PS[6]],
                   skip_group_check=(v != 0))

            def st_L(v):
                mm(PS[:, 6, :], negL[:], SPB[v % 3][:], False, True, [B_SPB[v % 3], B_const], [B_PS[6]],
                   skip_group_check=True)

            def st_EG_V(v):
                S.add("act", lambda e: e.activation(out=G32[:], in_=PS[:, 6, :], func=AF.Exp), [B_PS[6]], [B_G32])
                S.add("dve", lambda e, v=v: e.tensor_tensor(out=ATb[v % 2][:], in0=E32[v % 3][:], in1=G32[:],
                                                            op=ALU.mult), [B_E32[v % 3], B_G32], [B_AT[v % 2]])

            def st_O(v):
                n1, kb1, _, lk1 = units[v]
                i1 = slots[n1]
                for hh in (0, 2, 4, 6, 1, 3, 5, 7):
                    pbh = (hh % 2) * 64
                    mm(PS[pbh:pbh + 64, 7, (hh // 2) * 64:(hh // 2 + 1) * 64],
                       VVA[i1][:, hh, kb1 * 64:(kb1 + 1) * 64], ATb[v % 2][:, hh * 64:(hh + 1) * 64], False, True,
                       B_VVA[i1] + [B_AT[v % 2]], [B_PS[7]], skip_group_check=True)
                if lk1 and n1 + NKS < len(ktl):
                    load(n1 + NKS)

            stage_S(0)
            for u in range(NU):
                n, kb, diag, lastkb = units[u]
                i = slots[n]
                a = u % 2
                first, last = (u == 0), (u == NU - 1)
                if u >= 2:
                    st_L(u - 2)
                if u >= 1:
                    st_T(u - 1)
                S.add("act", lambda e, a=a, u=u: e.activation(out=E32[u % 3][:], in_=PS[:, 2 + a, :], func=AF.Exp),
                      [B_PS[2 + a]], [B_E32[u % 3]])
                S.add("act", lambda e, a=a: e.activation(out=Pb[a][:], in_=PS[:, a, :], func=AF.Exp), [B_PS[a]],
                      [B_Pb[a]])
                S.add("act", lambda e, u=u: e.activation(out=SPB[u % 3][:], in_=E32[u % 3][:], func=AF.Ln, bias=1.0),
                      [B_E32[u % 3]], [B_SPB[u % 3]])
                if u + 1 < NU:
                    stage_S(u + 1)
                if u >= 2:
                    st_O(u - 2)
                mm(PS[:, 4, :], lab[i][:, kb, :], Pb[a][:], first, last, [B_lab[i], B_Pb[a]], [B_PS[4]])
                mm(PS[:, 5, :], ones_b[:], Pb[a][:], first, last, [B_Pb[a], B_const], [B_PS[5]])
                if u >= 1:
                    st_EG_V(u - 1)
            if NU >= 2:
                st_L(NU - 2)
            st_T(NU - 1)
            st_EG_V(NU - 1)
            if NU >= 2:
                st_O(NU - 2)
            st_O(NU - 1)
            S.add("dve", lambda e: e.reciprocal(out=RDEN[:], in_=PS[:, 5, :]), [B_PS[5]], [B_RDEN])
            copy_op("act", OLT[:], PS[:, 4, :], [B_PS[4]], [B_OLT])
            for hh in (0, 2, 4, 6, 1, 3, 5, 7):
                pbh = (hh % 2) * 64
                mm(PS[pbh:pbh + 64, 5, (hh // 2) * 64:(hh // 2 + 1) * 64], Wuv[:, hh, :],
                   OLT[:, hh * 64:(hh + 1) * 64], True, True, [B_Wuv, B_OLT], [B_PS[5]], skip_group_check=True)
            for hh in range(8):
                pbh = (hh % 2) * 64
                S.add("dve", lambda e, hh=hh, pbh=pbh: e.tensor_tensor(
                    out=ON[pbh:pbh + 64, (hh // 2) * 64:(hh // 2 + 1) * 64],
                    in0=PS[pbh:pbh + 64, 5, (hh // 2) * 64:(hh // 2 + 1) * 64],
                    in1=RDEN[pbh:pbh + 64, hh * 64:(hh + 1) * 64], op=ALU.mult), [B_PS[5], B_RDEN], [B_ON])

            def headnorm_all(src, src_bufs, dst, dst_bufs, bank):
                S.add("act", lambda e: e.activation(out=SQO[:, 0:256], in_=src, func=AF.Square), src_bufs, [B_SQO])
                for pbh in (0, 64):
                    mm(PS[pbh:pbh + 64, bank, 0:256], ones_b[pbh:pbh + 64, 0:64], SQO[pbh:pbh + 64, 0:256], True,
                       True, [B_SQO, B_const], [B_PS[bank]], skip_group_check=True)
                S.add("act", lambda e: e.activation(out=RS[:, 0:256], in_=PS[:, bank, 0:256], func=AF.Ln, bias=EPS,
                                                    scale=1.0 / 64), [B_PS[bank]], [B_RS])
                S.add("act", lambda e: e.activation(out=RS[:, 0:256], in_=RS[:, 0:256], func=AF.Exp, scale=-0.5),
                      [B_RS], [B_RS])
                S.add("dve", lambda e: e.tensor_tensor(out=dst, in0=src.rearrange("p (h q) -> p h q", q=64),
                                                       in1=RS[:, 0:256].rearrange("p (h q) -> p h q", q=64),
                                                       op=ALU.mult), src_bufs + [B_RS], dst_bufs)

            headnorm_all(ON[:, 0:256], [B_ON], OT[:, 0:4, qc0:qc0 + NQ], B_OT[0:4], 4)
            headnorm_all(PS[:, 7, 0:256], [B_PS[7]], OT[:, 4:8, qc0:qc0 + NQ], B_OT[4:8], 6)

        def final_out(T):
            nv = T["nvalid"]
            YS = HT[:, 0:16, :].rearrange("p a b -> p (a b)").bitcast(F32).rearrange("p (s d) -> p s d", d=D)
            dma("sp", GPB[:], gpost_d[:, 3 * D:4 * D], [], [B_GPB])
            for s in range(4):
                c = stat_cols(1)
                S.add("act", lambda e, s=s, c=c: e.activation(out=TMP[:], in_=X[:, s, :], func=AF.Square,
                                                              accum_out=st[:, c:c + 1]), [B_X[s]], [B_TMP, B_st])
                rstd_from_ss(c, 1, 1.0 / D)
                S.add("dve", lambda e, s=s, c=c: e.scalar_tensor_tensor(out=YS[:, s, :], in0=X[:, s, :],
                                                                        scalar=st[:, c:c + 1], in1=GPB[:],
                                                                        op0=ALU.mult, op1=ALU.mult),
                      [B_X[s], B_st, B_GPB], B_HT[4 * s:4 * s + 4])
                orow = T["orow0"] + s * T["ostride"]
                dma("pool", T["y_out"][orow:orow + nv, :], YS[0:nv, s, :], B_HT[4 * s:4 * s + 4], [Buf("o")],
                    semb=B_HT[4 * s], final=True)

        def process_tile(T):
            dma("sp", X[:], T["x"].rearrange("(s p) d -> p s d", p=128), [], B_X)
            ffn(0, 4)
            chk("ffn1")
            proj_stage(T)
            chk("proj")
            chk("proj_%d" % T.get("idx", -1))
            if not T["own"]:
                return
            qside(T)
            chk("qside")
            for g in T["groups"]:
                if g["N"] == 64:
                    attention_stream(g)
                    continue
                for h in range(8):
                    attention_head(g, h)
                    chk("att1")
            chk("att")
            down_stage(OT, B_OT, 8, WO, B_WO, 4, 1)
            chk("wout")
            ffn(1, 4)
            final_out(T)

        for slot in range(NSLOT):
            own = (slot % 2 == 1)
            oi = slot // 2
            T = dict(idx=slot, own=own, prompt=True, x=xp[slot * 512:(slot + 1) * 512, :], tab0=slot * 512, cc=cc_p, ss=ss_p,
                     ccT=ccT_p, ssT=ssT_p, nvalid=128, orow0=oi * 512, ostride=128, kts=[slot], kmx_cols=[0] * 4,
                     lat_out=lat_p, kr_out=kr_p, k_out=k_p, v_out=v_p, y_out=y_p,
                     groups=[dict(qc0=0, N=512, ktl=[(slot, 4, True)] + [(k, 4, False) for k in
                                                                         reversed(range(slot))])])
            process_tile(T)
        if NSTREAM > 0:
            assert NSTREAM == 4
            kts = [NSLOT + s * (NPT + 1) + NPT for s in range(4)]
            groups = []
            for s in range(4):
                base = NSLOT + s * (NPT + 1)
                groups.append(dict(qc0=s * 128, N=64, ktl=[(kts[s], 1, True)] + [(base + k, 4, False) for k in
                                                                                reversed(range(NPT))]))
            T = dict(own=True, prompt=False, x=xs, tab0=0, cc=cc_s, ss=ss_s, ccT=ccT_s, ssT=ssT_s, nvalid=64, orow0=0,
                     ostride=64, kts=kts, kmx_cols=[1, 2, 3, 4], lat_out=lat_s, kr_out=kr_s, k_out=k_s, v_out=v_s,
                     y_out=y_s, groups=groups)
            process_tile(T)

        S.finalize(nc, out_sems)
    return nc, S


def _rope_tables(pos):
    half = 16
    inv_freq = (np.float32(10000.0) ** (-np.arange(half, dtype=np.float32) / np.float32(half))).astype(np.float32)
    ang = pos.astype(np.float32)[:, None] * inv_freq[None, :]
    c = np.cos(ang).astype(np.float32)
    s = np.sin(ang).astype(np.float32)
    cc = np.concatenate([c, c], axis=1)
    ss = np.concatenate([-s, s], axis=1)
    return np.ascontiguousarray(cc), np.ascontiguousarray(ss)


def _consts():
    j = np.arange(128)
    ident = np.eye(128, dtype=np.float32)
    negU = -(j[:, None] >= j[None, :]).astype(np.float32)
    negL = -(j[:, None] < j[None, :]).astype(np.float32)
    x = np.arange(896) - 384
    msb = np.where(j[:, None] >= x[None, :], NEG, 0.0).astype(np.float32)
    mmla = np.where((j[:, None] // 64) > (x[None, :] // 64), NEG, 0.0).astype(np.float32)
    return dict(ident=ident, negU=negU, negL=negL, msb=msb, mmla=mmla)


_PROG_CACHE = {}


def run_cores(inp, cfg, nseq):
    ncores = 2 * nseq
    NSLOT, NSTREAM, PAST = cfg.NSLOT, cfg.NSTREAM, cfg.PAST
    key = (NSLOT, NSTREAM, PAST, cfg.DFF)
    if key not in _PROG_CACHE:
        _PROG_CACHE[key] = build_program(cfg)
    nc, _ = _PROG_CACHE[key]
    f32 = lambda a: np.ascontiguousarray(np.asarray(a, dtype=np.float32))
    shared = dict(
        w_gate1=f32(inp["w_gate1"][0]), w_gate2=f32(inp["w_gate2"][0]), w_up1=f32(inp["w_up1"][0]),
        w_up2=f32(inp["w_up2"][0]), w_down1=f32(inp["w_down1"][0]), w_down2=f32(inp["w_down2"][0]),
        w_in=f32(inp["w_in"][0]), w_uq=f32(inp["w_uq"][0]).reshape(256, 768),
        w_uk=f32(inp["w_uk"][0]).reshape(128, 512), w_uv=f32(inp["w_uv"][0]).reshape(128, 512),
        w_out=f32(inp["w_out"][0]),
    )
    gp = np.stack([f32(inp["g_pre_ff1"][0]), f32(inp["g_pre_mix"][0]), f32(inp["g_pre_ff2"][0])])
    shared["gpre"] = np.ascontiguousarray(gp.reshape(3, 8, 128).transpose(2, 0, 1).reshape(128, 24))
    shared["gq"] = np.ascontiguousarray(f32(inp["g_q"][0]).reshape(2, 128).T)
    go = np.concatenate([f32(inp["g_mla_out"][0]).reshape(-1), f32(inp["g_sb_out"][0]).reshape(-1)])
    shared["gout"] = np.ascontiguousarray(go.reshape(8, 128).T)
    gpost = np.concatenate([f32(inp["g_post_ff1"][0]), f32(inp["g_post_mix"][0]), f32(inp["g_post_ff2"][0]),
                            f32(inp["g_final"][0])])
    shared["gpost"] = np.ascontiguousarray(np.broadcast_to(gpost[None, :], (128, 4 * D)))
    shared["gkv"] = np.ascontiguousarray(np.broadcast_to(f32(inp["g_kv"][0])[None, :], (128, 128)))
    shared.update(_consts())
    pos_s = np.zeros(NSTREAM * 128, dtype=np.float32)
    for s in range(NSTREAM):
        pos_s[s * 128:s * 128 + 64] = PAST + np.arange(64)
    cc_s, ss_s = _rope_tables(pos_s)
    shared.update(cc_s=cc_s, ss_s=ss_s, ccT_s=np.ascontiguousarray(cc_s.T), ssT_s=np.ascontiguousarray(ss_s.T))
    xprompt = f32(inp["x_prompt"])
    xsample = f32(inp["x_sample"])
    c_lat = f32(inp["cache_mla_latent"][0]); c_kr = f32(inp["cache_mla_krope"][0])
    c_k = f32(inp["cache_sb_k"][0]); c_v = f32(inp["cache_sb_v"][0])
    in_maps = []
    for core in range(ncores):
        seq, c = core // 2, core % 2
        m = dict(shared)
        if c == 0:
            m["xp"] = np.ascontiguousarray(xprompt[seq])
            pos = np.arange(NSLOT * 512, dtype=np.float32)
            km = np.zeros((1, NSLOT * 512), np.float32)
        else:
            m["xp"] = np.ascontiguousarray(np.concatenate([np.zeros((512, D), np.float32),
                                                           xprompt[seq][:(NSLOT - 1) * 512]], axis=0))
            pos = np.concatenate([np.zeros(512, np.float32), np.arange((NSLOT - 1) * 512, dtype=np.float32)])
            km = np.zeros((1, NSLOT * 512), np.float32)
            km[0, :512] = NEG
        cc, ss = _rope_tables(pos)
        m.update(cc_p=cc, ss_p=ss, ccT_p=np.ascontiguousarray(cc.T), ssT_p=np.ascontiguousarray(ss.T), kmask=km)
        st0 = core * NSTREAM
        xs = np.zeros((NSTREAM * 128, D), np.float32)
        for s in range(NSTREAM):
            xs[s * 128:s * 128 + 64] = xsample[st0 + s]
        m["xs"] = xs
        m["c_lat"] = np.ascontiguousarray(c_lat[st0:st0 + NSTREAM].reshape(NSTREAM * PAST, 128))
        m["c_kr"] = np.ascontiguousarray(c_kr[st0:st0 + NSTREAM].reshape(NSTREAM * PAST, 32))
        m["c_k"] = np.ascontiguousarray(c_k[st0:st0 + NSTREAM].reshape(NSTREAM * PAST, 512))
        m["c_v"] = np.ascontiguousarray(c_v[st0:st0 + NSTREAM].reshape(NSTREAM * PAST, 512))
        in_maps.append(m)
    res = run_bass_kernel_spmd(nc, in_maps, core_ids=list(range(ncores)))
    R = res.results
    SEQ = NSLOT * 512
    NB = ncores * NSTREAM
    y_p = np.zeros((nseq, SEQ, D), np.float32)
    lat_p = np.zeros((1, nseq, SEQ, 128), np.float32)
    kr_p = np.zeros((1, nseq, SEQ, 32), np.float32)
    k_p = np.zeros((1, nseq, SEQ, 8, 64), np.float32)
    v_p = np.zeros((1, nseq, SEQ, 8, 64), np.float32)
    y_s = np.zeros((NB, 64, D), np.float32)
    lat_s = np.zeros((1, NB, 64, 128), np.float32)
    kr_s = np.zeros((1, NB, 64, 32), np.float32)
    k_s = np.zeros((1, NB, 64, 8, 64), np.float32)
    v_s = np.zeros((1, NB, 64, 8, 64), np.float32)
    for core in range(ncores):
        seq, c = core // 2, core % 2
        r = R[core]
        for oi in range(cfg.NOWN):
            tile = 2 * oi + 1 if c == 0 else 2 * oi
            a, b = tile * 512, (tile + 1) * 512
            o0, o1 = oi * 512, (oi + 1) * 512
            y_p[seq, a:b] = r["y_p"][o0:o1]
            lat_p[0, seq, a:b] = r["lat_p"][o0:o1]
            kr_p[0, seq, a:b] = r["kr_p"][o0:o1]
            k_p[0, seq, a:b] = r["k_p"][o0:o1].reshape(512, 8, 64)
            v_p[0, seq, a:b] = r["v_p"][o0:o1].reshape(512, 8, 64)
        for s in range(NSTREAM):
            g = core * NSTREAM + s
            y_s[g] = r["y_s"][s * 64:(s + 1) * 64]
            lat_s[0, g] = r["lat_s"][s * 64:(s + 1) * 64]
            kr_s[0, g] = r["kr_s"][s * 64:(s + 1) * 64]
            k_s[0, g] = r["k_s"][s * 64:(s + 1) * 64].reshape(64, 8, 64)
            v_s[0, g] = r["v_s"][s * 64:(s + 1) * 64].reshape(64, 8, 64)
    return (y_p, y_s, lat_p, kr_p, k_p, v_p, lat_s, kr_s, k_s, v_s)


def kernel(**inputs):
    cfg = Cfg(nslot=16, nstream=4, past=4096, dff=2816)
    return run_cores(inputs, cfg, nseq=4)
```
